# Optimizing a Trainium2 kernel written in Bass

```python
import jax, jax.numpy as jnp
from jax import lax
import numpy as np

D_MODEL = 2048
BATCH = 2
SEQ = 16384
DEPTH = 1
DEC_BATCH = 8
DEC_SEQ = 4096
PAST_LEN = 128

N_META = 16
N_HEADS = 16
QK_NOPE_DIM = 128
QK_ROPE_DIM = 64
QK_DIM = QK_NOPE_DIM + QK_ROPE_DIM
V_DIM = 128
Q_LORA_RANK = 512
KV_LORA_RANK = 512
ATTN_WIDTH = N_HEADS * V_DIM
CONV_WIDTH = D_MODEL
CONV_K = 3
Q_BLOCK = 128
ROPE_THETA = 10000.0
NORM_EPS = 1e-6
IN_SPLIT_SIZES = (Q_LORA_RANK, KV_LORA_RANK, QK_ROPE_DIM, ATTN_WIDTH,
                  CONV_WIDTH, CONV_WIDTH, CONV_WIDTH, CONV_WIDTH, 2 * D_MODEL)
IN_WIDTH = int(sum(IN_SPLIT_SIZES))
IN_SPLIT_POINTS = tuple(int(p) for p in np.cumsum(IN_SPLIT_SIZES)[:-1])

kernel_name = 'hybrid_mla_shortconv_encoder'


def rmsnorm(x, w):
    xf = x.astype(jnp.float32)
    y = xf * lax.rsqrt(jnp.mean(xf * xf, axis=-1, keepdims=True) + NORM_EPS)
    return (y * w.astype(jnp.float32)).astype(x.dtype)


def rope_tables(seq_len, dtype):
    inv_freq = 1.0 / (ROPE_THETA ** (jnp.arange(0, QK_ROPE_DIM, 2, dtype=jnp.float32) / QK_ROPE_DIM))
    ang = jnp.arange(seq_len, dtype=jnp.float32)[:, None] * inv_freq[None, :]
    return jnp.cos(ang).astype(dtype), jnp.sin(ang).astype(dtype)


def apply_rope(x, cos, sin):
    x1, x2 = jnp.split(x, 2, axis=-1)
    return jnp.concatenate([x1 * cos - x2 * sin, x2 * cos + x1 * sin], axis=-1)


def mla_attention(q_nope, q_rope, k_nope, k_rope, v):
    b, seq_len, h, _ = q_nope.shape
    n_blocks = -(-seq_len // Q_BLOCK)
    pad = n_blocks * Q_BLOCK - seq_len
    scale = QK_DIM ** -0.5

    def to_blocks(q):
        q = jnp.pad(q, ((0, 0), (0, pad), (0, 0), (0, 0)))
        return q.reshape(b, n_blocks, Q_BLOCK, h, q.shape[-1]).transpose(1, 0, 2, 3, 4)

    def block(qs):
        qn_b, qr_b = qs
        s = (jnp.einsum('bqhd,bkhd->bhqk', qn_b, k_nope)
             + jnp.einsum('bqhr,bkr->bhqk', qr_b, k_rope)).astype(jnp.float32) * scale
        p = jax.nn.softmax(s, axis=-1).astype(v.dtype)
        return jnp.einsum('bhqk,bkhv->bqhv', p, v)

    o = lax.map(block, (to_blocks(q_nope), to_blocks(q_rope)))
    o = o.transpose(1, 0, 2, 3, 4).reshape(b, n_blocks * Q_BLOCK, h * V_DIM)
    return o[:, :seq_len]


def hybrid_layer(x, norm_w, w_in, b_gate, q_a_norm_w, w_uq, kv_a_norm_w, w_ukv,
                 w_o_attn, conv_w, w_o_conv, w_o):
    b, seq_len, _ = x.shape
    xn = rmsnorm(x, norm_w)
    proj = xn @ w_in
    q_a, c_kv, k_rope, z_attn, cx, cb, cc, z_conv, g_logits = jnp.split(proj, IN_SPLIT_POINTS, axis=-1)

    q = (rmsnorm(q_a, q_a_norm_w) @ w_uq).reshape(b, seq_len, N_HEADS, QK_DIM)
    q_nope, q_rope = q[..., :QK_NOPE_DIM], q[..., QK_NOPE_DIM:]
    kv = (rmsnorm(c_kv, kv_a_norm_w) @ w_ukv).reshape(b, seq_len, N_HEADS, QK_NOPE_DIM + V_DIM)
    k_nope, v = kv[..., :QK_NOPE_DIM], kv[..., QK_NOPE_DIM:]
    cos, sin = rope_tables(seq_len, x.dtype)
    q_rope = apply_rope(q_rope, cos[:, None, :], sin[:, None, :])
    k_rope = apply_rope(k_rope, cos, sin)
    o = mla_attention(q_nope, q_rope, k_nope, k_rope, v)
    y_attn = (o * jax.nn.silu(z_attn)) @ w_o_attn

    u = cc * cx
    u_prev = jnp.pad(u, ((0, 0), (1, 0), (0, 0)))[:, :-1]
    u_next = jnp.pad(u, ((0, 0), (0, 1), (0, 0)))[:, 1:]
    conv = conv_w[0] * u_prev + conv_w[1] * u + conv_w[2] * u_next
    y_conv = (cb * conv * jax.nn.silu(z_conv)) @ w_o_conv

    g = jax.nn.sigmoid(g_logits + b_gate)
    g_attn, g_conv = g[..., :D_MODEL], g[..., D_MODEL:]
    merged = g_attn * y_attn + g_conv * y_conv
    return x + merged @ w_o


def trunk(x, meta_tokens, norm_w, w_in, b_gate, q_a_norm_w, w_uq, kv_a_norm_w, w_ukv,
          w_o_attn, conv_w, w_o_conv, w_o, final_norm_w):
    b = x.shape[0]
    meta = jnp.broadcast_to(meta_tokens.astype(x.dtype)[None], (b, N_META, D_MODEL))
    h = jnp.concatenate([meta, x], axis=1)
    for l in range(DEPTH):
        h = hybrid_layer(h, norm_w[l], w_in[l], b_gate[l], q_a_norm_w[l], w_uq[l],
                         kv_a_norm_w[l], w_ukv[l], w_o_attn[l], conv_w[l], w_o_conv[l], w_o[l])
    h = rmsnorm(h, final_norm_w)
    return h[:, N_META:]


def setup_inputs(seed: int = 0) -> dict:
    key = jax.random.key(seed)
    ks = jax.random.split(key, 16)
    f32 = jnp.float32
    nrm = lambda k, shape, s: jax.random.normal(k, shape, f32) * s
    return {
        'x_prompt': nrm(ks[0], (BATCH, SEQ, D_MODEL), 1.0),
        'x_sample': nrm(ks[1], (DEC_BATCH, DEC_SEQ, D_MODEL), 1.0),
        'meta_tokens': nrm(ks[2], (N_META, D_MODEL), 1.0),
        'norm_w': 1.0 + nrm(ks[3], (DEPTH, D_MODEL), 0.02),
        'w_in': nrm(ks[4], (DEPTH, D_MODEL, IN_WIDTH), D_MODEL ** -0.5),
        'b_gate': nrm(ks[5], (DEPTH, 2 * D_MODEL), 0.01),
        'q_a_norm_w': 1.0 + nrm(ks[6], (DEPTH, Q_LORA_RANK), 0.02),
        'w_uq': nrm(ks[7], (DEPTH, Q_LORA_RANK, N_HEADS * QK_DIM), Q_LORA_RANK ** -0.5),
        'kv_a_norm_w': 1.0 + nrm(ks[8], (DEPTH, KV_LORA_RANK), 0.02),
        'w_ukv': nrm(ks[9], (DEPTH, KV_LORA_RANK, N_HEADS * (QK_NOPE_DIM + V_DIM)), KV_LORA_RANK ** -0.5),
        'w_o_attn': nrm(ks[10], (DEPTH, ATTN_WIDTH, D_MODEL), ATTN_WIDTH ** -0.5),
        'conv_w': nrm(ks[11], (DEPTH, CONV_K, CONV_WIDTH), CONV_K ** -0.5),
        'w_o_conv': nrm(ks[12], (DEPTH, CONV_WIDTH, D_MODEL), CONV_WIDTH ** -0.5),
        'w_o': nrm(ks[13], (DEPTH, D_MODEL, D_MODEL), D_MODEL ** -0.5),
        'final_norm_w': 1.0 + nrm(ks[14], (D_MODEL,), 0.02),
    }


def reference(x_prompt, x_sample, meta_tokens, norm_w, w_in, b_gate, q_a_norm_w, w_uq,
              kv_a_norm_w, w_ukv, w_o_attn, conv_w, w_o_conv, w_o, final_norm_w):
    y_prompt = trunk(x_prompt, meta_tokens, norm_w, w_in, b_gate, q_a_norm_w, w_uq, kv_a_norm_w,
                     w_ukv, w_o_attn, conv_w, w_o_conv, w_o, final_norm_w)
    y_sample = trunk(x_sample, meta_tokens, norm_w, w_in, b_gate, q_a_norm_w, w_uq, kv_a_norm_w,
                     w_ukv, w_o_attn, conv_w, w_o_conv, w_o, final_norm_w)
    return (y_prompt, y_sample)
```

```python
import numpy as np
import concourse.bass as bass
import concourse.mybir as mybir
from concourse.bass_utils import run_bass_kernel_spmd

F32 = mybir.dt.float32
BF16 = mybir.dt.bfloat16
AF = mybir.ActivationFunctionType
ALU = mybir.AluOpType

D = 2048
H = 16
NMETA = 16
INW = 15424
C_QA, C_CKV, C_KR, C_ZA, C_CX, C_CB, C_CC, C_ZC, C_GA, C_GC = 0, 512, 1024, 1088, 3136, 5184, 7232, 9280, 11328, 13376
EPS = 1e-6
SCALE = 192.0 ** -0.5
WC = 256
NWB = 3
TQ = 512
V_NW, V_QW, V_KW, V_CW, V_BG, V_N = 0, 16, 20, 24, 72, 104


class Buf:
    __slots__ = ("w", "r", "excl")

    def __init__(self, excl=False):
        self.w = None
        self.r = []
        self.excl = excl


class Sched:
    ENG = ("pe", "act", "dve", "pool", "sp")

    def __init__(self, nc):
        self.nc = nc
        self.ops = {e: [] for e in self.ENG}
        self.cnt = {}
        self.waited = {e: {} for e in self.ENG}
        self.sems = {}
        self._ctx = []
        self.pend = {e: ([], []) for e in self.ENG}
        for e in self.ENG:
            self.new_sem("p_" + e)

    def new_sem(self, name):
        cm = self.nc.semaphore(name)
        self.sems[name] = cm.__enter__()
        self._ctx.append(cm)
        self.cnt[name] = 0
        return name

    def _wait(self, eng, tok):
        if tok is None:
            return
        name, val = tok
        if self.waited[eng].get(name, 0) >= val:
            return
        self.waited[eng][name] = val
        sem = self.sems[name]
        self.ops[eng].append(lambda h, sem=sem, val=val: h.wait_ge(sem, val))

    def _hazards(self, eng, reads, writes):
        own = "p_" + eng
        for b in reads:
            self._wait(eng, b.w)
            if b.excl:
                for t in b.r:
                    if t[0] != own:
                        self._wait(eng, t)
        for b in writes:
            if b.w is not None and b.w[0] != own:
                self._wait(eng, b.w)
            for t in b.r:
                if t[0] != own:
                    self._wait(eng, t)

    def _assign(self, tok, reads, writes):
        for b in reads:
            b.r.append(tok)
        for b in writes:
            b.w = tok
            b.r = []

    def op(self, eng, fn, reads=(), writes=(), signal=True):
        self._hazards(eng, reads, writes)
        name = "p_" + eng
        pr, pw = self.pend[eng]
        if signal:
            self.cnt[name] += 1
            tok = (name, self.cnt[name])
            sem = self.sems[name]
            self.ops[eng].append(lambda h, fn=fn, sem=sem: fn(h).then_inc(sem, 1))
            self._assign(tok, list(reads) + pr, list(writes) + pw)
            pr.clear()
            pw.clear()
            return tok
        self.ops[eng].append(fn)
        pr.extend(reads)
        pw.extend(writes)
        return None

    def dma(self, eng, out, in_, sem, reads=(), writes=()):
        self._hazards(eng, reads, writes)
        self.cnt[sem] += 16
        tok = (sem, self.cnt[sem])
        s = self.sems[sem]
        self.ops[eng].append(lambda h, out=out, in_=in_, s=s: h.dma_start(out=out, in_=in_).then_inc(s, 16))
        self._assign(tok, reads, writes)
        return tok

    def barrier(self):
        for e in self.ENG:
            assert not self.pend[e][0] and not self.pend[e][1], e
        for e in self.ENG:
            for name, c in self.cnt.items():
                if c > 0 and name != "p_" + e:
                    self._wait(e, (name, c))

    def flush(self):
        S = self
        with self.nc.Block() as block:
            @block.tensor
            def _(h):
                for f in S.ops["pe"]:
                    f(h)

            @block.scalar
            def _(h):
                for f in S.ops["act"]:
                    f(h)

            @block.vector
            def _(h):
                for f in S.ops["dve"]:
                    f(h)

            @block.gpsimd
            def _(h):
                for f in S.ops["pool"]:
                    f(h)

            @block.sync
            def _(h):
                for f in S.ops["sp"]:
                    f(h)
        self.ops = {e: [] for e in self.ENG}

    def close(self):
        for cm in reversed(self._ctx):
            cm.__exit__(None, None, None)


class Ring:
    def __init__(self, S, name, tensors):
        self.t = tensors
        self.b = [Buf() for _ in tensors]
        self.s = [S.new_sem("%s%d" % (name, i)) for i in range(len(tensors))]
        self.i = 0

    def next(self):
        k = self.i % len(self.t)
        self.i += 1
        return self.t[k], self.b[k], self.s[k]


def build_program(LS, LP, NQ, stop=99):
    assert (LS - NMETA) % 512 == 0 and (LP - NMETA) % 512 == 0 and NQ % TQ == 0
    nc = bass.Bass("TRN2", target_bir_lowering=False)
    dram = lambda n, shp, dt, kind: nc.dram_tensor(n, list(shp), dt, kind=kind).ap()
    xs_d = dram("xs", (LS + 1, D), F32, "ExternalInput")
    xp_d = dram("xp", (LP + 1, D), F32, "ExternalInput")
    xqp_d = dram("xqp", (NQ + 2, D), F32, "ExternalInput")
    ropek_d = dram("ropek", (2, 64, LP), F32, "ExternalInput")
    ropeq_d = dram("ropeq", (2, 2, 64, NQ), F32, "ExternalInput")
    w_in_d = dram("w_in", (D, INW), F32, "ExternalInput")
    w_uq_d = dram("w_uq", (512, 3072), F32, "ExternalInput")
    w_ukv_d = dram("w_ukv", (512, 4096), F32, "ExternalInput")
    w_oa_d = dram("w_oa", (D, D), F32, "ExternalInput")
    w_oc_d = dram("w_oc", (D, D), F32, "ExternalInput")
    w_o_d = dram("w_o", (D, D), F32, "ExternalInput")
    cvec_d = dram("cvec", (128, V_N), F32, "ExternalInput")
    fnw_d = dram("fnw", (128, D), F32, "ExternalInput")
    ys_d = dram("ys", (NQ, D), F32, "ExternalOutput")
    yp_d = dram("yp", (NQ, D), F32, "ExternalOutput")
    wb_in = dram("wb_in", (D, INW), BF16, "Internal")
    wb_uq = dram("wb_uq", (512, 3072), BF16, "Internal")
    wb_uk = dram("wb_uk", (512, 2048), BF16, "Internal")
    wb_uv = dram("wb_uv", (512, 2048), BF16, "Internal")
    wb_oa = dram("wb_oa", (D, D), BF16, "Internal")
    wb_oc = dram("wb_oc", (D, D), BF16, "Internal")
    wb_o = dram("wb_o", (D, D), BF16, "Internal")
    NBLK = 1 + (LP - NMETA) // 128
    kT_d = dram("kT", (H, 128, LP), BF16, "Internal")
    krT_d = dram("krT", (64, LP), BF16, "Internal")
    V_d = dram("Vs", (H, 128, NBLK, 128), BF16, "Internal")

    S = Sched(nc)
    cms = []

    uid = [0]

    def sb(name, shape, dt):
        uid[0] += 1
        cm = nc.sbuf_tensor("sb%d_%s" % (uid[0], name), list(shape), dt)
        t = cm.__enter__()
        cms.append(cm)
        return t

    def free_to(n):
        while len(cms) > n:
            cms.pop().__exit__(None, None, None)

    psum_cms = [nc.psum_tensor("ps%d" % i, [128, 512], F32) for i in range(8)]
    PS = [cm.__enter__() for cm in psum_cms]
    PSB = [Buf(excl=True) for _ in range(8)]
    ident = sb("ident", (128, 128), BF16)
    ones_f = sb("ones_f", (128, 128), F32)
    cvec = sb("cvec", (128, V_N), F32)
    fnw = sb("fnw", (128, D), F32)
    xnT = sb("xnT", (128, 16, TQ + 2), BF16)
    hbuf = sb("hbuf", (128, 4, D), F32)
    xin = [hbuf[:, i, :] for i in range(4)]
    xsb = [sb("xsb%d" % i, (128, D), BF16) for i in range(2)]
    tmpA = sb("tmpA", (128, 4, TQ + 2), F32)
    stat = sb("stat", (128, 16), F32)
    B_ident, B_ones, B_cvec, B_fnw, B_xnT, B_tmpA = Buf(), Buf(), Buf(), Buf(), Buf(), Buf()
    R_xin = Ring(S, "xin", xin)
    R_xsb = Ring(S, "xsb", xsb)
    B_stat = [Buf() for _ in range(16)]
    stat_i = [0]
    sem_c = S.new_sem("cst")
    sem_st = [S.new_sem("st%d" % i) for i in range(4)]
    st_i = [0]
    n_global = len(cms)

    def store(out, in_, reads):
        s = sem_st[st_i[0] % 4]
        st_i[0] += 1
        return S.dma("pool", out, in_, s, reads=reads)

    def stat_slot():
        k = stat_i[0] % 16
        stat_i[0] += 1
        return stat[:, k:k + 1], B_stat[k]

    S.dma("sp", cvec[:], cvec_d, sem_c, writes=[B_cvec])
    S.dma("sp", fnw[:], fnw_d, sem_c, writes=[B_fnw])
    S.op("pool", lambda h: h.memset(ident[:], 0.0), writes=[B_ident])
    S.op("pool", lambda h: h.affine_select(out=ident[:], in_=ident[:], compare_op=ALU.not_equal, fill=1.0,
                                           base=0, pattern=[[-1, 128]], channel_multiplier=1),
         reads=[B_ident], writes=[B_ident])
    S.op("pool", lambda h: h.memset(ones_f[:], 1.0), writes=[B_ones])
    sem_w = S.new_sem("wcast")
    for r in range(0, D, 128):
        S.dma("pool", wb_in[r:r + 128, :], w_in_d[r:r + 128, :], sem_w)
    for r in range(0, D, 512):
        S.dma("pool", wb_oa[r:r + 512, :], w_oa_d[r:r + 512, :], sem_w)
        S.dma("pool", wb_oc[r:r + 512, :], w_oc_d[r:r + 512, :], sem_w)
        S.dma("pool", wb_o[r:r + 512, :], w_o_d[r:r + 512, :], sem_w)
    S.dma("pool", wb_uq, w_uq_d, sem_w)
    ukv4 = w_ukv_d.rearrange("k (h t d) -> k h t d", h=H, t=2)
    S.dma("pool", wb_uk.rearrange("k (h d) -> k h d", h=H), ukv4[:, :, 0, :], sem_w)
    S.dma("pool", wb_uv.rearrange("k (h d) -> k h d", h=H), ukv4[:, :, 1, :], sem_w)
    S.barrier()
    S.flush()

    def make_xnT(x_ap, row0, n, col0):
        off = 0
        while off < n:
            m = min(128, n - off)
            xt, xb_, xsem = R_xin.next()
            S.dma("sp", xt[0:m, :], x_ap[row0 + off:row0 + off + m, :], xsem, writes=[xb_])
            xs_t, xs_b, _ = R_xsb.next()
            ss, ss_b = stat_slot()
            S.op("act", lambda h, xt=xt, xs_t=xs_t, ss=ss, m=m: h.activation(
                out=xs_t[0:m, :], in_=xt[0:m, :], func=AF.Square, scale=float(D) ** -0.5, accum_out=ss[0:m, :]),
                reads=[xb_], writes=[xs_b, ss_b])
            S.op("act", lambda h, ss=ss, m=m: h.activation(out=ss[0:m, :], in_=ss[0:m, :], func=AF.Sqrt, bias=EPS, scale=1.0),
                 reads=[ss_b], writes=[ss_b])
            S.op("dve", lambda h, ss=ss, m=m: h.reciprocal(out=ss[0:m, :], in_=ss[0:m, :]), reads=[ss_b], writes=[ss_b])
            S.op("act", lambda h, xt=xt, xs_t=xs_t, ss=ss, m=m: h.activation(
                out=xs_t[0:m, :], in_=xt[0:m, :], func=AF.Copy, scale=ss[0:m, :]),
                reads=[xb_, ss_b], writes=[xs_b])
            for half in range(2):
                pb = 6 + half
                pv = PS[pb].bitcast(BF16)
                for j in range(8):
                    kc = half * 8 + j
                    S.op("pe", lambda h, pv=pv, xs_t=xs_t, j=j, kc=kc, m=m: h.transpose(
                        out=pv[:, j * 128:j * 128 + m], in_=xs_t[0:m, kc * 128:(kc + 1) * 128], identity=ident[0:m, 0:m]),
                        reads=[xs_b, B_ident], writes=[PSB[pb]], signal=(j == 7))
                for j in range(8):
                    kc = half * 8 + j
                    eng = "dve" if half == 0 else "act"
                    dst = xnT[:, kc, col0 + off:col0 + off + m]
                    src = pv[:, j * 128:j * 128 + m]
                    nw = cvec[:, V_NW + kc:V_NW + kc + 1]
                    if eng == "dve":
                        S.op("dve", lambda h, dst=dst, src=src, nw=nw: h.tensor_scalar(
                            out=dst, in0=src, scalar1=nw, scalar2=None, op0=ALU.mult),
                            reads=[PSB[pb], B_cvec], writes=[B_xnT])
                    else:
                        S.op("act", lambda h, dst=dst, src=src, nw=nw: h.activation(
                            out=dst, in_=src, func=AF.Copy, scale=nw),
                            reads=[PSB[pb], B_cvec], writes=[B_xnT])
            off += m

    def mm_group(pb, out_ap, pairs, reads):
        n = len(pairs)
        tok = None
        for i, (l, r) in enumerate(pairs):
            tok = S.op("pe", lambda h, l=l, r=r, i=i: h.matmul(out_ap, lhsT=l, rhs=r, start=(i == 0), stop=(i == n - 1)),
                       reads=reads, writes=[PSB[pb]], signal=(i == n - 1))
        return tok

    jobs = [
        dict(name="s", xk=xs_d, L=LS, xq=xs_d, xq_row0=NMETA - 1, rq=0, y=ys_d),
        dict(name="p", xk=xp_d, L=LP, xq=xqp_d, xq_row0=0, rq=1, y=yp_d),
    ]
    phase = [0]
    for job in jobs:
        L = job["L"]
        phase[0] += 1
        if phase[0] > stop:
            break
        wkv = sb("wkv", (128, 16, 576), BF16)
        wuk = sb("wuk", (128, 4, 2048), BF16)
        wuv = sb("wuv", (128, 4, 2048), BF16)
        ckvT = sb("ckvT", (128, 4, 512), BF16)
        rkb = sb("rkb", (128, 512), F32)
        rkc = sb("rkc", (128, 4), F32)
        kst = [sb("kst%d" % i, (128, 4, 512), BF16) for i in range(2)]
        vst = [sb("vst%d" % i, (128, 2048), BF16) for i in range(2)]
        krs = [sb("krs%d" % i, (64, 512), BF16) for i in range(2)]
        rtab = [sb("rtab%d" % i, (64, 2, 512), F32) for i in range(2)]
        rt1 = sb("rt1", (64, 512), F32)
        rt2 = sb("rt2", (64, 512), F32)
        B_wkv, B_wuk, B_wuv, B_ckvT, B_rkb, B_rkc, B_rt1, B_rt2 = [Buf() for _ in range(8)]
        R_kst = Ring(S, "kst" + job["name"], kst)
        R_vst = Ring(S, "vst" + job["name"], vst)
        R_krs = Ring(S, "krs" + job["name"], krs)
        R_rtab = Ring(S, "rtab" + job["name"], rtab)
        S.dma("sp", wkv[:], wb_in.rearrange("(kc p) c -> p kc c", p=128)[:, :, C_CKV:C_CKV + 576], sem_c, writes=[B_wkv])
        S.dma("sp", wuk[:], wb_uk.rearrange("(kc p) c -> p kc c", p=128), sem_c, writes=[B_wuk])
        S.dma("sp", wuv[:], wb_uv.rearrange("(kc p) c -> p kc c", p=128), sem_c, writes=[B_wuv])
        tiles = [(0, NMETA)] + [(NMETA + 512 * i, 512) for i in range((L - NMETA) // 512)]
        accb = [0]

        def next_acc():
            k = accb[0] % 6
            accb[0] += 1
            return k

        for (t0, n) in tiles:
            make_xnT(job["xk"], t0, n, 0)
            rt, rt_b, rt_s = R_rtab.next()
            S.dma("sp", rt[:, :, 0:n], ropek_d[:, :, t0:t0 + n].rearrange("t r n -> r t n"), rt_s, writes=[rt_b])
            for mt in range(4):
                pb = next_acc()
                mm_group(pb, PS[pb][:, 0:n], [(wkv[:, kc, mt * 128:(mt + 1) * 128], xnT[:, kc, 0:n]) for kc in range(16)],
                         reads=[B_wkv, B_xnT])
                S.op("dve", lambda h, pb=pb, mt=mt, n=n: h.tensor_scalar(
                    out=ckvT[:, mt, 0:n], in0=PS[pb][:, 0:n], scalar1=cvec[:, V_KW + mt:V_KW + mt + 1], scalar2=None, op0=ALU.mult),
                    reads=[PSB[pb], B_cvec], writes=[B_ckvT])
                S.op("act", lambda h, pb=pb, mt=mt, n=n: h.activation(out=tmpA[:, mt, 0:n], in_=PS[pb][:, 0:n], func=AF.Square),
                     reads=[PSB[pb]], writes=[B_tmpA])
            pb = next_acc()
            mm_group(pb, PS[pb][0:64, 0:n], [(wkv[:, kc, 512:576], xnT[:, kc, 0:n]) for kc in range(16)], reads=[B_wkv, B_xnT])
            kr_t, kr_b, _ = R_krs.next()
            S.op("dve", lambda h, pb=pb, n=n, rt=rt: h.tensor_tensor(out=rt1[:, 0:n], in0=PS[pb][0:64, 0:n], in1=rt[:, 0, 0:n], op=ALU.mult),
                 reads=[PSB[pb], rt_b], writes=[B_rt1])
            S.op("dve", lambda h, pb=pb, n=n, rt=rt: h.tensor_tensor(out=rt2[0:32, 0:n], in0=PS[pb][32:64, 0:n], in1=rt[32:64, 1, 0:n], op=ALU.mult),
                 reads=[PSB[pb], rt_b], writes=[B_rt2])
            S.op("dve", lambda h, pb=pb, n=n, rt=rt: h.tensor_tensor(out=rt2[32:64, 0:n], in0=PS[pb][0:32, 0:n], in1=rt[0:32, 1, 0:n], op=ALU.mult),
                 reads=[PSB[pb], rt_b], writes=[B_rt2])
            S.op("dve", lambda h, n=n, kr_t=kr_t: h.tensor_tensor(out=kr_t[:, 0:n], in0=rt1[:, 0:n], in1=rt2[:, 0:n], op=ALU.add),
                 reads=[B_rt1, B_rt2], writes=[kr_b])
            store(krT_d[:, t0:t0 + n], kr_t[:, 0:n], reads=[kr_b])
            pb = next_acc()
            mm_group(pb, PS[pb][:, 0:n], [(ones_f[:], tmpA[:, mt, 0:n]) for mt in range(4)], reads=[B_ones, B_tmpA])
            S.op("act", lambda h, pb=pb, n=n: h.activation(out=rkb[:, 0:n], in_=PS[pb][:, 0:n], func=AF.Sqrt, bias=EPS, scale=1.0 / 512),
                 reads=[PSB[pb]], writes=[B_rkb])
            S.op("dve", lambda h, n=n: h.reciprocal(out=rkb[:, 0:n], in_=rkb[:, 0:n]), reads=[B_rkb], writes=[B_rkb])
            nsub = (n + 127) // 128
            pb = next_acc()
            for s_ in range(nsub):
                m = min(128, n - s_ * 128)
                S.op("pe", lambda h, pb=pb, s_=s_, m=m: h.matmul(PS[pb][0:m, s_:s_ + 1], lhsT=rkb[0:1, s_ * 128:s_ * 128 + m],
                                                                  rhs=ones_f[0:1, 0:1], start=True, stop=True),
                     reads=[B_rkb, B_ones], writes=[PSB[pb]], signal=(s_ == nsub - 1))
            S.op("dve", lambda h, pb=pb, nsub=nsub: h.tensor_copy(out=rkc[:, 0:nsub], in_=PS[pb][:, 0:nsub]),
                 reads=[PSB[pb]], writes=[B_rkc])
            for g in range(4):
                ks_t, ks_b, _ = R_kst.next()
                for hh in range(4):
                    hd = g * 4 + hh
                    pb = next_acc()
                    mm_group(pb, PS[pb][:, 0:n], [(wuk[:, kc, hd * 128:(hd + 1) * 128], ckvT[:, kc, 0:n]) for kc in range(4)],
                             reads=[B_wuk, B_ckvT])
                    S.op("dve", lambda h, pb=pb, hh=hh, n=n, ks_t=ks_t: h.tensor_tensor(
                        out=ks_t[:, hh, 0:n], in0=PS[pb][:, 0:n], in1=rkb[:, 0:n], op=ALU.mult),
                        reads=[PSB[pb], B_rkb], writes=[ks_b])
                store(kT_d[g * 4:(g + 1) * 4, :, t0:t0 + n].rearrange("h p t -> p h t"), ks_t[:, :, 0:n], reads=[ks_b])
            for s_ in range(nsub):
                m = min(128, n - s_ * 128)
                blk = 0 if t0 == 0 else 1 + (t0 - NMETA) // 128 + s_
                vs_t, vs_b, _ = R_vst.next()
                for g in range(4):
                    pb = next_acc()
                    mm_group(pb, PS[pb][0:m, :], [(ckvT[:, kc, s_ * 128:s_ * 128 + m], wuv[:, kc, g * 512:(g + 1) * 512]) for kc in range(4)],
                             reads=[B_wuv, B_ckvT])
                    S.op("act", lambda h, pb=pb, g=g, m=m, s_=s_, vs_t=vs_t: h.activation(
                        out=vs_t[0:m, g * 512:(g + 1) * 512], in_=PS[pb][0:m, :], func=AF.Copy, scale=rkc[0:m, s_:s_ + 1]),
                        reads=[PSB[pb], B_rkc], writes=[vs_b])
                store(V_d[:, 0:m, blk, :].rearrange("h p d -> p h d"), vs_t[0:m, :].rearrange("p (h d) -> p h d", h=H), reads=[vs_b])
        S.barrier()
        S.flush()
        free_to(n_global)

        phase[0] += 1
        if phase[0] > stop:
            break
        CKB = 8 if (L - NMETA) % 1024 == 0 else 4
        CKT = CKB * 128
        nchunk = (L - NMETA) // CKT
        qanT = sb("qanT", (128, 4, TQ), BF16)
        rqb = sb("rqb", (128, TQ), F32)
        szT = sb("szT", (128, 16, TQ), BF16)
        gcT = sb("gcT", (128, 16, TQ), BF16)
        mgT = sb("mgT", (128, 16, TQ), BF16)
        tmpB = sb("tmpB", (128, 2, TQ), F32)
        silt = [sb("silt%d" % i, (128, TQ), F32) for i in range(2)]
        wts = [sb("wt%d" % i, (128, 16, WC), BF16) for i in range(NWB)]
        wqs = [sb("wq%d" % i, (128, 4, 192), BF16) for i in range(2)]
        kbs = [sb("kb%d" % i, (128, CKT + NMETA), BF16) for i in range(3)]
        krb = [sb("krb%d" % i, (128, CKT + NMETA), BF16) for i in range(3)]
        vbs = [sb("vb%d" % i, (128, CKB + 1, 132), BF16) for i in range(3)]
        pts = [sb("pt%d" % i, (128, TQ), BF16) for i in range(3)]
        qns = [sb("qn%d" % i, (128, TQ), BF16) for i in range(2)]
        qrs = [sb("qr%d" % i, (128, TQ), BF16) for i in range(2)]
        onb = sb("onb", (128, 4, 128), BF16)
        rq_tab = sb("rq_tab", (64, 2, TQ), F32)
        qt1 = sb("qt1", (64, TQ), F32)
        qt2 = sb("qt2", (64, TQ), F32)
        B_qanT, B_rqb, B_gcT, B_mgT, B_tmpB, B_onb, B_rqtab, B_qt1, B_qt2 = [Buf() for _ in range(9)]
        B_szT = [Buf() for _ in range(16)]
        B_hb = R_xin.b
        B_sil = [Buf(), Buf()]
        B_qr = [Buf(), Buf()]
        B_qn = [Buf(), Buf()]
        B_pt = [Buf() for _ in range(3)]
        nm = job["name"]
        R_wt = Ring(S, "wt" + nm, wts)
        R_wq = Ring(S, "wq" + nm, wqs)
        R_kb = Ring(S, "kb" + nm, kbs)
        R_krb = Ring(S, "krb" + nm, krb)
        R_vb = Ring(S, "vb" + nm, vbs)
        sem_q = S.new_sem("semq" + nm)
        sem_h = [S.new_sem("semh%s%d" % (nm, i)) for i in range(4)]
        for i in range(3):
            S.op("pool", lambda h, i=i: h.memset(krb[i][64:128, :], 0.0), writes=[R_krb.b[i]])
            S.op("pool", lambda h, i=i: h.memset(vbs[i][:, :, 128:129], 1.0), writes=[R_vb.b[i]])
        for i in range(2):
            S.op("pool", lambda h, i=i: h.memset(qrs[i][64:128, :], 0.0), writes=[B_qr[i]])

        wv_in = wb_in.rearrange("(kc p) c -> p kc c", p=128)
        wv_oa = wb_oa.rearrange("(kc p) c -> p kc c", p=128)
        wv_oc = wb_oc.rearrange("(kc p) c -> p kc c", p=128)
        wv_o = wb_o.rearrange("(kc p) c -> p kc c", p=128)
        wv_uq = wb_uq.rearrange("(kc p) c -> p kc c", p=128)

        for qt in range(NQ // TQ):
            r0 = job["xq_row0"] + qt * TQ
            plan = []
            for c0 in range(0, 512, WC):
                plan.append((wv_in, C_QA + c0))
            for c0 in range(0, 2048, WC):
                plan.append((wv_in, C_ZA + c0))
            n_pre = len(plan)
            for c0 in range(0, 2048, WC):
                for base in (C_CX, C_CC, C_CB, C_ZC):
                    plan.append((wv_in, base + c0))
            for c0 in range(0, 2048, WC):
                plan.append((wv_in, C_GA + c0))
                plan.append((wv_oa, c0))
                plan.append((wv_in, C_GC + c0))
                plan.append((wv_oc, c0))
            for c0 in range(0, 2048, WC):
                plan.append((wv_o, c0))
            loaded = {}
            nload = [0]

            def wload_upto(k):
                while nload[0] < min(k, len(plan)):
                    i = nload[0]
                    t, b, s = R_wt.next()
                    src, c0 = plan[i]
                    S.dma("sp", t[:], src[:, :, c0:c0 + WC], s, writes=[b])
                    loaded[i] = (t, b)
                    nload[0] += 1

            def wget(i):
                wload_upto(i + NWB)
                return loaded[i]

            accb = [0]

            def next_acc():
                k = accb[0] % 6
                accb[0] += 1
                return k

            make_xnT(job["xq"], r0, TQ + 2, 0)
            S.dma("sp", rq_tab[:], ropeq_d[job["rq"], :, :, qt * TQ:(qt + 1) * TQ].rearrange("t r n -> r t n"), sem_q, writes=[B_rqtab])
            XO = slice(1, TQ + 1)
            wi = 0
            for c0 in range(0, 512, WC):
                wt, wb_ = wget(wi); wi += 1
                for mi in range(WC // 128):
                    mt = (c0 // 128) + mi
                    pb = next_acc()
                    mm_group(pb, PS[pb][:, :], [(wt[:, kc, mi * 128:(mi + 1) * 128], xnT[:, kc, XO]) for kc in range(16)], reads=[wb_, B_xnT])
                    S.op("dve", lambda h, pb=pb, mt=mt: h.tensor_scalar(out=qanT[:, mt, :], in0=PS[pb][:, :], scalar1=cvec[:, V_QW + mt:V_QW + mt + 1],
                                                                      scalar2=None, op0=ALU.mult), reads=[PSB[pb], B_cvec], writes=[B_qanT])
                    S.op("act", lambda h, pb=pb, mt=mt: h.activation(out=tmpA[:, mt, 0:TQ], in_=PS[pb][:, :], func=AF.Square),
                         reads=[PSB[pb]], writes=[B_tmpA])
            pb = next_acc()
            mm_group(pb, PS[pb][:, :], [(ones_f[:], tmpA[:, mt, 0:TQ]) for mt in range(4)], reads=[B_ones, B_tmpA])
            S.op("act", lambda h, pb=pb: h.activation(out=rqb[:], in_=PS[pb][:, :], func=AF.Sqrt, bias=EPS, scale=1.0 / 512),
                 reads=[PSB[pb]], writes=[B_rqb])
            S.op("dve", lambda h: h.reciprocal(out=rqb[:], in_=rqb[:]), reads=[B_rqb], writes=[B_rqb])
            S.op("dve", lambda h: h.tensor_tensor(out=rq_tab[:, 0, :], in0=rq_tab[:, 0, :], in1=rqb[0:64, :], op=ALU.mult),
                 reads=[B_rqtab, B_rqb], writes=[B_rqtab])
            S.op("dve", lambda h: h.tensor_tensor(out=rq_tab[:, 1, :], in0=rq_tab[:, 1, :], in1=rqb[0:64, :], op=ALU.mult),
                 reads=[B_rqtab, B_rqb], writes=[B_rqtab])
            for c0 in range(0, 2048, WC):
                wt, wb_ = wget(wi); wi += 1
                for mi in range(WC // 128):
                    c = (c0 // 128) + mi
                    pb = next_acc()
                    mm_group(pb, PS[pb][:, :], [(wt[:, kc, mi * 128:(mi + 1) * 128], xnT[:, kc, XO]) for kc in range(16)], reads=[wb_, B_xnT])
                    S.op("act", lambda h, pb=pb, c=c: h.activation(out=szT[:, c, :], in_=PS[pb][:, :], func=AF.Silu),
                         reads=[PSB[pb]], writes=[B_szT[c]])
            assert wi == n_pre
            wload_upto(n_pre + NWB)

            ST = [0, 1]
            OB = [[2, 3], [4, 5]]
            PQN, PQR = 6, 7

            def q_proj(hd):
                wq, wq_b, wq_s = R_wq.next()
                S.dma("sp", wq[:], wv_uq[:, :, hd * 192:(hd + 1) * 192], wq_s, writes=[wq_b])
                i = hd % 2
                mm_group(PQN, PS[PQN][:, :], [(wq[:, kc, 0:128], qanT[:, kc, :]) for kc in range(4)], reads=[wq_b, B_qanT])
                S.op("dve", lambda h, i=i: h.tensor_tensor(out=qns[i][:], in0=PS[PQN][:, :], in1=rqb[:], op=ALU.mult),
                     reads=[PSB[PQN], B_rqb], writes=[B_qn[i]])
                mm_group(PQR, PS[PQR][0:64, :], [(wq[:, kc, 128:192], qanT[:, kc, :]) for kc in range(4)], reads=[wq_b, B_qanT])
                S.op("dve", lambda h: h.tensor_tensor(out=qt1[:], in0=PS[PQR][0:64, :], in1=rq_tab[:, 0, :], op=ALU.mult),
                     reads=[PSB[PQR], B_rqtab], writes=[B_qt1])
                S.op("dve", lambda h: h.tensor_tensor(out=qt2[0:32, :], in0=PS[PQR][32:64, :], in1=rq_tab[32:64, 1, :], op=ALU.mult),
                     reads=[PSB[PQR], B_rqtab], writes=[B_qt2])
                S.op("dve", lambda h: h.tensor_tensor(out=qt2[32:64, :], in0=PS[PQR][0:32, :], in1=rq_tab[0:32, 1, :], op=ALU.mult),
                     reads=[PSB[PQR], B_rqtab], writes=[B_qt2])
                S.op("dve", lambda h, i=i: h.tensor_tensor(out=qrs[i][0:64, :], in0=qt1[:], in1=qt2[:], op=ALU.add),
                     reads=[B_qt1, B_qt2], writes=[B_qr[i]])

            items = []
            for hd in range(H):
                for ck in range(nchunk):
                    nb = CKB + (1 if ck == 0 else 0)
                    for bi in range(nb):
                        items.append((hd, ck, bi))
            chunk_bufs = {}

            def load_chunk(hd, ck):
                kt, kb_, ks_ = R_kb.next()
                krt, krb_, krs_ = R_krb.next()
                vt, vb_, vs_ = R_vb.next()
                tok0 = 0 if ck == 0 else NMETA + ck * CKT
                ntok = CKT + (NMETA if ck == 0 else 0)
                blk0 = 0 if ck == 0 else 1 + ck * CKB
                nb = CKB + (1 if ck == 0 else 0)
                S.dma("sp", kt[:, 0:ntok], kT_d[hd, :, tok0:tok0 + ntok], ks_, writes=[kb_])
                S.dma("sp", krt[0:64, 0:ntok], krT_d[:, tok0:tok0 + ntok], krs_, writes=[krb_])
                S.dma("sp", vt[:, 0:nb, 0:128], V_d[hd, :, blk0:blk0 + nb, :], vs_, writes=[vb_])
                chunk_bufs[(hd, ck)] = (kt, kb_, krt, krb_, vt, vb_)

            def item_geom(it):
                hd, ck, bi = it
                if ck == 0:
                    nk = NMETA if bi == 0 else 128
                    col = 0 if bi == 0 else NMETA + (bi - 1) * 128
                else:
                    nk = 128
                    col = bi * 128
                return nk, col

            def qk(k):
                hd, ck, bi = items[k]
                if bi == 0:
                    if (hd, ck) not in chunk_bufs:
                        load_chunk(hd, ck)
                    nxt = (hd, ck + 1) if ck + 1 < nchunk else ((hd + 1, 0) if hd + 1 < H else None)
                    if nxt is not None and nxt not in chunk_bufs:
                        load_chunk(*nxt)
                kt, kb_, krt, krb_, vt, vb_ = chunk_bufs[(hd, ck)]
                nk, col = item_geom(items[k])
                st = ST[k % 2]
                i = hd % 2
                S.op("pe", lambda h, st=st, nk=nk, col=col, kt=kt, i=i: h.matmul(
                    PS[st][0:nk, :], lhsT=kt[:, col:col + nk], rhs=qns[i][:], start=True, stop=False),
                    reads=[kb_, B_qn[i]], writes=[PSB[st]], signal=False)
                S.op("pe", lambda h, st=st, nk=nk, col=col, krt=krt, i=i: h.matmul(
                    PS[st][0:nk, :], lhsT=krt[:, col:col + nk], rhs=qrs[i][:], start=False, stop=True),
                    reads=[krb_, B_qr[i]], writes=[PSB[st]])
                j = k % 3
                S.op("act", lambda h, st=st, nk=nk, j=j: h.activation(out=pts[j][0:nk, :], in_=PS[st][0:nk, :], func=AF.Exp, scale=SCALE),
                     reads=[PSB[st]], writes=[B_pt[j]])

            def pv(k):
                hd, ck, bi = items[k]
                kt, kb_, krt, krb_, vt, vb_ = chunk_bufs[(hd, ck)]
                nk, col = item_geom(items[k])
                j = k % 3
                ob = OB[hd % 2]
                first = (ck == 0 and bi == 0)
                last = (ck == nchunk - 1 and bi == CKB + (1 if ck == 0 else 0) - 1)
                for qs in range(4):
                    bank = ob[qs // 2]
                    co = (qs % 2) * 256
                    S.op("pe", lambda h, bank=bank, co=co, nk=nk, j=j, qs=qs, vt=vt, bi=bi, first=first, last=last: h.matmul(
                        PS[bank][:, co:co + 129], lhsT=pts[j][0:nk, qs * 128:(qs + 1) * 128], rhs=vt[0:nk, bi, 0:129],
                        start=(first and qs % 2 == 0), stop=last, skip_group_check=True),
                        reads=[B_pt[j], vb_], writes=[PSB[bank]], signal=(qs == 3))
                if last and ck == nchunk - 1:
                    del chunk_bufs[(hd, ck)]

            def head_norm(hd):
                ob = OB[hd % 2]
                for qs in range(4):
                    bank = ob[qs // 2]
                    co = (qs % 2) * 256
                    rc, rc_b = stat_slot()
                    S.op("dve", lambda h, bank=bank, co=co, rc=rc: h.reciprocal(out=rc, in_=PS[bank][:, co + 128:co + 129]),
                         reads=[PSB[bank]], writes=[rc_b])
                    S.op("dve", lambda h, bank=bank, co=co, rc=rc, qs=qs: h.tensor_scalar(
                        out=onb[:, qs, :], in0=PS[bank][:, co:co + 128], scalar1=rc, scalar2=None, op0=ALU.mult),
                        reads=[PSB[bank], rc_b], writes=[B_onb])

            def head_out(hd):
                pv_ = PS[PQN].bitcast(BF16)
                for qs in range(4):
                    S.op("pe", lambda h, qs=qs, pv_=pv_: h.transpose(out=pv_[:, qs * 128:(qs + 1) * 128], in_=onb[:, qs, :], identity=ident[:]),
                         reads=[B_onb, B_ident], writes=[PSB[PQN]], signal=(qs == 3))
                S.op("dve", lambda h, hd=hd, pv_=pv_: h.tensor_tensor(out=szT[:, hd, :], in0=pv_[:, 0:TQ], in1=szT[:, hd, :], op=ALU.mult),
                     reads=[PSB[PQN], B_szT[hd]], writes=[B_szT[hd]])

            deferred = {}
            q_proj(0)
            qk(0)
            nit = len(items)
            per_head = nit // H
            for k in range(nit):
                hd, ck, bi = items[k]
                if k + 1 < nit:
                    qk(k + 1)
                pv(k)
                kin = k % per_head
                if kin == per_head - 1:
                    head_norm(hd)
                    deferred[k + 2] = hd
                if kin == min(2, per_head - 2) and hd + 1 < H:
                    q_proj(hd + 1)
                if k in deferred:
                    head_out(deferred.pop(k))
            for k in sorted(deferred):
                head_out(deferred.pop(k))

            NM = WC // 128
            for c0 in range(0, 2048, WC):
                wt, wb_ = wget(wi); wi += 1
                for mi in range(NM):
                    pb = next_acc()
                    pb2 = next_acc()
                    for kc in range(16):
                        S.op("pe", lambda h, pb=pb, wt=wt, mi=mi, kc=kc: h.matmul(
                            PS[pb][:, :], lhsT=wt[:, kc, mi * 128:(mi + 1) * 128], rhs=xnT[:, kc, 0:TQ], start=(kc == 0), stop=(kc == 15)),
                            reads=[wb_, B_xnT], writes=[PSB[pb]], signal=False)
                        S.op("pe", lambda h, pb2=pb2, wt=wt, mi=mi, kc=kc: h.matmul(
                            PS[pb2][:, 0:2], lhsT=wt[:, kc, mi * 128:(mi + 1) * 128], rhs=xnT[:, kc, TQ:TQ + 2], start=(kc == 0), stop=(kc == 15)),
                            reads=[wb_, B_xnT], writes=[PSB[pb2]], signal=(kc == 15))
                    S.op("act", lambda h, pb=pb, mi=mi: h.activation(out=tmpA[:, mi, 0:TQ], in_=PS[pb][:, :], func=AF.Copy),
                         reads=[PSB[pb]], writes=[B_tmpA])
                    S.op("act", lambda h, pb2=pb2, mi=mi: h.activation(out=tmpA[:, mi, TQ:TQ + 2], in_=PS[pb2][:, 0:2], func=AF.Copy),
                         reads=[PSB[pb2]], writes=[B_tmpA])
                wt, wb_ = wget(wi); wi += 1
                for mi in range(NM):
                    c = c0 // 128 + mi
                    pb = next_acc()
                    pb2 = next_acc()
                    for kc in range(16):
                        S.op("pe", lambda h, pb=pb, wt=wt, mi=mi, kc=kc: h.matmul(
                            PS[pb][:, :], lhsT=wt[:, kc, mi * 128:(mi + 1) * 128], rhs=xnT[:, kc, 0:TQ], start=(kc == 0), stop=(kc == 15)),
                            reads=[wb_, B_xnT], writes=[PSB[pb]], signal=False)
                        S.op("pe", lambda h, pb2=pb2, wt=wt, mi=mi, kc=kc: h.matmul(
                            PS[pb2][:, 0:2], lhsT=wt[:, kc, mi * 128:(mi + 1) * 128], rhs=xnT[:, kc, TQ:TQ + 2], start=(kc == 0), stop=(kc == 15)),
                            reads=[wb_, B_xnT], writes=[PSB[pb2]], signal=(kc == 15))
                    S.op("dve", lambda h, pb=pb, mi=mi: h.tensor_tensor(out=tmpA[:, mi, 0:TQ], in0=PS[pb][:, :], in1=tmpA[:, mi, 0:TQ], op=ALU.mult),
                         reads=[PSB[pb], B_tmpA], writes=[B_tmpA])
                    S.op("dve", lambda h, pb2=pb2, mi=mi: h.tensor_tensor(out=tmpA[:, mi, TQ:TQ + 2], in0=PS[pb2][:, 0:2], in1=tmpA[:, mi, TQ:TQ + 2], op=ALU.mult),
                         reads=[PSB[pb2], B_tmpA], writes=[B_tmpA])
                    cw = lambda k_, c=c: cvec[:, V_CW + c * 3 + k_:V_CW + c * 3 + k_ + 1]
                    S.op("dve", lambda h, mi=mi, cw=cw: h.tensor_scalar(out=tmpB[:, mi, :], in0=tmpA[:, mi, 0:TQ], scalar1=cw(0), scalar2=None, op0=ALU.mult),
                         reads=[B_tmpA, B_cvec], writes=[B_tmpB])
                    S.op("dve", lambda h, mi=mi, cw=cw: h.scalar_tensor_tensor(out=tmpB[:, mi, :], in0=tmpA[:, mi, 1:TQ + 1], scalar=cw(1), in1=tmpB[:, mi, :],
                                                                               op0=ALU.mult, op1=ALU.add), reads=[B_tmpA, B_tmpB, B_cvec], writes=[B_tmpB])
                    S.op("dve", lambda h, mi=mi, cw=cw: h.scalar_tensor_tensor(out=tmpB[:, mi, :], in0=tmpA[:, mi, 2:TQ + 2], scalar=cw(2), in1=tmpB[:, mi, :],
                                                                               op0=ALU.mult, op1=ALU.add), reads=[B_tmpA, B_tmpB, B_cvec], writes=[B_tmpB])
                wt, wb_ = wget(wi); wi += 1
                for mi in range(NM):
                    pb = next_acc()
                    mm_group(pb, PS[pb][:, :], [(wt[:, kc, mi * 128:(mi + 1) * 128], xnT[:, kc, XO]) for kc in range(16)], reads=[wb_, B_xnT])
                    S.op("dve", lambda h, pb=pb, mi=mi: h.tensor_tensor(out=tmpB[:, mi, :], in0=PS[pb][:, :], in1=tmpB[:, mi, :], op=ALU.mult),
                         reads=[PSB[pb], B_tmpB], writes=[B_tmpB])
                wt, wb_ = wget(wi); wi += 1
                for mi in range(NM):
                    c = c0 // 128 + mi
                    pb = next_acc()
                    mm_group(pb, PS[pb][:, :], [(wt[:, kc, mi * 128:(mi + 1) * 128], xnT[:, kc, XO]) for kc in range(16)], reads=[wb_, B_xnT])
                    si = c % 2
                    S.op("act", lambda h, pb=pb, si=si: h.activation(out=silt[si][:], in_=PS[pb][:, :], func=AF.Silu),
                         reads=[PSB[pb]], writes=[B_sil[si]])
                    S.op("dve", lambda h, si=si, mi=mi, c=c: h.tensor_tensor(out=gcT[:, c, :], in0=tmpB[:, mi, :], in1=silt[si][:], op=ALU.mult),
                         reads=[B_tmpB, B_sil[si]], writes=[B_gcT])
            for c0 in range(0, 2048, WC):
                wt, wb_ = wget(wi); wi += 1
                for mi in range(NM):
                    c = c0 // 128 + mi
                    pb = next_acc()
                    mm_group(pb, PS[pb][:, :], [(wt[:, kc, mi * 128:(mi + 1) * 128], xnT[:, kc, XO]) for kc in range(16)], reads=[wb_, B_xnT])
                    S.op("act", lambda h, pb=pb, mi=mi, c=c: h.activation(out=tmpA[:, mi, 0:TQ], in_=PS[pb][:, :], func=AF.Sigmoid,
                                                                           bias=cvec[:, V_BG + c:V_BG + c + 1], scale=1.0),
                         reads=[PSB[pb], B_cvec], writes=[B_tmpA])
                wt, wb_ = wget(wi); wi += 1
                for mi in range(NM):
                    pb = next_acc()
                    mm_group(pb, PS[pb][:, :], [(wt[:, kc, mi * 128:(mi + 1) * 128], szT[:, kc, :]) for kc in range(16)], reads=[wb_] + B_szT)
                    S.op("dve", lambda h, pb=pb, mi=mi: h.tensor_tensor(out=tmpA[:, mi, 0:TQ], in0=PS[pb][:, :], in1=tmpA[:, mi, 0:TQ], op=ALU.mult),
                         reads=[PSB[pb], B_tmpA], writes=[B_tmpA])
                wt, wb_ = wget(wi); wi += 1
                for mi in range(NM):
                    c = c0 // 128 + mi
                    pb = next_acc()
                    mm_group(pb, PS[pb][:, :], [(wt[:, kc, mi * 128:(mi + 1) * 128], xnT[:, kc, XO]) for kc in range(16)], reads=[wb_, B_xnT])
                    S.op("act", lambda h, pb=pb, mi=mi, c=c: h.activation(out=tmpB[:, mi, :], in_=PS[pb][:, :], func=AF.Sigmoid,
                                                                           bias=cvec[:, V_BG + 16 + c:V_BG + 16 + c + 1], scale=1.0),
                         reads=[PSB[pb], B_cvec], writes=[B_tmpB])
                wt, wb_ = wget(wi); wi += 1
                for mi in range(NM):
                    c = c0 // 128 + mi
                    pb = next_acc()
                    mm_group(pb, PS[pb][:, :], [(wt[:, kc, mi * 128:(mi + 1) * 128], gcT[:, kc, :]) for kc in range(16)], reads=[wb_, B_gcT])
                    S.op("dve", lambda h, pb=pb, mi=mi: h.tensor_tensor(out=tmpB[:, mi, :], in0=PS[pb][:, :], in1=tmpB[:, mi, :], op=ALU.mult),
                         reads=[PSB[pb], B_tmpB], writes=[B_tmpB])
                    S.op("dve", lambda h, mi=mi, c=c: h.tensor_tensor(out=mgT[:, c, :], in0=tmpA[:, mi, 0:TQ], in1=tmpB[:, mi, :], op=ALU.add),
                         reads=[B_tmpA, B_tmpB], writes=[B_mgT])
            for s_ in range(4):
                S.dma("sp", hbuf[:, s_, :], job["xq"][r0 + 1 + s_ * 128:r0 + 1 + (s_ + 1) * 128, :], R_xin.s[s_], writes=[B_hb[s_]])
            for c0 in range(0, 2048, WC):
                wt, wb_ = wget(wi); wi += 1
                for s_ in range(4):
                    pb = next_acc()
                    mm_group(pb, PS[pb][:, 0:WC], [(mgT[:, kc, s_ * 128:(s_ + 1) * 128], wt[:, kc, :]) for kc in range(16)], reads=[wb_, B_mgT])
                    S.op("dve", lambda h, pb=pb, s_=s_, c0=c0: h.tensor_tensor(out=hbuf[:, s_, c0:c0 + WC], in0=PS[pb][:, 0:WC], in1=hbuf[:, s_, c0:c0 + WC], op=ALU.add),
                         reads=[PSB[pb], B_hb[s_]], writes=[B_hb[s_]])
            assert wi == len(plan)
            for s_ in range(4):
                xs_t, xs_b, _ = R_xsb.next()
                ss, ss_b = stat_slot()
                S.op("act", lambda h, s_=s_, xs_t=xs_t, ss=ss: h.activation(out=xs_t[:], in_=hbuf[:, s_, :], func=AF.Square, scale=float(D) ** -0.5, accum_out=ss),
                     reads=[B_hb[s_]], writes=[xs_b, ss_b])
                S.op("act", lambda h, ss=ss: h.activation(out=ss, in_=ss, func=AF.Sqrt, bias=EPS, scale=1.0), reads=[ss_b], writes=[ss_b])
                S.op("dve", lambda h, ss=ss: h.reciprocal(out=ss, in_=ss), reads=[ss_b], writes=[ss_b])
                S.op("dve", lambda h, s_=s_, ss=ss: h.scalar_tensor_tensor(out=hbuf[:, s_, :], in0=hbuf[:, s_, :], scalar=ss, in1=fnw[:], op0=ALU.mult, op1=ALU.mult),
                     reads=[B_hb[s_], ss_b, B_fnw], writes=[B_hb[s_]])
                S.dma("pool", job["y"][qt * TQ + s_ * 128:qt * TQ + (s_ + 1) * 128, :], hbuf[:, s_, :], sem_h[s_], reads=[B_hb[s_]])
        S.barrier()
        S.flush()
        free_to(n_global)

    free_to(0)
    for cm in reversed(psum_cms):
        cm.__exit__(None, None, None)
    S.close()
    return nc


def _rope_tables(npos):
    inv_freq = (np.float32(1.0) / (np.float32(10000.0) ** (np.arange(0, 64, 2, dtype=np.float32) / np.float32(64)))).astype(np.float32)
    ang = (np.arange(npos, dtype=np.float32)[:, None] * inv_freq[None, :]).astype(np.float32)
    cos = np.cos(ang).astype(np.float32).T
    sin = np.sin(ang).astype(np.float32).T
    tab = np.empty((2, 64, npos), np.float32)
    tab[0, 0:32] = cos
    tab[0, 32:64] = cos
    tab[1, 0:32] = sin
    tab[1, 32:64] = -sin
    return tab


_PROG_CACHE = {}


def kernel(x_prompt, x_sample, meta_tokens, norm_w, w_in, b_gate, q_a_norm_w, w_uq, kv_a_norm_w, w_ukv,
           w_o_attn, conv_w, w_o_conv, w_o, final_norm_w):
    f = lambda a: np.ascontiguousarray(np.asarray(a, dtype=np.float32))
    x_prompt, x_sample, meta = f(x_prompt), f(x_sample), f(meta_tokens)
    B, SP, _ = x_prompt.shape
    NS, SS, _ = x_sample.shape
    assert NS == 8 and B == 2 and SP == 4 * SS
    NQ = SS
    LS, LP = NMETA + SS, NMETA + SP
    key = (LS, LP, NQ)
    if key not in _PROG_CACHE:
        import os
        _PROG_CACHE[key] = build_program(LS, LP, NQ, stop=int(os.environ.get("K_STOP", "99")))
    nc = _PROG_CACHE[key]
    cvec = np.zeros((128, V_N), np.float32)
    cvec[:, V_NW:V_NW + 16] = f(norm_w)[0].reshape(16, 128).T
    cvec[:, V_QW:V_QW + 4] = f(q_a_norm_w)[0].reshape(4, 128).T
    cvec[:, V_KW:V_KW + 4] = f(kv_a_norm_w)[0].reshape(4, 128).T
    cw = f(conv_w)[0]
    cvec[:, V_CW:V_CW + 48] = cw.reshape(3, 16, 128).transpose(2, 1, 0).reshape(128, 48)
    cvec[:, V_BG:V_BG + 32] = f(b_gate)[0].reshape(32, 128).T
    fnw = np.ascontiguousarray(np.broadcast_to(f(final_norm_w)[None, :], (128, D)))
    ropek = _rope_tables(LP)
    zrow = np.zeros((1, D), np.float32)
    xps = [np.concatenate([meta, x_prompt[b], zrow], axis=0) for b in range(B)]
    shared = dict(w_in=f(w_in)[0], w_uq=f(w_uq)[0], w_ukv=f(w_ukv)[0], w_oa=f(w_o_attn)[0], w_oc=f(w_o_conv)[0],
                  w_o=f(w_o)[0], cvec=cvec, fnw=fnw, ropek=ropek)
    in_maps = []
    for c in range(8):
        b, j = c // 4, c % 4
        xs = np.concatenate([meta, x_sample[c], zrow], axis=0)
        r0 = NMETA - 1 + NQ * j
        xqp = np.ascontiguousarray(xps[b][r0:r0 + NQ + 2])
        ropeq = np.stack([ropek[:, :, NMETA:NMETA + NQ], ropek[:, :, NMETA + NQ * j:NMETA + NQ * (j + 1)]], axis=0)
        m = dict(shared)
        m.update(xs=xs, xp=xps[b], xqp=xqp, ropeq=np.ascontiguousarray(ropeq))
        in_maps.append(m)
    res = run_bass_kernel_spmd(nc, in_maps, core_ids=list(range(8)))
    y_prompt = np.empty((B, SP, D), np.float32)
    y_sample = np.empty((NS, SS, D), np.float32)
    for c in range(8):
        b, j = c // 4, c % 4
        y_sample[c] = res.results[c]["ys"]
        y_prompt[b, NQ * j:NQ * (j + 1)] = res.results[c]["yp"]
    return (y_prompt, y_sample)
```

```python
import numpy as np
import concourse.bass as bass
import concourse.mybir as mybir
from concourse.bass_utils import run_bass_kernel_spmd

F32 = mybir.dt.float32
BF16 = mybir.dt.bfloat16
AF = mybir.ActivationFunctionType
ALU = mybir.AluOpType

D = 2048
H = 16
NMETA = 16
INW = 15424
C_QA, C_CKV, C_KR, C_ZA, C_CX, C_CB, C_CC, C_ZC, C_GA, C_GC = 0, 512, 1024, 1088, 3136, 5184, 7232, 9280, 11328, 13376
EPS = 1e-6
SCALE = 192.0 ** -0.5
WC = 256
NWB = 3
TQ = 512
V_NW, V_QW, V_KW, V_CW, V_BG, V_N = 0, 16, 20, 24, 72, 104


class Buf:
    __slots__ = ("w", "r", "excl")

    def __init__(self, excl=False):
        self.w = None
        self.r = []
        self.excl = excl


class Sched:
    ENG = ("pe", "act", "dve", "pool", "sp")

    def __init__(self, nc):
        self.nc = nc
        self.ops = {e: [] for e in self.ENG}
        self.cnt = {}
        self.waited = {e: {} for e in self.ENG}
        self.sems = {}
        self._ctx = []
        self.pend = {e: ([], []) for e in self.ENG}
        for e in self.ENG:
            self.new_sem("p_" + e)

    def new_sem(self, name):
        cm = self.nc.semaphore(name)
        self.sems[name] = cm.__enter__()
        self._ctx.append(cm)
        self.cnt[name] = 0
        return name

    def _wait(self, eng, tok):
        if tok is None:
            return
        name, val = tok
        if self.waited[eng].get(name, 0) >= val:
            return
        self.waited[eng][name] = val
        sem = self.sems[name]
        self.ops[eng].append(lambda h, sem=sem, val=val: h.wait_ge(sem, val))

    def _hazards(self, eng, reads, writes):
        own = "p_" + eng
        for b in reads:
            self._wait(eng, b.w)
            if b.excl:
                for t in b.r:
                    if t[0] != own:
                        self._wait(eng, t)
        for b in writes:
            if b.w is not None and b.w[0] != own:
                self._wait(eng, b.w)
            for t in b.r:
                if t[0] != own:
                    self._wait(eng, t)

    def _assign(self, tok, reads, writes):
        for b in reads:
            b.r.append(tok)
        for b in writes:
            b.w = tok
            b.r = []

    def op(self, eng, fn, reads=(), writes=(), signal=True):
        self._hazards(eng, reads, writes)
        name = "p_" + eng
        pr, pw = self.pend[eng]
        if signal:
            self.cnt[name] += 1
            tok = (name, self.cnt[name])
            sem = self.sems[name]
            self.ops[eng].append(lambda h, fn=fn, sem=sem: fn(h).then_inc(sem, 1))
            self._assign(tok, list(reads) + pr, list(writes) + pw)
            pr.clear()
            pw.clear()
            return tok
        self.ops[eng].append(fn)
        pr.extend(reads)
        pw.extend(writes)
        return None

    def dma(self, eng, out, in_, sem, reads=(), writes=()):
        self._hazards(eng, reads, writes)
        self.cnt[sem] += 16
        tok = (sem, self.cnt[sem])
        s = self.sems[sem]
        self.ops[eng].append(lambda h, out=out, in_=in_, s=s: h.dma_start(out=out, in_=in_).then_inc(s, 16))
        self._assign(tok, reads, writes)
        return tok

    def barrier(self):
        for e in self.ENG:
            assert not self.pend[e][0] and not self.pend[e][1], e
        for e in self.ENG:
            for name, c in self.cnt.items():
                if c > 0 and name != "p_" + e:
                    self._wait(e, (name, c))

    def flush(self):
        S = self
        with self.nc.Block() as block:
            @block.tensor
            def _(h):
                for f in S.ops["pe"]:
                    f(h)

            @block.scalar
            def _(h):
                for f in S.ops["act"]:
                    f(h)

            @block.vector
            def _(h):
                for f in S.ops["dve"]:
                    f(h)

            @block.gpsimd
            def _(h):
                for f in S.ops["pool"]:
                    f(h)

            @block.sync
            def _(h):
                for f in S.ops["sp"]:
                    f(h)
        self.ops = {e: [] for e in self.ENG}

    def close(self):
        for cm in reversed(self._ctx):
            cm.__exit__(None, None, None)


class Ring:
    def __init__(self, S, name, tensors):
        self.t = tensors
        self.b = [Buf() for _ in tensors]
        self.s = [S.new_sem("%s%d" % (name, i)) for i in range(len(tensors))]
        self.i = 0

    def next(self):
        k = self.i % len(self.t)
        self.i += 1
        return self.t[k], self.b[k], self.s[k]


def build_program(LS, LP, NQ, stop=99):
    assert (LS - NMETA) % 512 == 0 and (LP - NMETA) % 512 == 0 and NQ % TQ == 0
    nc = bass.Bass("TRN2", target_bir_lowering=False)
    dram = lambda n, shp, dt, kind: nc.dram_tensor(n, list(shp), dt, kind=kind).ap()
    xs_d = dram("xs", (LS + 1, D), F32, "ExternalInput")
    xp_d = dram("xp", (LP + 1, D), F32, "ExternalInput")
    xqp_d = dram("xqp", (NQ + 2, D), F32, "ExternalInput")
    ropek_d = dram("ropek", (2, 64, LP), F32, "ExternalInput")
    ropeq_d = dram("ropeq", (2, 2, 64, NQ), F32, "ExternalInput")
    w_in_d = dram("w_in", (D, INW), F32, "ExternalInput")
    w_uq_d = dram("w_uq", (512, 3072), F32, "ExternalInput")
    w_ukv_d = dram("w_ukv", (512, 4096), F32, "ExternalInput")
    w_oa_d = dram("w_oa", (D, D), F32, "ExternalInput")
    w_oc_d = dram("w_oc", (D, D), F32, "ExternalInput")
    w_o_d = dram("w_o", (D, D), F32, "ExternalInput")
    cvec_d = dram("cvec", (128, V_N), F32, "ExternalInput")
    fnw_d = dram("fnw", (128, D), F32, "ExternalInput")
    ys_d = dram("ys", (NQ, D), F32, "ExternalOutput")
    yp_d = dram("yp", (NQ, D), F32, "ExternalOutput")
    wb_in = dram("wb_in", (D, INW), BF16, "Internal")
    wb_uq = dram("wb_uq", (512, 3072), BF16, "Internal")
    wb_uk = dram("wb_uk", (512, 2048), BF16, "Internal")
    wb_uv = dram("wb_uv", (512, 2048), BF16, "Internal")
    wb_oa = dram("wb_oa", (D, D), BF16, "Internal")
    wb_oc = dram("wb_oc", (D, D), BF16, "Internal")
    wb_o = dram("wb_o", (D, D), BF16, "Internal")
    NBLK = 1 + (LP - NMETA) // 128
    kT_d = dram("kT", (H, 128, LP), BF16, "Internal")
    krT_d = dram("krT", (64, LP), BF16, "Internal")
    V_d = dram("Vs", (H, 128, NBLK, 128), BF16, "Internal")

    S = Sched(nc)
    cms = []

    uid = [0]

    def sb(name, shape, dt):
        uid[0] += 1
        cm = nc.sbuf_tensor("sb%d_%s" % (uid[0], name), list(shape), dt)
        t = cm.__enter__()
        cms.append(cm)
        return t

    def free_to(n):
        while len(cms) > n:
            cms.pop().__exit__(None, None, None)

    psum_cms = [nc.psum_tensor("ps%d" % i, [128, 512], F32) for i in range(8)]
    PS = [cm.__enter__() for cm in psum_cms]
    PSB = [Buf(excl=True) for _ in range(8)]
    ident = sb("ident", (128, 128), BF16)
    ones_f = sb("ones_f", (128, 128), F32)
    cvec = sb("cvec", (128, V_N), F32)
    fnw = sb("fnw", (128, D), F32)
    xnT = sb("xnT", (128, 16, TQ + 2), BF16)
    hbuf = sb("hbuf", (128, 4, D), F32)
    xin = [hbuf[:, i, :] for i in range(4)]
    xsb = [sb("xsb%d" % i, (128, D), BF16) for i in range(2)]
    tmpA = sb("tmpA", (128, 4, TQ + 2), F32)
    stat = sb("stat", (128, 16), F32)
    B_ident, B_ones, B_cvec, B_fnw, B_xnT, B_tmpA = Buf(), Buf(), Buf(), Buf(), Buf(), Buf()
    R_xin = Ring(S, "xin", xin)
    R_xsb = Ring(S, "xsb", xsb)
    B_stat = [Buf() for _ in range(16)]
    stat_i = [0]
    sem_c = S.new_sem("cst")
    sem_st = [S.new_sem("st%d" % i) for i in range(4)]
    st_i = [0]
    n_global = len(cms)

    def store(out, in_, reads):
        s = sem_st[st_i[0] % 4]
        st_i[0] += 1
        return S.dma("pool", out, in_, s, reads=reads)

    def stat_slot():
        k = stat_i[0] % 16
        stat_i[0] += 1
        return stat[:, k:k + 1], B_stat[k]

    S.dma("sp", cvec[:], cvec_d, sem_c, writes=[B_cvec])
    S.dma("sp", fnw[:], fnw_d, sem_c, writes=[B_fnw])
    S.op("pool", lambda h: h.memset(ident[:], 0.0), writes=[B_ident])
    S.op("pool", lambda h: h.affine_select(out=ident[:], in_=ident[:], compare_op=ALU.not_equal, fill=1.0,
                                           base=0, pattern=[[-1, 128]], channel_multiplier=1),
         reads=[B_ident], writes=[B_ident])
    S.op("pool", lambda h: h.memset(ones_f[:], 1.0), writes=[B_ones])
    sem_w = S.new_sem("wcast")
    for r in range(0, D, 128):
        S.dma("pool", wb_in[r:r + 128, :], w_in_d[r:r + 128, :], sem_w)
    for r in range(0, D, 512):
        S.dma("pool", wb_oa[r:r + 512, :], w_oa_d[r:r + 512, :], sem_w)
        S.dma("pool", wb_oc[r:r + 512, :], w_oc_d[r:r + 512, :], sem_w)
        S.dma("pool", wb_o[r:r + 512, :], w_o_d[r:r + 512, :], sem_w)
    S.dma("pool", wb_uq, w_uq_d, sem_w)
    ukv4 = w_ukv_d.rearrange("k (h t d) -> k h t d", h=H, t=2)
    S.dma("pool", wb_uk.rearrange("k (h d) -> k h d", h=H), ukv4[:, :, 0, :], sem_w)
    S.dma("pool", wb_uv.rearrange("k (h d) -> k h d", h=H), ukv4[:, :, 1, :], sem_w)
    S.barrier()
    S.flush()

    def make_xnT_gen(x_ap, row0, n, col0, dst=None, dstB=None):
        dst = xnT if dst is None else dst
        dstB = B_xnT if dstB is None else dstB

        def stage_b(xs_t, xs_b, off, m):
            for half in range(2):
                pb = 6 + half
                pv = PS[pb].bitcast(BF16)
                for j in range(8):
                    kc = half * 8 + j
                    S.op("pe", lambda h, pv=pv, xs_t=xs_t, j=j, kc=kc, m=m: h.transpose(
                        out=pv[:, j * 128:j * 128 + m], in_=xs_t[0:m, kc * 128:(kc + 1) * 128], identity=ident[0:m, 0:m]),
                        reads=[xs_b, B_ident], writes=[PSB[pb]], signal=(j == 7))
                for j in range(8):
                    kc = half * 8 + j
                    d_ = dst[:, kc, col0 + off:col0 + off + m]
                    src = pv[:, j * 128:j * 128 + m]
                    nw = cvec[:, V_NW + kc:V_NW + kc + 1]
                    if half == 0:
                        S.op("dve", lambda h, d_=d_, src=src, nw=nw: h.tensor_scalar(
                            out=d_, in0=src, scalar1=nw, scalar2=None, op0=ALU.mult),
                            reads=[PSB[pb], B_cvec], writes=[dstB])
                    else:
                        S.op("act", lambda h, d_=d_, src=src, nw=nw: h.activation(
                            out=d_, in_=src, func=AF.Copy, scale=nw),
                            reads=[PSB[pb], B_cvec], writes=[dstB])

        off = 0
        prev = None
        while off < n:
            m = min(128, n - off)
            xt, xb_, xsem = R_xin.next()
            S.dma("sp", xt[0:m, :], x_ap[row0 + off:row0 + off + m, :], xsem, writes=[xb_])
            xs_t, xs_b, _ = R_xsb.next()
            ss, ss_b = stat_slot()
            S.op("act", lambda h, xt=xt, xs_t=xs_t, ss=ss, m=m: h.activation(
                out=xs_t[0:m, :], in_=xt[0:m, :], func=AF.Square, scale=float(D) ** -0.5, accum_out=ss[0:m, :]),
                reads=[xb_], writes=[xs_b, ss_b])
            S.op("act", lambda h, ss=ss, m=m: h.activation(out=ss[0:m, :], in_=ss[0:m, :], func=AF.Sqrt, bias=EPS, scale=1.0),
                 reads=[ss_b], writes=[ss_b])
            S.op("dve", lambda h, ss=ss, m=m: h.reciprocal(out=ss[0:m, :], in_=ss[0:m, :]), reads=[ss_b], writes=[ss_b])
            S.op("act", lambda h, xt=xt, xs_t=xs_t, ss=ss, m=m: h.activation(
                out=xs_t[0:m, :], in_=xt[0:m, :], func=AF.Copy, scale=ss[0:m, :]),
                reads=[xb_, ss_b], writes=[xs_b])
            if prev is not None:
                stage_b(*prev)
            prev = (xs_t, xs_b, off, m)
            off += m
            yield
        stage_b(*prev)
        yield

    def make_xnT(x_ap, row0, n, col0):
        for _ in make_xnT_gen(x_ap, row0, n, col0):
            pass

    def mm_group(pb, out_ap, pairs, reads):
        n = len(pairs)
        tok = None
        for i, (l, r) in enumerate(pairs):
            tok = S.op("pe", lambda h, l=l, r=r, i=i: h.matmul(out_ap, lhsT=l, rhs=r, start=(i == 0), stop=(i == n - 1)),
                       reads=reads, writes=[PSB[pb]], signal=(i == n - 1))
        return tok

    jobs = [
        dict(name="s", xk=xs_d, L=LS, xq=xs_d, xq_row0=NMETA - 1, rq=0, y=ys_d),
        dict(name="p", xk=xp_d, L=LP, xq=xqp_d, xq_row0=0, rq=1, y=yp_d),
    ]
    phase = [0]
    for job in jobs:
        L = job["L"]
        phase[0] += 1
        if phase[0] > stop:
            break
        wkv = sb("wkv", (128, 16, 576), BF16)
        wuk = sb("wuk", (128, 4, 2048), BF16)
        wuv = sb("wuv", (128, 4, 2048), BF16)
        ckvT = sb("ckvT", (128, 4, 512), BF16)
        rkb = sb("rkb", (128, 512), F32)
        rkc = sb("rkc", (128, 4), F32)
        kst = [sb("kst%d" % i, (128, 4, 512), BF16) for i in range(2)]
        vst = [sb("vst%d" % i, (128, 2048), BF16) for i in range(2)]
        krs = [sb("krs%d" % i, (64, 512), BF16) for i in range(2)]
        rtab = [sb("rtab%d" % i, (64, 2, 512), F32) for i in range(2)]
        rt1 = sb("rt1", (64, 512), F32)
        rt2 = sb("rt2", (64, 512), F32)
        B_wkv, B_wuk, B_wuv, B_ckvT, B_rkb, B_rkc, B_rt1, B_rt2 = [Buf() for _ in range(8)]
        R_kst = Ring(S, "kst" + job["name"], kst)
        R_vst = Ring(S, "vst" + job["name"], vst)
        R_krs = Ring(S, "krs" + job["name"], krs)
        R_rtab = Ring(S, "rtab" + job["name"], rtab)
        S.dma("sp", wkv[:], wb_in.rearrange("(kc p) c -> p kc c", p=128)[:, :, C_CKV:C_CKV + 576], sem_c, writes=[B_wkv])
        S.dma("sp", wuk[:], wb_uk.rearrange("(kc p) c -> p kc c", p=128), sem_c, writes=[B_wuk])
        S.dma("sp", wuv[:], wb_uv.rearrange("(kc p) c -> p kc c", p=128), sem_c, writes=[B_wuv])
        tiles = [(0, NMETA)] + [(NMETA + 512 * i, 512) for i in range((L - NMETA) // 512)]
        accb = [0]

        def next_acc():
            k = accb[0] % 6
            accb[0] += 1
            return k

        xnT2 = sb("xnT2", (128, 16, 512), BF16)
        B_xnT2 = Buf()
        xn_bufs = [(xnT, B_xnT), (xnT2, B_xnT2)]
        for _ in make_xnT_gen(job["xk"], tiles[0][0], tiles[0][1], 0, *xn_bufs[0]):
            pass
        for ti, (t0, n) in enumerate(tiles):
            xnT_c, B_xnT_c = xn_bufs[ti % 2]
            if ti + 1 < len(tiles):
                gen = make_xnT_gen(job["xk"], tiles[ti + 1][0], tiles[ti + 1][1], 0, *xn_bufs[(ti + 1) % 2])
            else:
                gen = iter(())
            rt, rt_b, rt_s = R_rtab.next()
            S.dma("sp", rt[:, :, 0:n], ropek_d[:, :, t0:t0 + n].rearrange("t r n -> r t n"), rt_s, writes=[rt_b])
            for mt in range(4):
                pb = next_acc()
                mm_group(pb, PS[pb][:, 0:n], [(wkv[:, kc, mt * 128:(mt + 1) * 128], xnT_c[:, kc, 0:n]) for kc in range(16)],
                         reads=[B_wkv, B_xnT_c])
                S.op("dve", lambda h, pb=pb, mt=mt, n=n: h.tensor_scalar(
                    out=ckvT[:, mt, 0:n], in0=PS[pb][:, 0:n], scalar1=cvec[:, V_KW + mt:V_KW + mt + 1], scalar2=None, op0=ALU.mult),
                    reads=[PSB[pb], B_cvec], writes=[B_ckvT])
                S.op("act", lambda h, pb=pb, mt=mt, n=n: h.activation(out=tmpA[:, mt, 0:n], in_=PS[pb][:, 0:n], func=AF.Square),
                     reads=[PSB[pb]], writes=[B_tmpA])
            pb = next_acc()
            mm_group(pb, PS[pb][0:64, 0:n], [(wkv[:, kc, 512:576], xnT_c[:, kc, 0:n]) for kc in range(16)], reads=[B_wkv, B_xnT_c])
            kr_t, kr_b, _ = R_krs.next()
            S.op("dve", lambda h, pb=pb, n=n, rt=rt: h.tensor_tensor(out=rt1[:, 0:n], in0=PS[pb][0:64, 0:n], in1=rt[:, 0, 0:n], op=ALU.mult),
                 reads=[PSB[pb], rt_b], writes=[B_rt1])
            S.op("dve", lambda h, pb=pb, n=n, rt=rt: h.tensor_tensor(out=rt2[0:32, 0:n], in0=PS[pb][32:64, 0:n], in1=rt[32:64, 1, 0:n], op=ALU.mult),
                 reads=[PSB[pb], rt_b], writes=[B_rt2])
            S.op("dve", lambda h, pb=pb, n=n, rt=rt: h.tensor_tensor(out=rt2[32:64, 0:n], in0=PS[pb][0:32, 0:n], in1=rt[0:32, 1, 0:n], op=ALU.mult),
                 reads=[PSB[pb], rt_b], writes=[B_rt2])
            S.op("dve", lambda h, n=n, kr_t=kr_t: h.tensor_tensor(out=kr_t[:, 0:n], in0=rt1[:, 0:n], in1=rt2[:, 0:n], op=ALU.add),
                 reads=[B_rt1, B_rt2], writes=[kr_b])
            store(krT_d[:, t0:t0 + n], kr_t[:, 0:n], reads=[kr_b])
            pb = next_acc()
            mm_group(pb, PS[pb][:, 0:n], [(ones_f[:], tmpA[:, mt, 0:n]) for mt in range(4)], reads=[B_ones, B_tmpA])
            S.op("act", lambda h, pb=pb, n=n: h.activation(out=rkb[:, 0:n], in_=PS[pb][:, 0:n], func=AF.Sqrt, bias=EPS, scale=1.0 / 512),
                 reads=[PSB[pb]], writes=[B_rkb])
            S.op("dve", lambda h, n=n: h.reciprocal(out=rkb[:, 0:n], in_=rkb[:, 0:n]), reads=[B_rkb], writes=[B_rkb])
            nsub = (n + 127) // 128
            pb = next_acc()
            for s_ in range(nsub):
                m = min(128, n - s_ * 128)
                S.op("pe", lambda h, pb=pb, s_=s_, m=m: h.matmul(PS[pb][0:m, s_:s_ + 1], lhsT=rkb[0:1, s_ * 128:s_ * 128 + m],
                                                                  rhs=ones_f[0:1, 0:1], start=True, stop=True),
                     reads=[B_rkb, B_ones], writes=[PSB[pb]], signal=(s_ == nsub - 1))
            S.op("dve", lambda h, pb=pb, nsub=nsub: h.tensor_copy(out=rkc[:, 0:nsub], in_=PS[pb][:, 0:nsub]),
                 reads=[PSB[pb]], writes=[B_rkc])
            for g in range(4):
                ks_t, ks_b, _ = R_kst.next()
                for hh in range(4):
                    hd = g * 4 + hh
                    pb = next_acc()
                    mm_group(pb, PS[pb][:, 0:n], [(wuk[:, kc, hd * 128:(hd + 1) * 128], ckvT[:, kc, 0:n]) for kc in range(4)],
                             reads=[B_wuk, B_ckvT])
                    S.op("dve", lambda h, pb=pb, hh=hh, n=n, ks_t=ks_t: h.tensor_tensor(
                        out=ks_t[:, hh, 0:n], in0=PS[pb][:, 0:n], in1=rkb[:, 0:n], op=ALU.mult),
                        reads=[PSB[pb], B_rkb], writes=[ks_b])
                store(kT_d[g * 4:(g + 1) * 4, :, t0:t0 + n].rearrange("h p t -> p h t"), ks_t[:, :, 0:n], reads=[ks_b])
                next(gen, None)
            for s_ in range(nsub):
                m = min(128, n - s_ * 128)
                blk = 0 if t0 == 0 else 1 + (t0 - NMETA) // 128 + s_
                vs_t, vs_b, _ = R_vst.next()
                for g in range(4):
                    pb = next_acc()
                    mm_group(pb, PS[pb][0:m, :], [(ckvT[:, kc, s_ * 128:s_ * 128 + m], wuv[:, kc, g * 512:(g + 1) * 512]) for kc in range(4)],
                             reads=[B_wuv, B_ckvT])
                    S.op("act", lambda h, pb=pb, g=g, m=m, s_=s_, vs_t=vs_t: h.activation(
                        out=vs_t[0:m, g * 512:(g + 1) * 512], in_=PS[pb][0:m, :], func=AF.Copy, scale=rkc[0:m, s_:s_ + 1]),
                        reads=[PSB[pb], B_rkc], writes=[vs_b])
                store(V_d[:, 0:m, blk, :].rearrange("h p d -> p h d"), vs_t[0:m, :].rearrange("p (h d) -> p h d", h=H), reads=[vs_b])
                next(gen, None)
            for _ in gen:
                pass
        S.barrier()
        S.flush()
        free_to(n_global)

        phase[0] += 1
        if phase[0] > stop:
            break
        CKB = 8 if (L - NMETA) % 1024 == 0 else 4
        CKT = CKB * 128
        nchunk = (L - NMETA) // CKT
        qanT = sb("qanT", (128, 4, TQ), BF16)
        rqb = sb("rqb", (128, TQ), F32)
        szT = sb("szT", (128, 16, TQ), BF16)
        gcT = sb("gcT", (128, 16, TQ), BF16)
        mgT = sb("mgT", (128, 16, TQ), BF16)
        tmpB = sb("tmpB", (128, 2, TQ), F32)
        silt = [sb("silt%d" % i, (128, TQ), F32) for i in range(2)]
        wts = [sb("wt%d" % i, (128, 16, WC), BF16) for i in range(NWB)]
        wqs = [sb("wq%d" % i, (128, 4, 192), BF16) for i in range(2)]
        kbs = [sb("kb%d" % i, (128, CKT + NMETA), BF16) for i in range(3)]
        krb = [sb("krb%d" % i, (128, CKT + NMETA), BF16) for i in range(3)]
        vbs = [sb("vb%d" % i, (128, CKB + 1, 132), BF16) for i in range(3)]
        pts = [sb("pt%d" % i, (128, TQ), BF16) for i in range(4)]
        qns = [sb("qn%d" % i, (128, TQ), BF16) for i in range(2)]
        qrs = [sb("qr%d" % i, (128, TQ), BF16) for i in range(2)]
        onb = sb("onb", (128, 4, 128), BF16)
        rq_tab = sb("rq_tab", (64, 2, TQ), F32)
        qt1 = sb("qt1", (64, TQ), F32)
        qt2 = sb("qt2", (64, TQ), F32)
        B_qanT, B_rqb, B_gcT, B_mgT, B_tmpB, B_onb, B_rqtab, B_qt1, B_qt2 = [Buf() for _ in range(9)]
        B_szT = [Buf() for _ in range(16)]
        B_hb = R_xin.b
        B_sil = [Buf(), Buf()]
        B_qr = [Buf(), Buf()]
        B_qn = [Buf(), Buf()]
        B_pt = [Buf() for _ in range(4)]
        nm = job["name"]
        R_wt = Ring(S, "wt" + nm, wts)
        R_wq = Ring(S, "wq" + nm, wqs)
        R_kb = Ring(S, "kb" + nm, kbs)
        R_krb = Ring(S, "krb" + nm, krb)
        R_vb = Ring(S, "vb" + nm, vbs)
        sem_q = S.new_sem("semq" + nm)
        sem_h = [S.new_sem("semh%s%d" % (nm, i)) for i in range(4)]
        for i in range(3):
            S.op("pool", lambda h, i=i: h.memset(krb[i][64:128, :], 0.0), writes=[R_krb.b[i]])
            S.op("pool", lambda h, i=i: h.memset(vbs[i][:, :, 128:129], 1.0), writes=[R_vb.b[i]])
        for i in range(2):
            S.op("pool", lambda h, i=i: h.memset(qrs[i][64:128, :], 0.0), writes=[B_qr[i]])

        wv_in = wb_in.rearrange("(kc p) c -> p kc c", p=128)
        wv_oa = wb_oa.rearrange("(kc p) c -> p kc c", p=128)
        wv_oc = wb_oc.rearrange("(kc p) c -> p kc c", p=128)
        wv_o = wb_o.rearrange("(kc p) c -> p kc c", p=128)
        wv_uq = wb_uq.rearrange("(kc p) c -> p kc c", p=128)

        for qt in range(NQ // TQ):
            r0 = job["xq_row0"] + qt * TQ
            plan = []
            for c0 in range(0, 512, WC):
                plan.append((wv_in, C_QA + c0))
            for c0 in range(0, 2048, WC):
                plan.append((wv_in, C_ZA + c0))
            n_pre = len(plan)
            for c0 in range(0, 2048, WC):
                for base in (C_CX, C_CC, C_CB, C_ZC):
                    plan.append((wv_in, base + c0))
            for c0 in range(0, 2048, WC):
                plan.append((wv_in, C_GA + c0))
                plan.append((wv_oa, c0))
                plan.append((wv_in, C_GC + c0))
                plan.append((wv_oc, c0))
            for c0 in range(0, 2048, WC):
                plan.append((wv_o, c0))
            loaded = {}
            nload = [0]

            def wload_upto(k):
                while nload[0] < min(k, len(plan)):
                    i = nload[0]
                    t, b, s = R_wt.next()
                    src, c0 = plan[i]
                    S.dma("sp", t[:], src[:, :, c0:c0 + WC], s, writes=[b])
                    loaded[i] = (t, b)
                    nload[0] += 1

            def wget(i):
                wload_upto(i + NWB)
                return loaded[i]

            accb = [0]

            def next_acc():
                k = accb[0] % 6
                accb[0] += 1
                return k

            make_xnT(job["xq"], r0, TQ + 2, 0)
            S.dma("sp", rq_tab[:], ropeq_d[job["rq"], :, :, qt * TQ:(qt + 1) * TQ].rearrange("t r n -> r t n"), sem_q, writes=[B_rqtab])
            XO = slice(1, TQ + 1)
            wi = 0
            for c0 in range(0, 512, WC):
                wt, wb_ = wget(wi); wi += 1
                for mi in range(WC // 128):
                    mt = (c0 // 128) + mi
                    pb = next_acc()
                    mm_group(pb, PS[pb][:, :], [(wt[:, kc, mi * 128:(mi + 1) * 128], xnT[:, kc, XO]) for kc in range(16)], reads=[wb_, B_xnT])
                    S.op("dve", lambda h, pb=pb, mt=mt: h.tensor_scalar(out=qanT[:, mt, :], in0=PS[pb][:, :], scalar1=cvec[:, V_QW + mt:V_QW + mt + 1],
                                                                      scalar2=None, op0=ALU.mult), reads=[PSB[pb], B_cvec], writes=[B_qanT])
                    S.op("act", lambda h, pb=pb, mt=mt: h.activation(out=tmpA[:, mt, 0:TQ], in_=PS[pb][:, :], func=AF.Square),
                         reads=[PSB[pb]], writes=[B_tmpA])
            pb = next_acc()
            mm_group(pb, PS[pb][:, :], [(ones_f[:], tmpA[:, mt, 0:TQ]) for mt in range(4)], reads=[B_ones, B_tmpA])
            S.op("act", lambda h, pb=pb: h.activation(out=rqb[:], in_=PS[pb][:, :], func=AF.Sqrt, bias=EPS, scale=1.0 / 512),
                 reads=[PSB[pb]], writes=[B_rqb])
            S.op("dve", lambda h: h.reciprocal(out=rqb[:], in_=rqb[:]), reads=[B_rqb], writes=[B_rqb])
            S.op("dve", lambda h: h.tensor_tensor(out=rq_tab[:, 0, :], in0=rq_tab[:, 0, :], in1=rqb[0:64, :], op=ALU.mult),
                 reads=[B_rqtab, B_rqb], writes=[B_rqtab])
            S.op("dve", lambda h: h.tensor_tensor(out=rq_tab[:, 1, :], in0=rq_tab[:, 1, :], in1=rqb[0:64, :], op=ALU.mult),
                 reads=[B_rqtab, B_rqb], writes=[B_rqtab])
            for c0 in range(0, 2048, WC):
                wt, wb_ = wget(wi); wi += 1
                for mi in range(WC // 128):
                    c = (c0 // 128) + mi
                    pb = next_acc()
                    mm_group(pb, PS[pb][:, :], [(wt[:, kc, mi * 128:(mi + 1) * 128], xnT[:, kc, XO]) for kc in range(16)], reads=[wb_, B_xnT])
                    S.op("act", lambda h, pb=pb, c=c: h.activation(out=szT[:, c, :], in_=PS[pb][:, :], func=AF.Silu),
                         reads=[PSB[pb]], writes=[B_szT[c]])
            assert wi == n_pre
            wload_upto(n_pre + NWB)

            ST = [0, 1, 2]
            OB = [[3, 4], [5, 6]]
            PQN, PQR = 7, 7

            def q_proj(hd):
                wq, wq_b, wq_s = R_wq.next()
                S.dma("sp", wq[:], wv_uq[:, :, hd * 192:(hd + 1) * 192], wq_s, writes=[wq_b])
                i = hd % 2
                mm_group(PQN, PS[PQN][:, :], [(wq[:, kc, 0:128], qanT[:, kc, :]) for kc in range(4)], reads=[wq_b, B_qanT])
                S.op("dve", lambda h, i=i: h.tensor_tensor(out=qns[i][:], in0=PS[PQN][:, :], in1=rqb[:], op=ALU.mult),
                     reads=[PSB[PQN], B_rqb], writes=[B_qn[i]])
                mm_group(PQR, PS[PQR][0:64, :], [(wq[:, kc, 128:192], qanT[:, kc, :]) for kc in range(4)], reads=[wq_b, B_qanT])
                S.op("dve", lambda h: h.tensor_tensor(out=qt1[:], in0=PS[PQR][0:64, :], in1=rq_tab[:, 0, :], op=ALU.mult),
                     reads=[PSB[PQR], B_rqtab], writes=[B_qt1])
                S.op("dve", lambda h: h.tensor_tensor(out=qt2[0:32, :], in0=PS[PQR][32:64, :], in1=rq_tab[32:64, 1, :], op=ALU.mult),
                     reads=[PSB[PQR], B_rqtab], writes=[B_qt2])
                S.op("dve", lambda h: h.tensor_tensor(out=qt2[32:64, :], in0=PS[PQR][0:32, :], in1=rq_tab[0:32, 1, :], op=ALU.mult),
                     reads=[PSB[PQR], B_rqtab], writes=[B_qt2])
                S.op("dve", lambda h, i=i: h.tensor_tensor(out=qrs[i][0:64, :], in0=qt1[:], in1=qt2[:], op=ALU.add),
                     reads=[B_qt1, B_qt2], writes=[B_qr[i]])

            items = []
            for hd in range(H):
                for ck in range(nchunk):
                    nb = CKB + (1 if ck == 0 else 0)
                    for bi in range(nb):
                        items.append((hd, ck, bi))
            chunk_bufs = {}

            def load_chunk(hd, ck):
                kt, kb_, ks_ = R_kb.next()
                krt, krb_, krs_ = R_krb.next()
                vt, vb_, vs_ = R_vb.next()
                tok0 = 0 if ck == 0 else NMETA + ck * CKT
                ntok = CKT + (NMETA if ck == 0 else 0)
                blk0 = 0 if ck == 0 else 1 + ck * CKB
                nb = CKB + (1 if ck == 0 else 0)
                S.dma("sp", kt[:, 0:ntok], kT_d[hd, :, tok0:tok0 + ntok], ks_, writes=[kb_])
                S.dma("sp", krt[0:64, 0:ntok], krT_d[:, tok0:tok0 + ntok], krs_, writes=[krb_])
                S.dma("sp", vt[:, 0:nb, 0:128], V_d[hd, :, blk0:blk0 + nb, :], vs_, writes=[vb_])
                chunk_bufs[(hd, ck)] = (kt, kb_, krt, krb_, vt, vb_)

            def item_geom(it):
                hd, ck, bi = it
                if ck == 0:
                    nk = NMETA if bi == 0 else 128
                    col = 0 if bi == 0 else NMETA + (bi - 1) * 128
                else:
                    nk = 128
                    col = bi * 128
                return nk, col

            def qk(k):
                hd, ck, bi = items[k]
                if bi == 0:
                    if (hd, ck) not in chunk_bufs:
                        load_chunk(hd, ck)
                    nxt = (hd, ck + 1) if ck + 1 < nchunk else ((hd + 1, 0) if hd + 1 < H else None)
                    if nxt is not None and nxt not in chunk_bufs:
                        load_chunk(*nxt)
                kt, kb_, krt, krb_, vt, vb_ = chunk_bufs[(hd, ck)]
                nk, col = item_geom(items[k])
                st = ST[k % 3]
                i = hd % 2
                S.op("pe", lambda h, st=st, nk=nk, col=col, kt=kt, i=i: h.matmul(
                    PS[st][0:nk, :], lhsT=kt[:, col:col + nk], rhs=qns[i][:], start=True, stop=False),
                    reads=[kb_, B_qn[i]], writes=[PSB[st]], signal=False)
                S.op("pe", lambda h, st=st, nk=nk, col=col, krt=krt, i=i: h.matmul(
                    PS[st][0:nk, :], lhsT=krt[:, col:col + nk], rhs=qrs[i][:], start=False, stop=True),
                    reads=[krb_, B_qr[i]], writes=[PSB[st]])
                j = k % 4
                S.op("act", lambda h, st=st, nk=nk, j=j: h.activation(out=pts[j][0:nk, :], in_=PS[st][0:nk, :], func=AF.Exp, scale=SCALE),
                     reads=[PSB[st]], writes=[B_pt[j]])

            def pv(k):
                hd, ck, bi = items[k]
                kt, kb_, krt, krb_, vt, vb_ = chunk_bufs[(hd, ck)]
                nk, col = item_geom(items[k])
                j = k % 4
                ob = OB[hd % 2]
                first = (ck == 0 and bi == 0)
                last = (ck == nchunk - 1 and bi == CKB + (1 if ck == 0 else 0) - 1)
                for qs in range(4):
                    bank = ob[qs // 2]
                    co = (qs % 2) * 256
                    S.op("pe", lambda h, bank=bank, co=co, nk=nk, j=j, qs=qs, vt=vt, bi=bi, first=first, last=last: h.matmul(
                        PS[bank][:, co:co + 129], lhsT=pts[j][0:nk, qs * 128:(qs + 1) * 128], rhs=vt[0:nk, bi, 0:129],
                        start=(first and qs % 2 == 0), stop=last, skip_group_check=True),
                        reads=[B_pt[j], vb_], writes=[PSB[bank]], signal=(qs == 3))
                if last and ck == nchunk - 1:
                    del chunk_bufs[(hd, ck)]

            def head_norm(hd):
                ob = OB[hd % 2]
                for qs in range(4):
                    bank = ob[qs // 2]
                    co = (qs % 2) * 256
                    rc, rc_b = stat_slot()
                    S.op("dve", lambda h, bank=bank, co=co, rc=rc: h.reciprocal(out=rc, in_=PS[bank][:, co + 128:co + 129]),
                         reads=[PSB[bank]], writes=[rc_b])
                    S.op("dve", lambda h, bank=bank, co=co, rc=rc, qs=qs: h.tensor_scalar(
                        out=onb[:, qs, :], in0=PS[bank][:, co:co + 128], scalar1=rc, scalar2=None, op0=ALU.mult),
                        reads=[PSB[bank], rc_b], writes=[B_onb])

            def head_out(hd):
                pv_ = PS[PQN].bitcast(BF16)
                for qs in range(4):
                    S.op("pe", lambda h, qs=qs, pv_=pv_: h.transpose(out=pv_[:, qs * 128:(qs + 1) * 128], in_=onb[:, qs, :], identity=ident[:]),
                         reads=[B_onb, B_ident], writes=[PSB[PQN]], signal=(qs == 3))
                S.op("dve", lambda h, hd=hd, pv_=pv_: h.tensor_tensor(out=szT[:, hd, :], in0=pv_[:, 0:TQ], in1=szT[:, hd, :], op=ALU.mult),
                     reads=[PSB[PQN], B_szT[hd]], writes=[B_szT[hd]])

            deferred = {}
            q_proj(0)
            qk(0)
            qk(1)
            nit = len(items)
            per_head = nit // H
            for k in range(nit):
                hd, ck, bi = items[k]
                if k + 2 < nit:
                    qk(k + 2)
                pv(k)
                kin = k % per_head
                if kin == per_head - 1:
                    head_norm(hd)
                    deferred[k + 2] = hd
                if kin == min(2, per_head - 2) and hd + 1 < H:
                    q_proj(hd + 1)
                if k in deferred:
                    head_out(deferred.pop(k))
            for k in sorted(deferred):
                head_out(deferred.pop(k))

            NM = WC // 128
            for c0 in range(0, 2048, WC):
                wt, wb_ = wget(wi); wi += 1
                for mi in range(NM):
                    pb = next_acc()
                    pb2 = next_acc()
                    for kc in range(16):
                        S.op("pe", lambda h, pb=pb, wt=wt, mi=mi, kc=kc: h.matmul(
                            PS[pb][:, :], lhsT=wt[:, kc, mi * 128:(mi + 1) * 128], rhs=xnT[:, kc, 0:TQ], start=(kc == 0), stop=(kc == 15)),
                            reads=[wb_, B_xnT], writes=[PSB[pb]], signal=False)
                        S.op("pe", lambda h, pb2=pb2, wt=wt, mi=mi, kc=kc: h.matmul(
                            PS[pb2][:, 0:2], lhsT=wt[:, kc, mi * 128:(mi + 1) * 128], rhs=xnT[:, kc, TQ:TQ + 2], start=(kc == 0), stop=(kc == 15)),
                            reads=[wb_, B_xnT], writes=[PSB[pb2]], signal=(kc == 15))
                    S.op("act", lambda h, pb=pb, mi=mi: h.activation(out=tmpA[:, mi, 0:TQ], in_=PS[pb][:, :], func=AF.Copy),
                         reads=[PSB[pb]], writes=[B_tmpA])
                    S.op("act", lambda h, pb2=pb2, mi=mi: h.activation(out=tmpA[:, mi, TQ:TQ + 2], in_=PS[pb2][:, 0:2], func=AF.Copy),
                         reads=[PSB[pb2]], writes=[B_tmpA])
                wt, wb_ = wget(wi); wi += 1
                for mi in range(NM):
                    c = c0 // 128 + mi
                    pb = next_acc()
                    pb2 = next_acc()
                    for kc in range(16):
                        S.op("pe", lambda h, pb=pb, wt=wt, mi=mi, kc=kc: h.matmul(
                            PS[pb][:, :], lhsT=wt[:, kc, mi * 128:(mi + 1) * 128], rhs=xnT[:, kc, 0:TQ], start=(kc == 0), stop=(kc == 15)),
                            reads=[wb_, B_xnT], writes=[PSB[pb]], signal=False)
                        S.op("pe", lambda h, pb2=pb2, wt=wt, mi=mi, kc=kc: h.matmul(
                            PS[pb2][:, 0:2], lhsT=wt[:, kc, mi * 128:(mi + 1) * 128], rhs=xnT[:, kc, TQ:TQ + 2], start=(kc == 0), stop=(kc == 15)),
                            reads=[wb_, B_xnT], writes=[PSB[pb2]], signal=(kc == 15))
                    S.op("dve", lambda h, pb=pb, mi=mi: h.tensor_tensor(out=tmpA[:, mi, 0:TQ], in0=PS[pb][:, :], in1=tmpA[:, mi, 0:TQ], op=ALU.mult),
                         reads=[PSB[pb], B_tmpA], writes=[B_tmpA])
                    S.op("dve", lambda h, pb2=pb2, mi=mi: h.tensor_tensor(out=tmpA[:, mi, TQ:TQ + 2], in0=PS[pb2][:, 0:2], in1=tmpA[:, mi, TQ:TQ + 2], op=ALU.mult),
                         reads=[PSB[pb2], B_tmpA], writes=[B_tmpA])
                    cw = lambda k_, c=c: cvec[:, V_CW + c * 3 + k_:V_CW + c * 3 + k_ + 1]
                    S.op("dve", lambda h, mi=mi, cw=cw: h.tensor_scalar(out=tmpB[:, mi, :], in0=tmpA[:, mi, 0:TQ], scalar1=cw(0), scalar2=None, op0=ALU.mult),
                         reads=[B_tmpA, B_cvec], writes=[B_tmpB])
                    S.op("dve", lambda h, mi=mi, cw=cw: h.scalar_tensor_tensor(out=tmpB[:, mi, :], in0=tmpA[:, mi, 1:TQ + 1], scalar=cw(1), in1=tmpB[:, mi, :],
                                                                               op0=ALU.mult, op1=ALU.add), reads=[B_tmpA, B_tmpB, B_cvec], writes=[B_tmpB])
                    S.op("dve", lambda h, mi=mi, cw=cw: h.scalar_tensor_tensor(out=tmpB[:, mi, :], in0=tmpA[:, mi, 2:TQ + 2], scalar=cw(2), in1=tmpB[:, mi, :],
                                                                               op0=ALU.mult, op1=ALU.add), reads=[B_tmpA, B_tmpB, B_cvec], writes=[B_tmpB])
                wt, wb_ = wget(wi); wi += 1
                for mi in range(NM):
                    pb = next_acc()
                    mm_group(pb, PS[pb][:, :], [(wt[:, kc, mi * 128:(mi + 1) * 128], xnT[:, kc, XO]) for kc in range(16)], reads=[wb_, B_xnT])
                    S.op("dve", lambda h, pb=pb, mi=mi: h.tensor_tensor(out=tmpB[:, mi, :], in0=PS[pb][:, :], in1=tmpB[:, mi, :], op=ALU.mult),
                         reads=[PSB[pb], B_tmpB], writes=[B_tmpB])
                wt, wb_ = wget(wi); wi += 1
                for mi in range(NM):
                    c = c0 // 128 + mi
                    pb = next_acc()
                    mm_group(pb, PS[pb][:, :], [(wt[:, kc, mi * 128:(mi + 1) * 128], xnT[:, kc, XO]) for kc in range(16)], reads=[wb_, B_xnT])
                    si = c % 2
                    S.op("act", lambda h, pb=pb, si=si: h.activation(out=silt[si][:], in_=PS[pb][:, :], func=AF.Silu),
                         reads=[PSB[pb]], writes=[B_sil[si]])
                    S.op("dve", lambda h, si=si, mi=mi, c=c: h.tensor_tensor(out=gcT[:, c, :], in0=tmpB[:, mi, :], in1=silt[si][:], op=ALU.mult),
                         reads=[B_tmpB, B_sil[si]], writes=[B_gcT])
            for c0 in range(0, 2048, WC):
                wt, wb_ = wget(wi); wi += 1
                for mi in range(NM):
                    c = c0 // 128 + mi
                    pb = next_acc()
                    mm_group(pb, PS[pb][:, :], [(wt[:, kc, mi * 128:(mi + 1) * 128], xnT[:, kc, XO]) for kc in range(16)], reads=[wb_, B_xnT])
                    S.op("act", lambda h, pb=pb, mi=mi, c=c: h.activation(out=tmpA[:, mi, 0:TQ], in_=PS[pb][:, :], func=AF.Sigmoid,
                                                                           bias=cvec[:, V_BG + c:V_BG + c + 1], scale=1.0),
                         reads=[PSB[pb], B_cvec], writes=[B_tmpA])
                wt, wb_ = wget(wi); wi += 1
                for mi in range(NM):
                    pb = next_acc()
                    mm_group(pb, PS[pb][:, :], [(wt[:, kc, mi * 128:(mi + 1) * 128], szT[:, kc, :]) for kc in range(16)], reads=[wb_] + B_szT)
                    S.op("dve", lambda h, pb=pb, mi=mi: h.tensor_tensor(out=tmpA[:, mi, 0:TQ], in0=PS[pb][:, :], in1=tmpA[:, mi, 0:TQ], op=ALU.mult),
                         reads=[PSB[pb], B_tmpA], writes=[B_tmpA])
                wt, wb_ = wget(wi); wi += 1
                for mi in range(NM):
                    c = c0 // 128 + mi
                    pb = next_acc()
                    mm_group(pb, PS[pb][:, :], [(wt[:, kc, mi * 128:(mi + 1) * 128], xnT[:, kc, XO]) for kc in range(16)], reads=[wb_, B_xnT])
                    S.op("act", lambda h, pb=pb, mi=mi, c=c: h.activation(out=tmpB[:, mi, :], in_=PS[pb][:, :], func=AF.Sigmoid,
                                                                           bias=cvec[:, V_BG + 16 + c:V_BG + 16 + c + 1], scale=1.0),
                         reads=[PSB[pb], B_cvec], writes=[B_tmpB])
                wt, wb_ = wget(wi); wi += 1
                for mi in range(NM):
                    c = c0 // 128 + mi
                    pb = next_acc()
                    mm_group(pb, PS[pb][:, :], [(wt[:, kc, mi * 128:(mi + 1) * 128], gcT[:, kc, :]) for kc in range(16)], reads=[wb_, B_gcT])
                    S.op("dve", lambda h, pb=pb, mi=mi: h.tensor_tensor(out=tmpB[:, mi, :], in0=PS[pb][:, :], in1=tmpB[:, mi, :], op=ALU.mult),
                         reads=[PSB[pb], B_tmpB], writes=[B_tmpB])
                    S.op("dve", lambda h, mi=mi, c=c: h.tensor_tensor(out=mgT[:, c, :], in0=tmpA[:, mi, 0:TQ], in1=tmpB[:, mi, :], op=ALU.add),
                         reads=[B_tmpA, B_tmpB], writes=[B_mgT])
            for s_ in range(4):
                S.dma("sp", hbuf[:, s_, :], job["xq"][r0 + 1 + s_ * 128:r0 + 1 + (s_ + 1) * 128, :], R_xin.s[s_], writes=[B_hb[s_]])
            for c0 in range(0, 2048, WC):
                wt, wb_ = wget(wi); wi += 1
                for s_ in range(4):
                    pb = next_acc()
                    mm_group(pb, PS[pb][:, 0:WC], [(mgT[:, kc, s_ * 128:(s_ + 1) * 128], wt[:, kc, :]) for kc in range(16)], reads=[wb_, B_mgT])
                    S.op("dve", lambda h, pb=pb, s_=s_, c0=c0: h.tensor_tensor(out=hbuf[:, s_, c0:c0 + WC], in0=PS[pb][:, 0:WC], in1=hbuf[:, s_, c0:c0 + WC], op=ALU.add),
                         reads=[PSB[pb], B_hb[s_]], writes=[B_hb[s_]])
            assert wi == len(plan)
            for s_ in range(4):
                xs_t, xs_b, _ = R_xsb.next()
                ss, ss_b = stat_slot()
                S.op("act", lambda h, s_=s_, xs_t=xs_t, ss=ss: h.activation(out=xs_t[:], in_=hbuf[:, s_, :], func=AF.Square, scale=float(D) ** -0.5, accum_out=ss),
                     reads=[B_hb[s_]], writes=[xs_b, ss_b])
                S.op("act", lambda h, ss=ss: h.activation(out=ss, in_=ss, func=AF.Sqrt, bias=EPS, scale=1.0), reads=[ss_b], writes=[ss_b])
                S.op("dve", lambda h, ss=ss: h.reciprocal(out=ss, in_=ss), reads=[ss_b], writes=[ss_b])
                S.op("dve", lambda h, s_=s_, ss=ss: h.scalar_tensor_tensor(out=hbuf[:, s_, :], in0=hbuf[:, s_, :], scalar=ss, in1=fnw[:], op0=ALU.mult, op1=ALU.mult),
                     reads=[B_hb[s_], ss_b, B_fnw], writes=[B_hb[s_]])
                S.dma("pool", job["y"][qt * TQ + s_ * 128:qt * TQ + (s_ + 1) * 128, :], hbuf[:, s_, :], sem_h[s_], reads=[B_hb[s_]])
        S.barrier()
        S.flush()
        free_to(n_global)

    free_to(0)
    for cm in reversed(psum_cms):
        cm.__exit__(None, None, None)
    S.close()
    return nc


def _rope_tables(npos):
    inv_freq = (np.float32(1.0) / (np.float32(10000.0) ** (np.arange(0, 64, 2, dtype=np.float32) / np.float32(64)))).astype(np.float32)
    ang = (np.arange(npos, dtype=np.float32)[:, None] * inv_freq[None, :]).astype(np.float32)
    cos = np.cos(ang).astype(np.float32).T
    sin = np.sin(ang).astype(np.float32).T
    tab = np.empty((2, 64, npos), np.float32)
    tab[0, 0:32] = cos
    tab[0, 32:64] = cos
    tab[1, 0:32] = sin
    tab[1, 32:64] = -sin
    return tab


_PROG_CACHE = {}


def kernel(x_prompt, x_sample, meta_tokens, norm_w, w_in, b_gate, q_a_norm_w, w_uq, kv_a_norm_w, w_ukv,
           w_o_attn, conv_w, w_o_conv, w_o, final_norm_w):
    f = lambda a: np.ascontiguousarray(np.asarray(a, dtype=np.float32))
    x_prompt, x_sample, meta = f(x_prompt), f(x_sample), f(meta_tokens)
    B, SP, _ = x_prompt.shape
    NS, SS, _ = x_sample.shape
    assert NS == 8 and B == 2 and SP == 4 * SS
    NQ = SS
    LS, LP = NMETA + SS, NMETA + SP
    key = (LS, LP, NQ)
    if key not in _PROG_CACHE:
        import os
        _PROG_CACHE[key] = build_program(LS, LP, NQ, stop=int(os.environ.get("K_STOP", "99")))
    nc = _PROG_CACHE[key]
    cvec = np.zeros((128, V_N), np.float32)
    cvec[:, V_NW:V_NW + 16] = f(norm_w)[0].reshape(16, 128).T
    cvec[:, V_QW:V_QW + 4] = f(q_a_norm_w)[0].reshape(4, 128).T
    cvec[:, V_KW:V_KW + 4] = f(kv_a_norm_w)[0].reshape(4, 128).T
    cw = f(conv_w)[0]
    cvec[:, V_CW:V_CW + 48] = cw.reshape(3, 16, 128).transpose(2, 1, 0).reshape(128, 48)
    cvec[:, V_BG:V_BG + 32] = f(b_gate)[0].reshape(32, 128).T
    fnw = np.ascontiguousarray(np.broadcast_to(f(final_norm_w)[None, :], (128, D)))
    ropek = _rope_tables(LP)
    zrow = np.zeros((1, D), np.float32)
    xps = [np.concatenate([meta, x_prompt[b], zrow], axis=0) for b in range(B)]
    shared = dict(w_in=f(w_in)[0], w_uq=f(w_uq)[0], w_ukv=f(w_ukv)[0], w_oa=f(w_o_attn)[0], w_oc=f(w_o_conv)[0],
                  w_o=f(w_o)[0], cvec=cvec, fnw=fnw, ropek=ropek)
    in_maps = []
    for c in range(8):
        b, j = c // 4, c % 4
        xs = np.concatenate([meta, x_sample[c], zrow], axis=0)
        r0 = NMETA - 1 + NQ * j
        xqp = np.ascontiguousarray(xps[b][r0:r0 + NQ + 2])
        ropeq = np.stack([ropek[:, :, NMETA:NMETA + NQ], ropek[:, :, NMETA + NQ * j:NMETA + NQ * (j + 1)]], axis=0)
        m = dict(shared)
        m.update(xs=xs, xp=xps[b], xqp=xqp, ropeq=np.ascontiguousarray(ropeq))
        in_maps.append(m)
    res = run_bass_kernel_spmd(nc, in_maps, core_ids=list(range(8)))
    y_prompt = np.empty((B, SP, D), np.float32)
    y_sample = np.empty((NS, SS, D), np.float32)
    for c in range(8):
        b, j = c // 4, c % 4
        y_sample[c] = res.results[c]["ys"]
        y_prompt[b, NQ * j:NQ * (j + 1)] = res.results[c]["yp"]
    return (y_prompt, y_sample)
```

```python
import numpy as np
import concourse.bass as bass
import concourse.mybir as mybir
from concourse.bass_utils import run_bass_kernel_spmd

F32 = mybir.dt.float32
BF16 = mybir.dt.bfloat16
AF = mybir.ActivationFunctionType
ALU = mybir.AluOpType

D = 2048
H = 16
NMETA = 16
INW = 15424
C_QA, C_CKV, C_KR, C_ZA, C_CX, C_CB, C_CC, C_ZC, C_GA, C_GC = 0, 512, 1024, 1088, 3136, 5184, 7232, 9280, 11328, 13376
EPS = 1e-6
SCALE = 192.0 ** -0.5
WC = 256
NWB = 3
TQ = 512
V_NW, V_QW, V_KW, V_CW, V_BG, V_N = 0, 16, 20, 24, 72, 104


class Buf:
    __slots__ = ("w", "r", "excl")

    def __init__(self, excl=False):
        self.w = None
        self.r = []
        self.excl = excl


class Sched:
    ENG = ("pe", "act", "dve", "pool", "sp")

    def __init__(self, nc):
        self.nc = nc
        self.ops = {e: [] for e in self.ENG}
        self.cnt = {}
        self.waited = {e: {} for e in self.ENG}
        self.sems = {}
        self._ctx = []
        self.pend = {e: ([], []) for e in self.ENG}
        for e in self.ENG:
            self.new_sem("p_" + e)

    def new_sem(self, name):
        cm = self.nc.semaphore(name)
        self.sems[name] = cm.__enter__()
        self._ctx.append(cm)
        self.cnt[name] = 0
        return name

    def _wait(self, eng, tok):
        if tok is None:
            return
        name, val = tok
        if self.waited[eng].get(name, 0) >= val:
            return
        self.waited[eng][name] = val
        sem = self.sems[name]
        self.ops[eng].append(lambda h, sem=sem, val=val: h.wait_ge(sem, val))

    def _hazards(self, eng, reads, writes):
        own = "p_" + eng
        for b in reads:
            self._wait(eng, b.w)
            if b.excl:
                for t in b.r:
                    if t[0] != own:
                        self._wait(eng, t)
        for b in writes:
            if b.w is not None and b.w[0] != own:
                self._wait(eng, b.w)
            for t in b.r:
                if t[0] != own:
                    self._wait(eng, t)

    def _assign(self, tok, reads, writes):
        for b in reads:
            b.r.append(tok)
        for b in writes:
            b.w = tok
            b.r = []

    def op(self, eng, fn, reads=(), writes=(), signal=True):
        self._hazards(eng, reads, writes)
        name = "p_" + eng
        pr, pw = self.pend[eng]
        if signal:
            self.cnt[name] += 1
            tok = (name, self.cnt[name])
            sem = self.sems[name]
            self.ops[eng].append(lambda h, fn=fn, sem=sem: fn(h).then_inc(sem, 1))
            self._assign(tok, list(reads) + pr, list(writes) + pw)
            pr.clear()
            pw.clear()
            return tok
        self.ops[eng].append(fn)
        pr.extend(reads)
        pw.extend(writes)
        return None

    def dma(self, eng, out, in_, sem, reads=(), writes=()):
        self._hazards(eng, reads, writes)
        self.cnt[sem] += 16
        tok = (sem, self.cnt[sem])
        s = self.sems[sem]
        self.ops[eng].append(lambda h, out=out, in_=in_, s=s: h.dma_start(out=out, in_=in_).then_inc(s, 16))
        self._assign(tok, reads, writes)
        return tok

    def barrier(self):
        for e in self.ENG:
            assert not self.pend[e][0] and not self.pend[e][1], e
        for e in self.ENG:
            for name, c in self.cnt.items():
                if c > 0 and name != "p_" + e:
                    self._wait(e, (name, c))

    def flush(self):
        S = self
        with self.nc.Block() as block:
            @block.tensor
            def _(h):
                for f in S.ops["pe"]:
                    f(h)

            @block.scalar
            def _(h):
                for f in S.ops["act"]:
                    f(h)

            @block.vector
            def _(h):
                for f in S.ops["dve"]:
                    f(h)

            @block.gpsimd
            def _(h):
                for f in S.ops["pool"]:
                    f(h)

            @block.sync
            def _(h):
                for f in S.ops["sp"]:
                    f(h)
        self.ops = {e: [] for e in self.ENG}

    def close(self):
        for cm in reversed(self._ctx):
            cm.__exit__(None, None, None)


class Ring:
    def __init__(self, S, name, tensors):
        self.t = tensors
        self.b = [Buf() for _ in tensors]
        self.s = [S.new_sem("%s%d" % (name, i)) for i in range(len(tensors))]
        self.i = 0

    def next(self):
        k = self.i % len(self.t)
        self.i += 1
        return self.t[k], self.b[k], self.s[k]


def build_program(LS, LP, NQ, stop=99):
    assert (LS - NMETA) % 512 == 0 and (LP - NMETA) % 512 == 0 and NQ % TQ == 0
    nc = bass.Bass("TRN2", target_bir_lowering=False)
    dram = lambda n, shp, dt, kind: nc.dram_tensor(n, list(shp), dt, kind=kind).ap()
    xs_d = dram("xs", (LS + 1, D), F32, "ExternalInput")
    xp_d = dram("xp", (LP + 1, D), F32, "ExternalInput")
    xqp_d = dram("xqp", (NQ + 2, D), F32, "ExternalInput")
    ropek_d = dram("ropek", (2, 64, LP), F32, "ExternalInput")
    ropeq_d = dram("ropeq", (2, 2, 64, NQ), F32, "ExternalInput")
    w_in_d = dram("w_in", (D, INW), F32, "ExternalInput")
    w_uq_d = dram("w_uq", (512, 3072), F32, "ExternalInput")
    w_ukv_d = dram("w_ukv", (512, 4096), F32, "ExternalInput")
    w_oa_d = dram("w_oa", (D, D), F32, "ExternalInput")
    w_oc_d = dram("w_oc", (D, D), F32, "ExternalInput")
    w_o_d = dram("w_o", (D, D), F32, "ExternalInput")
    cvec_d = dram("cvec", (128, V_N), F32, "ExternalInput")
    fnw_d = dram("fnw", (128, D), F32, "ExternalInput")
    ys_d = dram("ys", (NQ, D), F32, "ExternalOutput")
    yp_d = dram("yp", (NQ, D), F32, "ExternalOutput")
    wb_in = dram("wb_in", (D, INW), BF16, "Internal")
    wb_uq = dram("wb_uq", (512, 3072), BF16, "Internal")
    wb_uk = dram("wb_uk", (512, 2048), BF16, "Internal")
    wb_uv = dram("wb_uv", (512, 2048), BF16, "Internal")
    wb_oa = dram("wb_oa", (D, D), BF16, "Internal")
    wb_oc = dram("wb_oc", (D, D), BF16, "Internal")
    wb_o = dram("wb_o", (D, D), BF16, "Internal")
    NBLK = 1 + (LP - NMETA) // 128
    kT_d = dram("kT", (H, 128, LP), BF16, "Internal")
    krT_d = dram("krT", (64, LP), BF16, "Internal")
    V_d = dram("Vs", (H, 128, NBLK, 128), BF16, "Internal")

    S = Sched(nc)
    cms = []

    uid = [0]

    def sb(name, shape, dt):
        uid[0] += 1
        cm = nc.sbuf_tensor("sb%d_%s" % (uid[0], name), list(shape), dt)
        t = cm.__enter__()
        cms.append(cm)
        return t

    def free_to(n):
        while len(cms) > n:
            cms.pop().__exit__(None, None, None)

    psum_cms = [nc.psum_tensor("ps%d" % i, [128, 512], F32) for i in range(8)]
    PS = [cm.__enter__() for cm in psum_cms]
    PSB = [Buf(excl=True) for _ in range(8)]
    ident = sb("ident", (128, 128), BF16)
    ones_f = sb("ones_f", (128, 128), F32)
    cvec = sb("cvec", (128, V_N), F32)
    fnw = sb("fnw", (128, D), F32)
    xnT = sb("xnT", (128, 16, TQ + 2), BF16)
    hbuf = sb("hbuf", (128, 4, D), F32)
    xin = [hbuf[:, i, :] for i in range(4)]
    xsb = [sb("xsb%d" % i, (128, D), BF16) for i in range(2)]
    tmpA = sb("tmpA", (128, 4, TQ + 2), F32)
    stat = sb("stat", (128, 16), F32)
    B_ident, B_ones, B_cvec, B_fnw, B_xnT, B_tmpA = Buf(), Buf(), Buf(), Buf(), Buf(), Buf()
    R_xin = Ring(S, "xin", xin)
    R_xsb = Ring(S, "xsb", xsb)
    B_stat = [Buf() for _ in range(16)]
    stat_i = [0]
    csem_i = [0]

    def csem():
        csem_i[0] += 1
        return S.new_sem("cst%d" % csem_i[0])
    sem_st = [S.new_sem("st%d" % i) for i in range(4)]
    st_i = [0]
    n_global = len(cms)

    def store(out, in_, reads, sem):
        return S.dma("pool", out, in_, sem, reads=reads)

    def stat_slot():
        k = stat_i[0] % 16
        stat_i[0] += 1
        return stat[:, k:k + 1], B_stat[k]

    S.dma("sp", cvec[:], cvec_d, csem(), writes=[B_cvec])
    S.dma("sp", fnw[:], fnw_d, csem(), writes=[B_fnw])
    S.op("pool", lambda h: h.memset(ident[:], 0.0), writes=[B_ident])
    S.op("pool", lambda h: h.affine_select(out=ident[:], in_=ident[:], compare_op=ALU.not_equal, fill=1.0,
                                           base=0, pattern=[[-1, 128]], channel_multiplier=1),
         reads=[B_ident], writes=[B_ident])
    S.op("pool", lambda h: h.memset(ones_f[:], 1.0), writes=[B_ones])
    sem_w = S.new_sem("wcast")
    sem_wk = S.new_sem("wcastk")
    B_wbkv = Buf()
    ukv4 = w_ukv_d.rearrange("k (h t d) -> k h t d", h=H, t=2)
    S.dma("pool", wb_in[:, C_CKV:C_CKV + 576], w_in_d[:, C_CKV:C_CKV + 576], sem_wk, writes=[B_wbkv])
    S.dma("pool", wb_uk.rearrange("k (h d) -> k h d", h=H), ukv4[:, :, 0, :], sem_wk, writes=[B_wbkv])
    S.dma("pool", wb_uv.rearrange("k (h d) -> k h d", h=H), ukv4[:, :, 1, :], sem_wk, writes=[B_wbkv])
    for r in range(0, D, 128):
        S.dma("pool", wb_in[r:r + 128, 0:C_CKV], w_in_d[r:r + 128, 0:C_CKV], sem_w)
        S.dma("pool", wb_in[r:r + 128, C_ZA:INW], w_in_d[r:r + 128, C_ZA:INW], sem_w)
    for r in range(0, D, 512):
        S.dma("pool", wb_oa[r:r + 512, :], w_oa_d[r:r + 512, :], sem_w)
        S.dma("pool", wb_oc[r:r + 512, :], w_oc_d[r:r + 512, :], sem_w)
        S.dma("pool", wb_o[r:r + 512, :], w_o_d[r:r + 512, :], sem_w)
    S.dma("pool", wb_uq, w_uq_d, sem_w)

    def make_xnT_gen(x_ap, row0, n, col0, dst=None, dstB=None):
        dst = xnT if dst is None else dst
        dstB = B_xnT if dstB is None else dstB

        def stage_b(xs_t, xs_b, off, m):
            for half in range(2):
                pb = 6 + half
                pv = PS[pb].bitcast(BF16)
                for j in range(8):
                    kc = half * 8 + j
                    S.op("pe", lambda h, pv=pv, xs_t=xs_t, j=j, kc=kc, m=m: h.transpose(
                        out=pv[:, j * 128:j * 128 + m], in_=xs_t[0:m, kc * 128:(kc + 1) * 128], identity=ident[0:m, 0:m]),
                        reads=[xs_b, B_ident], writes=[PSB[pb]], signal=(j == 7))
                for j in range(8):
                    kc = half * 8 + j
                    d_ = dst[:, kc, col0 + off:col0 + off + m]
                    src = pv[:, j * 128:j * 128 + m]
                    nw = cvec[:, V_NW + kc:V_NW + kc + 1]
                    if half == 0:
                        S.op("dve", lambda h, d_=d_, src=src, nw=nw: h.tensor_scalar(
                            out=d_, in0=src, scalar1=nw, scalar2=None, op0=ALU.mult),
                            reads=[PSB[pb], B_cvec], writes=[dstB])
                    else:
                        S.op("act", lambda h, d_=d_, src=src, nw=nw: h.activation(
                            out=d_, in_=src, func=AF.Copy, scale=nw),
                            reads=[PSB[pb], B_cvec], writes=[dstB])

        off = 0
        prev = None
        while off < n:
            m = min(128, n - off)
            xt, xb_, xsem = R_xin.next()
            S.dma("sp", xt[0:m, :], x_ap[row0 + off:row0 + off + m, :], xsem, writes=[xb_])
            xs_t, xs_b, _ = R_xsb.next()
            ss, ss_b = stat_slot()
            S.op("act", lambda h, xt=xt, xs_t=xs_t, ss=ss, m=m: h.activation(
                out=xs_t[0:m, :], in_=xt[0:m, :], func=AF.Square, scale=float(D) ** -0.5, accum_out=ss[0:m, :]),
                reads=[xb_], writes=[xs_b, ss_b])
            S.op("act", lambda h, ss=ss, m=m: h.activation(out=ss[0:m, :], in_=ss[0:m, :], func=AF.Sqrt, bias=EPS, scale=1.0),
                 reads=[ss_b], writes=[ss_b])
            S.op("dve", lambda h, ss=ss, m=m: h.reciprocal(out=ss[0:m, :], in_=ss[0:m, :]), reads=[ss_b], writes=[ss_b])
            S.op("act", lambda h, xt=xt, xs_t=xs_t, ss=ss, m=m: h.activation(
                out=xs_t[0:m, :], in_=xt[0:m, :], func=AF.Copy, scale=ss[0:m, :]),
                reads=[xb_, ss_b], writes=[xs_b])
            if prev is not None:
                stage_b(*prev)
            prev = (xs_t, xs_b, off, m)
            off += m
            yield
        stage_b(*prev)
        yield

    def make_xnT(x_ap, row0, n, col0):
        for _ in make_xnT_gen(x_ap, row0, n, col0):
            pass

    def mm_group(pb, out_ap, pairs, reads):
        n = len(pairs)
        tok = None
        for i, (l, r) in enumerate(pairs):
            tok = S.op("pe", lambda h, l=l, r=r, i=i: h.matmul(out_ap, lhsT=l, rhs=r, start=(i == 0), stop=(i == n - 1)),
                       reads=reads, writes=[PSB[pb]], signal=(i == n - 1))
        return tok

    jobs = [
        dict(name="s", xk=xs_d, L=LS, xq=xs_d, xq_row0=NMETA - 1, rq=0, y=ys_d),
        dict(name="p", xk=xp_d, L=LP, xq=xqp_d, xq_row0=0, rq=1, y=yp_d),
    ]
    phase = [0]
    for job in jobs:
        L = job["L"]
        phase[0] += 1
        if phase[0] > stop:
            break
        wkv = sb("wkv", (128, 16, 576), BF16)
        wuk = sb("wuk", (128, 4, 2048), BF16)
        wuv = sb("wuv", (128, 4, 2048), BF16)
        ckvT = sb("ckvT", (128, 4, 512), BF16)
        rkb = sb("rkb", (128, 512), F32)
        rkc = sb("rkc", (128, 4), F32)
        kst = [sb("kst%d" % i, (128, 4, 512), BF16) for i in range(2)]
        vst = [sb("vst%d" % i, (128, 2048), BF16) for i in range(2)]
        krs = [sb("krs%d" % i, (64, 512), BF16) for i in range(2)]
        rtab = [sb("rtab%d" % i, (64, 2, 512), F32) for i in range(2)]
        rt1 = sb("rt1", (64, 512), F32)
        rt2 = sb("rt2", (64, 512), F32)
        B_wkv, B_wuk, B_wuv, B_ckvT, B_rkb, B_rkc, B_rt1, B_rt2 = [Buf() for _ in range(8)]
        R_kst = Ring(S, "kst" + job["name"], kst)
        R_vst = Ring(S, "vst" + job["name"], vst)
        R_krs = Ring(S, "krs" + job["name"], krs)
        R_rtab = Ring(S, "rtab" + job["name"], rtab)
        S.dma("sp", wkv[:], wb_in.rearrange("(kc p) c -> p kc c", p=128)[:, :, C_CKV:C_CKV + 576], csem(), reads=[B_wbkv], writes=[B_wkv])
        S.dma("sp", wuk[:], wb_uk.rearrange("(kc p) c -> p kc c", p=128), csem(), reads=[B_wbkv], writes=[B_wuk])
        S.dma("sp", wuv[:], wb_uv.rearrange("(kc p) c -> p kc c", p=128), csem(), reads=[B_wbkv], writes=[B_wuv])
        tiles = [(0, NMETA)] + [(NMETA + 512 * i, 512) for i in range((L - NMETA) // 512)]
        accb = [0]

        def next_acc():
            k = accb[0] % 6
            accb[0] += 1
            return k

        xnT2 = sb("xnT2", (128, 16, 512), BF16)
        B_xnT2 = Buf()
        xn_bufs = [(xnT, B_xnT), (xnT2, B_xnT2)]
        for _ in make_xnT_gen(job["xk"], tiles[0][0], tiles[0][1], 0, *xn_bufs[0]):
            pass
        for ti, (t0, n) in enumerate(tiles):
            xnT_c, B_xnT_c = xn_bufs[ti % 2]
            if ti + 1 < len(tiles):
                gen = make_xnT_gen(job["xk"], tiles[ti + 1][0], tiles[ti + 1][1], 0, *xn_bufs[(ti + 1) % 2])
            else:
                gen = iter(())
            rt, rt_b, rt_s = R_rtab.next()
            S.dma("sp", rt[:, :, 0:n], ropek_d[:, :, t0:t0 + n].rearrange("t r n -> r t n"), rt_s, writes=[rt_b])
            for mt in range(4):
                pb = next_acc()
                mm_group(pb, PS[pb][:, 0:n], [(wkv[:, kc, mt * 128:(mt + 1) * 128], xnT_c[:, kc, 0:n]) for kc in range(16)],
                         reads=[B_wkv, B_xnT_c])
                S.op("dve", lambda h, pb=pb, mt=mt, n=n: h.tensor_scalar(
                    out=ckvT[:, mt, 0:n], in0=PS[pb][:, 0:n], scalar1=cvec[:, V_KW + mt:V_KW + mt + 1], scalar2=None, op0=ALU.mult),
                    reads=[PSB[pb], B_cvec], writes=[B_ckvT])
                S.op("act", lambda h, pb=pb, mt=mt, n=n: h.activation(out=tmpA[:, mt, 0:n], in_=PS[pb][:, 0:n], func=AF.Square),
                     reads=[PSB[pb]], writes=[B_tmpA])
            pb = next_acc()
            mm_group(pb, PS[pb][0:64, 0:n], [(wkv[:, kc, 512:576], xnT_c[:, kc, 0:n]) for kc in range(16)], reads=[B_wkv, B_xnT_c])
            kr_t, kr_b, kr_s = R_krs.next()
            S.op("dve", lambda h, pb=pb, n=n, rt=rt: h.tensor_tensor(out=rt1[:, 0:n], in0=PS[pb][0:64, 0:n], in1=rt[:, 0, 0:n], op=ALU.mult),
                 reads=[PSB[pb], rt_b], writes=[B_rt1])
            S.op("dve", lambda h, pb=pb, n=n, rt=rt: h.tensor_tensor(out=rt2[0:32, 0:n], in0=PS[pb][32:64, 0:n], in1=rt[32:64, 1, 0:n], op=ALU.mult),
                 reads=[PSB[pb], rt_b], writes=[B_rt2])
            S.op("dve", lambda h, pb=pb, n=n, rt=rt: h.tensor_tensor(out=rt2[32:64, 0:n], in0=PS[pb][0:32, 0:n], in1=rt[0:32, 1, 0:n], op=ALU.mult),
                 reads=[PSB[pb], rt_b], writes=[B_rt2])
            S.op("dve", lambda h, n=n, kr_t=kr_t: h.tensor_tensor(out=kr_t[:, 0:n], in0=rt1[:, 0:n], in1=rt2[:, 0:n], op=ALU.add),
                 reads=[B_rt1, B_rt2], writes=[kr_b])
            store(krT_d[:, t0:t0 + n], kr_t[:, 0:n], reads=[kr_b], sem=kr_s)
            pb = next_acc()
            mm_group(pb, PS[pb][:, 0:n], [(ones_f[:], tmpA[:, mt, 0:n]) for mt in range(4)], reads=[B_ones, B_tmpA])
            S.op("act", lambda h, pb=pb, n=n: h.activation(out=rkb[:, 0:n], in_=PS[pb][:, 0:n], func=AF.Sqrt, bias=EPS, scale=1.0 / 512),
                 reads=[PSB[pb]], writes=[B_rkb])
            S.op("dve", lambda h, n=n: h.reciprocal(out=rkb[:, 0:n], in_=rkb[:, 0:n]), reads=[B_rkb], writes=[B_rkb])
            nsub = (n + 127) // 128
            pb = next_acc()
            for s_ in range(nsub):
                m = min(128, n - s_ * 128)
                S.op("pe", lambda h, pb=pb, s_=s_, m=m: h.matmul(PS[pb][0:m, s_:s_ + 1], lhsT=rkb[0:1, s_ * 128:s_ * 128 + m],
                                                                  rhs=ones_f[0:1, 0:1], start=True, stop=True),
                     reads=[B_rkb, B_ones], writes=[PSB[pb]], signal=(s_ == nsub - 1))
            S.op("dve", lambda h, pb=pb, nsub=nsub: h.tensor_copy(out=rkc[:, 0:nsub], in_=PS[pb][:, 0:nsub]),
                 reads=[PSB[pb]], writes=[B_rkc])
            for g in range(4):
                ks_t, ks_b, ks_s = R_kst.next()
                for hh in range(4):
                    hd = g * 4 + hh
                    pb = next_acc()
                    mm_group(pb, PS[pb][:, 0:n], [(wuk[:, kc, hd * 128:(hd + 1) * 128], ckvT[:, kc, 0:n]) for kc in range(4)],
                             reads=[B_wuk, B_ckvT])
                    S.op("dve", lambda h, pb=pb, hh=hh, n=n, ks_t=ks_t: h.tensor_tensor(
                        out=ks_t[:, hh, 0:n], in0=PS[pb][:, 0:n], in1=rkb[:, 0:n], op=ALU.mult),
                        reads=[PSB[pb], B_rkb], writes=[ks_b])
                store(kT_d[g * 4:(g + 1) * 4, :, t0:t0 + n].rearrange("h p t -> p h t"), ks_t[:, :, 0:n], reads=[ks_b], sem=ks_s)
                next(gen, None)
            for s_ in range(nsub):
                m = min(128, n - s_ * 128)
                blk = 0 if t0 == 0 else 1 + (t0 - NMETA) // 128 + s_
                vs_t, vs_b, vs_s = R_vst.next()
                for g in range(4):
                    pb = next_acc()
                    mm_group(pb, PS[pb][0:m, :], [(ckvT[:, kc, s_ * 128:s_ * 128 + m], wuv[:, kc, g * 512:(g + 1) * 512]) for kc in range(4)],
                             reads=[B_wuv, B_ckvT])
                    S.op("act", lambda h, pb=pb, g=g, m=m, s_=s_, vs_t=vs_t: h.activation(
                        out=vs_t[0:m, g * 512:(g + 1) * 512], in_=PS[pb][0:m, :], func=AF.Copy, scale=rkc[0:m, s_:s_ + 1]),
                        reads=[PSB[pb], B_rkc], writes=[vs_b])
                store(V_d[:, 0:m, blk, :].rearrange("h p d -> p h d"), vs_t[0:m, :].rearrange("p (h d) -> p h d", h=H), reads=[vs_b], sem=vs_s)
                next(gen, None)
            for _ in gen:
                pass
        S.barrier()
        S.flush()
        free_to(n_global)

        phase[0] += 1
        if phase[0] > stop:
            break
        CKB = 8 if (L - NMETA) % 1024 == 0 else 4
        CKT = CKB * 128
        nchunk = (L - NMETA) // CKT
        qanT = sb("qanT", (128, 4, TQ), BF16)
        rqb = sb("rqb", (128, TQ), F32)
        szT = sb("szT", (128, 16, TQ), BF16)
        gcT = sb("gcT", (128, 16, TQ), BF16)
        mgT = sb("mgT", (128, 16, TQ), BF16)
        tmpB = sb("tmpB", (128, 2, TQ), F32)
        silt = [sb("silt%d" % i, (128, TQ), F32) for i in range(2)]
        wts = [sb("wt%d" % i, (128, 16, WC), BF16) for i in range(NWB)]
        wqs = [sb("wq%d" % i, (128, 4, 192), BF16) for i in range(2)]
        kbs = [sb("kb%d" % i, (128, CKT + NMETA), BF16) for i in range(3)]
        krb = [sb("krb%d" % i, (128, CKT + NMETA), BF16) for i in range(3)]
        vbs = [sb("vb%d" % i, (128, CKB + 1, 132), BF16) for i in range(3)]
        pts = [sb("pt%d" % i, (128, TQ), BF16) for i in range(4)]
        qns = [sb("qn%d" % i, (128, TQ), BF16) for i in range(2)]
        qrs = [sb("qr%d" % i, (128, TQ), BF16) for i in range(2)]
        onb = sb("onb", (128, 4, 128), BF16)
        rq_tab = sb("rq_tab", (64, 2, TQ), F32)
        qt1 = sb("qt1", (64, TQ), F32)
        qt2 = sb("qt2", (64, TQ), F32)
        B_qanT, B_rqb, B_gcT, B_mgT, B_tmpB, B_onb, B_rqtab, B_qt1, B_qt2 = [Buf() for _ in range(9)]
        B_szT = [Buf() for _ in range(16)]
        B_hb = R_xin.b
        B_sil = [Buf(), Buf()]
        B_qr = [Buf(), Buf()]
        B_qn = [Buf(), Buf()]
        B_pt = [Buf() for _ in range(4)]
        nm = job["name"]
        R_wt = Ring(S, "wt" + nm, wts)
        R_wq = Ring(S, "wq" + nm, wqs)
        R_kb = Ring(S, "kb" + nm, kbs)
        R_krb = Ring(S, "krb" + nm, krb)
        R_vb = Ring(S, "vb" + nm, vbs)
        sem_q = S.new_sem("semq" + nm)
        sem_h = [S.new_sem("semh%s%d" % (nm, i)) for i in range(4)]
        for i in range(3):
            S.op("pool", lambda h, i=i: h.memset(krb[i][64:128, :], 0.0), writes=[R_krb.b[i]])
            S.op("pool", lambda h, i=i: h.memset(vbs[i][:, :, 128:129], 1.0), writes=[R_vb.b[i]])
        for i in range(2):
            S.op("pool", lambda h, i=i: h.memset(qrs[i][64:128, :], 0.0), writes=[B_qr[i]])

        wv_in = wb_in.rearrange("(kc p) c -> p kc c", p=128)
        wv_oa = wb_oa.rearrange("(kc p) c -> p kc c", p=128)
        wv_oc = wb_oc.rearrange("(kc p) c -> p kc c", p=128)
        wv_o = wb_o.rearrange("(kc p) c -> p kc c", p=128)
        wv_uq = wb_uq.rearrange("(kc p) c -> p kc c", p=128)

        for qt in range(NQ // TQ):
            r0 = job["xq_row0"] + qt * TQ
            plan = []
            for c0 in range(0, 512, WC):
                plan.append((wv_in, C_QA + c0))
            for c0 in range(0, 2048, WC):
                plan.append((wv_in, C_ZA + c0))
            n_pre = len(plan)
            for c0 in range(0, 2048, WC):
                for base in (C_CX, C_CC, C_CB, C_ZC):
                    plan.append((wv_in, base + c0))
            for c0 in range(0, 2048, WC):
                plan.append((wv_in, C_GA + c0))
                plan.append((wv_oa, c0))
                plan.append((wv_in, C_GC + c0))
                plan.append((wv_oc, c0))
            for c0 in range(0, 2048, WC):
                plan.append((wv_o, c0))
            loaded = {}
            nload = [0]

            def wload_upto(k):
                while nload[0] < min(k, len(plan)):
                    i = nload[0]
                    t, b, s = R_wt.next()
                    src, c0 = plan[i]
                    S.dma("sp", t[:], src[:, :, c0:c0 + WC], s, writes=[b])
                    loaded[i] = (t, b)
                    nload[0] += 1

            def wget(i):
                wload_upto(i + NWB)
                return loaded[i]

            accb = [0]

            def next_acc():
                k = accb[0] % 6
                accb[0] += 1
                return k

            make_xnT(job["xq"], r0, TQ + 2, 0)
            S.dma("sp", rq_tab[:], ropeq_d[job["rq"], :, :, qt * TQ:(qt + 1) * TQ].rearrange("t r n -> r t n"), sem_q, writes=[B_rqtab])
            XO = slice(1, TQ + 1)
            wi = 0
            for c0 in range(0, 512, WC):
                wt, wb_ = wget(wi); wi += 1
                for mi in range(WC // 128):
                    mt = (c0 // 128) + mi
                    pb = next_acc()
                    mm_group(pb, PS[pb][:, :], [(wt[:, kc, mi * 128:(mi + 1) * 128], xnT[:, kc, XO]) for kc in range(16)], reads=[wb_, B_xnT])
                    S.op("dve", lambda h, pb=pb, mt=mt: h.tensor_scalar(out=qanT[:, mt, :], in0=PS[pb][:, :], scalar1=cvec[:, V_QW + mt:V_QW + mt + 1],
                                                                      scalar2=None, op0=ALU.mult), reads=[PSB[pb], B_cvec], writes=[B_qanT])
                    S.op("act", lambda h, pb=pb, mt=mt: h.activation(out=tmpA[:, mt, 0:TQ], in_=PS[pb][:, :], func=AF.Square),
                         reads=[PSB[pb]], writes=[B_tmpA])
            pb = next_acc()
            mm_group(pb, PS[pb][:, :], [(ones_f[:], tmpA[:, mt, 0:TQ]) for mt in range(4)], reads=[B_ones, B_tmpA])
            S.op("act", lambda h, pb=pb: h.activation(out=rqb[:], in_=PS[pb][:, :], func=AF.Sqrt, bias=EPS, scale=1.0 / 512),
                 reads=[PSB[pb]], writes=[B_rqb])
            S.op("dve", lambda h: h.reciprocal(out=rqb[:], in_=rqb[:]), reads=[B_rqb], writes=[B_rqb])
            S.op("dve", lambda h: h.tensor_tensor(out=rq_tab[:, 0, :], in0=rq_tab[:, 0, :], in1=rqb[0:64, :], op=ALU.mult),
                 reads=[B_rqtab, B_rqb], writes=[B_rqtab])
            S.op("dve", lambda h: h.tensor_tensor(out=rq_tab[:, 1, :], in0=rq_tab[:, 1, :], in1=rqb[0:64, :], op=ALU.mult),
                 reads=[B_rqtab, B_rqb], writes=[B_rqtab])
            for c0 in range(0, 2048, WC):
                wt, wb_ = wget(wi); wi += 1
                for mi in range(WC // 128):
                    c = (c0 // 128) + mi
                    pb = next_acc()
                    mm_group(pb, PS[pb][:, :], [(wt[:, kc, mi * 128:(mi + 1) * 128], xnT[:, kc, XO]) for kc in range(16)], reads=[wb_, B_xnT])
                    S.op("act", lambda h, pb=pb, c=c: h.activation(out=szT[:, c, :], in_=PS[pb][:, :], func=AF.Silu),
                         reads=[PSB[pb]], writes=[B_szT[c]])
            assert wi == n_pre
            wload_upto(n_pre + NWB)

            ST = [0, 1, 2]
            OB = [[3, 4], [5, 6]]
            PQN, PQR = 7, 7

            def q_proj(hd):
                wq, wq_b, wq_s = R_wq.next()
                S.dma("sp", wq[:], wv_uq[:, :, hd * 192:(hd + 1) * 192], wq_s, writes=[wq_b])
                i = hd % 2
                mm_group(PQN, PS[PQN][:, :], [(wq[:, kc, 0:128], qanT[:, kc, :]) for kc in range(4)], reads=[wq_b, B_qanT])
                S.op("dve", lambda h, i=i: h.tensor_tensor(out=qns[i][:], in0=PS[PQN][:, :], in1=rqb[:], op=ALU.mult),
                     reads=[PSB[PQN], B_rqb], writes=[B_qn[i]])
                mm_group(PQR, PS[PQR][0:64, :], [(wq[:, kc, 128:192], qanT[:, kc, :]) for kc in range(4)], reads=[wq_b, B_qanT])
                S.op("dve", lambda h: h.tensor_tensor(out=qt1[:], in0=PS[PQR][0:64, :], in1=rq_tab[:, 0, :], op=ALU.mult),
                     reads=[PSB[PQR], B_rqtab], writes=[B_qt1])
                S.op("dve", lambda h: h.tensor_tensor(out=qt2[0:32, :], in0=PS[PQR][32:64, :], in1=rq_tab[32:64, 1, :], op=ALU.mult),
                     reads=[PSB[PQR], B_rqtab], writes=[B_qt2])
                S.op("dve", lambda h: h.tensor_tensor(out=qt2[32:64, :], in0=PS[PQR][0:32, :], in1=rq_tab[0:32, 1, :], op=ALU.mult),
                     reads=[PSB[PQR], B_rqtab], writes=[B_qt2])
                S.op("dve", lambda h, i=i: h.tensor_tensor(out=qrs[i][0:64, :], in0=qt1[:], in1=qt2[:], op=ALU.add),
                     reads=[B_qt1, B_qt2], writes=[B_qr[i]])

            items = []
            for hd in range(H):
                for ck in range(nchunk):
                    nb = CKB + (1 if ck == 0 else 0)
                    for bi in range(nb):
                        items.append((hd, ck, bi))
            chunk_bufs = {}

            def load_chunk(hd, ck):
                kt, kb_, ks_ = R_kb.next()
                krt, krb_, krs_ = R_krb.next()
                vt, vb_, vs_ = R_vb.next()
                tok0 = 0 if ck == 0 else NMETA + ck * CKT
                ntok = CKT + (NMETA if ck == 0 else 0)
                blk0 = 0 if ck == 0 else 1 + ck * CKB
                nb = CKB + (1 if ck == 0 else 0)
                S.dma("sp", kt[:, 0:ntok], kT_d[hd, :, tok0:tok0 + ntok], ks_, writes=[kb_])
                S.dma("sp", krt[0:64, 0:ntok], krT_d[:, tok0:tok0 + ntok], krs_, writes=[krb_])
                S.dma("sp", vt[:, 0:nb, 0:128], V_d[hd, :, blk0:blk0 + nb, :], vs_, writes=[vb_])
                chunk_bufs[(hd, ck)] = (kt, kb_, krt, krb_, vt, vb_)

            def item_geom(it):
                hd, ck, bi = it
                if ck == 0:
                    nk = NMETA if bi == 0 else 128
                    col = 0 if bi == 0 else NMETA + (bi - 1) * 128
                else:
                    nk = 128
                    col = bi * 128
                return nk, col

            def qk(k):
                hd, ck, bi = items[k]
                if bi == 0:
                    if (hd, ck) not in chunk_bufs:
                        load_chunk(hd, ck)
                    nxt = (hd, ck + 1) if ck + 1 < nchunk else ((hd + 1, 0) if hd + 1 < H else None)
                    if nxt is not None and nxt not in chunk_bufs:
                        load_chunk(*nxt)
                kt, kb_, krt, krb_, vt, vb_ = chunk_bufs[(hd, ck)]
                nk, col = item_geom(items[k])
                st = ST[k % 3]
                i = hd % 2
                S.op("pe", lambda h, st=st, nk=nk, col=col, kt=kt, i=i: h.matmul(
                    PS[st][0:nk, :], lhsT=kt[:, col:col + nk], rhs=qns[i][:], start=True, stop=False),
                    reads=[kb_, B_qn[i]], writes=[PSB[st]], signal=False)
                S.op("pe", lambda h, st=st, nk=nk, col=col, krt=krt, i=i: h.matmul(
                    PS[st][0:nk, :], lhsT=krt[:, col:col + nk], rhs=qrs[i][:], start=False, stop=True),
                    reads=[krb_, B_qr[i]], writes=[PSB[st]])
                j = k % 4
                S.op("act", lambda h, st=st, nk=nk, j=j: h.activation(out=pts[j][0:nk, :], in_=PS[st][0:nk, :], func=AF.Exp, scale=SCALE),
                     reads=[PSB[st]], writes=[B_pt[j]])

            def pv(k):
                hd, ck, bi = items[k]
                kt, kb_, krt, krb_, vt, vb_ = chunk_bufs[(hd, ck)]
                nk, col = item_geom(items[k])
                j = k % 4
                ob = OB[hd % 2]
                first = (ck == 0 and bi == 0)
                last = (ck == nchunk - 1 and bi == CKB + (1 if ck == 0 else 0) - 1)
                for qs in range(4):
                    bank = ob[qs // 2]
                    co = (qs % 2) * 256
                    S.op("pe", lambda h, bank=bank, co=co, nk=nk, j=j, qs=qs, vt=vt, bi=bi, first=first, last=last: h.matmul(
                        PS[bank][:, co:co + 129], lhsT=pts[j][0:nk, qs * 128:(qs + 1) * 128], rhs=vt[0:nk, bi, 0:129],
                        start=(first and qs % 2 == 0), stop=last, skip_group_check=True),
                        reads=[B_pt[j], vb_], writes=[PSB[bank]], signal=(qs == 3))
                if last and ck == nchunk - 1:
                    del chunk_bufs[(hd, ck)]

            def head_norm(hd):
                ob = OB[hd % 2]
                for qs in range(4):
                    bank = ob[qs // 2]
                    co = (qs % 2) * 256
                    rc, rc_b = stat_slot()
                    S.op("dve", lambda h, bank=bank, co=co, rc=rc: h.reciprocal(out=rc, in_=PS[bank][:, co + 128:co + 129]),
                         reads=[PSB[bank]], writes=[rc_b])
                    S.op("dve", lambda h, bank=bank, co=co, rc=rc, qs=qs: h.tensor_scalar(
                        out=onb[:, qs, :], in0=PS[bank][:, co:co + 128], scalar1=rc, scalar2=None, op0=ALU.mult),
                        reads=[PSB[bank], rc_b], writes=[B_onb])

            def head_out(hd):
                pv_ = PS[PQN].bitcast(BF16)
                for qs in range(4):
                    S.op("pe", lambda h, qs=qs, pv_=pv_: h.transpose(out=pv_[:, qs * 128:(qs + 1) * 128], in_=onb[:, qs, :], identity=ident[:]),
                         reads=[B_onb, B_ident], writes=[PSB[PQN]], signal=(qs == 3))
                S.op("dve", lambda h, hd=hd, pv_=pv_: h.tensor_tensor(out=szT[:, hd, :], in0=pv_[:, 0:TQ], in1=szT[:, hd, :], op=ALU.mult),
                     reads=[PSB[PQN], B_szT[hd]], writes=[B_szT[hd]])

            deferred = {}
            q_proj(0)
            qk(0)
            qk(1)
            nit = len(items)
            per_head = nit // H
            for k in range(nit):
                hd, ck, bi = items[k]
                if k + 2 < nit:
                    qk(k + 2)
                pv(k)
                kin = k % per_head
                if kin == per_head - 1:
                    head_norm(hd)
                    deferred[k + 2] = hd
                if kin == min(2, per_head - 2) and hd + 1 < H:
                    q_proj(hd + 1)
                if k in deferred:
                    head_out(deferred.pop(k))
            for k in sorted(deferred):
                head_out(deferred.pop(k))

            NM = WC // 128
            for c0 in range(0, 2048, WC):
                wt, wb_ = wget(wi); wi += 1
                for mi in range(NM):
                    pb = next_acc()
                    pb2 = next_acc()
                    for kc in range(16):
                        S.op("pe", lambda h, pb=pb, wt=wt, mi=mi, kc=kc: h.matmul(
                            PS[pb][:, :], lhsT=wt[:, kc, mi * 128:(mi + 1) * 128], rhs=xnT[:, kc, 0:TQ], start=(kc == 0), stop=(kc == 15)),
                            reads=[wb_, B_xnT], writes=[PSB[pb]], signal=False)
                        S.op("pe", lambda h, pb2=pb2, wt=wt, mi=mi, kc=kc: h.matmul(
                            PS[pb2][:, 0:2], lhsT=wt[:, kc, mi * 128:(mi + 1) * 128], rhs=xnT[:, kc, TQ:TQ + 2], start=(kc == 0), stop=(kc == 15)),
                            reads=[wb_, B_xnT], writes=[PSB[pb2]], signal=(kc == 15))
                    S.op("act", lambda h, pb=pb, mi=mi: h.activation(out=tmpA[:, mi, 0:TQ], in_=PS[pb][:, :], func=AF.Copy),
                         reads=[PSB[pb]], writes=[B_tmpA])
                    S.op("act", lambda h, pb2=pb2, mi=mi: h.activation(out=tmpA[:, mi, TQ:TQ + 2], in_=PS[pb2][:, 0:2], func=AF.Copy),
                         reads=[PSB[pb2]], writes=[B_tmpA])
                wt, wb_ = wget(wi); wi += 1
                for mi in range(NM):
                    c = c0 // 128 + mi
                    pb = next_acc()
                    pb2 = next_acc()
                    for kc in range(16):
                        S.op("pe", lambda h, pb=pb, wt=wt, mi=mi, kc=kc: h.matmul(
                            PS[pb][:, :], lhsT=wt[:, kc, mi * 128:(mi + 1) * 128], rhs=xnT[:, kc, 0:TQ], start=(kc == 0), stop=(kc == 15)),
                            reads=[wb_, B_xnT], writes=[PSB[pb]], signal=False)
                        S.op("pe", lambda h, pb2=pb2, wt=wt, mi=mi, kc=kc: h.matmul(
                            PS[pb2][:, 0:2], lhsT=wt[:, kc, mi * 128:(mi + 1) * 128], rhs=xnT[:, kc, TQ:TQ + 2], start=(kc == 0), stop=(kc == 15)),
                            reads=[wb_, B_xnT], writes=[PSB[pb2]], signal=(kc == 15))
                    S.op("dve", lambda h, pb=pb, mi=mi: h.tensor_tensor(out=tmpA[:, mi, 0:TQ], in0=PS[pb][:, :], in1=tmpA[:, mi, 0:TQ], op=ALU.mult),
                         reads=[PSB[pb], B_tmpA], writes=[B_tmpA])
                    S.op("dve", lambda h, pb2=pb2, mi=mi: h.tensor_tensor(out=tmpA[:, mi, TQ:TQ + 2], in0=PS[pb2][:, 0:2], in1=tmpA[:, mi, TQ:TQ + 2], op=ALU.mult),
                         reads=[PSB[pb2], B_tmpA], writes=[B_tmpA])
                    cw = lambda k_, c=c: cvec[:, V_CW + c * 3 + k_:V_CW + c * 3 + k_ + 1]
                    S.op("dve", lambda h, mi=mi, cw=cw: h.tensor_scalar(out=tmpB[:, mi, :], in0=tmpA[:, mi, 0:TQ], scalar1=cw(0), scalar2=None, op0=ALU.mult),
                         reads=[B_tmpA, B_cvec], writes=[B_tmpB])
                    S.op("dve", lambda h, mi=mi, cw=cw: h.scalar_tensor_tensor(out=tmpB[:, mi, :], in0=tmpA[:, mi, 1:TQ + 1], scalar=cw(1), in1=tmpB[:, mi, :],
                                                                               op0=ALU.mult, op1=ALU.add), reads=[B_tmpA, B_tmpB, B_cvec], writes=[B_tmpB])
                    S.op("dve", lambda h, mi=mi, cw=cw: h.scalar_tensor_tensor(out=tmpB[:, mi, :], in0=tmpA[:, mi, 2:TQ + 2], scalar=cw(2), in1=tmpB[:, mi, :],
                                                                               op0=ALU.mult, op1=ALU.add), reads=[B_tmpA, B_tmpB, B_cvec], writes=[B_tmpB])
                wt, wb_ = wget(wi); wi += 1
                for mi in range(NM):
                    pb = next_acc()
                    mm_group(pb, PS[pb][:, :], [(wt[:, kc, mi * 128:(mi + 1) * 128], xnT[:, kc, XO]) for kc in range(16)], reads=[wb_, B_xnT])
                    S.op("dve", lambda h, pb=pb, mi=mi: h.tensor_tensor(out=tmpB[:, mi, :], in0=PS[pb][:, :], in1=tmpB[:, mi, :], op=ALU.mult),
                         reads=[PSB[pb], B_tmpB], writes=[B_tmpB])
                wt, wb_ = wget(wi); wi += 1
                for mi in range(NM):
                    c = c0 // 128 + mi
                    pb = next_acc()
                    mm_group(pb, PS[pb][:, :], [(wt[:, kc, mi * 128:(mi + 1) * 128], xnT[:, kc, XO]) for kc in range(16)], reads=[wb_, B_xnT])
                    si = c % 2
                    S.op("act", lambda h, pb=pb, si=si: h.activation(out=silt[si][:], in_=PS[pb][:, :], func=AF.Silu),
                         reads=[PSB[pb]], writes=[B_sil[si]])
                    S.op("dve", lambda h, si=si, mi=mi, c=c: h.tensor_tensor(out=gcT[:, c, :], in0=tmpB[:, mi, :], in1=silt[si][:], op=ALU.mult),
                         reads=[B_tmpB, B_sil[si]], writes=[B_gcT])
            for c0 in range(0, 2048, WC):
                wt, wb_ = wget(wi); wi += 1
                for mi in range(NM):
                    c = c0 // 128 + mi
                    pb = next_acc()
                    mm_group(pb, PS[pb][:, :], [(wt[:, kc, mi * 128:(mi + 1) * 128], xnT[:, kc, XO]) for kc in range(16)], reads=[wb_, B_xnT])
                    S.op("act", lambda h, pb=pb, mi=mi, c=c: h.activation(out=tmpA[:, mi, 0:TQ], in_=PS[pb][:, :], func=AF.Sigmoid,
                                                                           bias=cvec[:, V_BG + c:V_BG + c + 1], scale=1.0),
                         reads=[PSB[pb], B_cvec], writes=[B_tmpA])
                wt, wb_ = wget(wi); wi += 1
                for mi in range(NM):
                    pb = next_acc()
                    mm_group(pb, PS[pb][:, :], [(wt[:, kc, mi * 128:(mi + 1) * 128], szT[:, kc, :]) for kc in range(16)], reads=[wb_] + B_szT)
                    S.op("dve", lambda h, pb=pb, mi=mi: h.tensor_tensor(out=tmpA[:, mi, 0:TQ], in0=PS[pb][:, :], in1=tmpA[:, mi, 0:TQ], op=ALU.mult),
                         reads=[PSB[pb], B_tmpA], writes=[B_tmpA])
                wt, wb_ = wget(wi); wi += 1
                for mi in range(NM):
                    c = c0 // 128 + mi
                    pb = next_acc()
                    mm_group(pb, PS[pb][:, :], [(wt[:, kc, mi * 128:(mi + 1) * 128], xnT[:, kc, XO]) for kc in range(16)], reads=[wb_, B_xnT])
                    S.op("act", lambda h, pb=pb, mi=mi, c=c: h.activation(out=tmpB[:, mi, :], in_=PS[pb][:, :], func=AF.Sigmoid,
                                                                           bias=cvec[:, V_BG + 16 + c:V_BG + 16 + c + 1], scale=1.0),
                         reads=[PSB[pb], B_cvec], writes=[B_tmpB])
                wt, wb_ = wget(wi); wi += 1
                for mi in range(NM):
                    c = c0 // 128 + mi
                    pb = next_acc()
                    mm_group(pb, PS[pb][:, :], [(wt[:, kc, mi * 128:(mi + 1) * 128], gcT[:, kc, :]) for kc in range(16)], reads=[wb_, B_gcT])
                    S.op("dve", lambda h, pb=pb, mi=mi: h.tensor_tensor(out=tmpB[:, mi, :], in0=PS[pb][:, :], in1=tmpB[:, mi, :], op=ALU.mult),
                         reads=[PSB[pb], B_tmpB], writes=[B_tmpB])
                    S.op("dve", lambda h, mi=mi, c=c: h.tensor_tensor(out=mgT[:, c, :], in0=tmpA[:, mi, 0:TQ], in1=tmpB[:, mi, :], op=ALU.add),
                         reads=[B_tmpA, B_tmpB], writes=[B_mgT])
            for s_ in range(4):
                S.dma("sp", hbuf[:, s_, :], job["xq"][r0 + 1 + s_ * 128:r0 + 1 + (s_ + 1) * 128, :], R_xin.s[s_], writes=[B_hb[s_]])
            for c0 in range(0, 2048, WC):
                wt, wb_ = wget(wi); wi += 1
                for s_ in range(4):
                    pb = next_acc()
                    mm_group(pb, PS[pb][:, 0:WC], [(mgT[:, kc, s_ * 128:(s_ + 1) * 128], wt[:, kc, :]) for kc in range(16)], reads=[wb_, B_mgT])
                    S.op("dve", lambda h, pb=pb, s_=s_, c0=c0: h.tensor_tensor(out=hbuf[:, s_, c0:c0 + WC], in0=PS[pb][:, 0:WC], in1=hbuf[:, s_, c0:c0 + WC], op=ALU.add),
                         reads=[PSB[pb], B_hb[s_]], writes=[B_hb[s_]])
            assert wi == len(plan)
            for s_ in range(4):
                xs_t, xs_b, _ = R_xsb.next()
                ss, ss_b = stat_slot()
                S.op("act", lambda h, s_=s_, xs_t=xs_t, ss=ss: h.activation(out=xs_t[:], in_=hbuf[:, s_, :], func=AF.Square, scale=float(D) ** -0.5, accum_out=ss),
                     reads=[B_hb[s_]], writes=[xs_b, ss_b])
                S.op("act", lambda h, ss=ss: h.activation(out=ss, in_=ss, func=AF.Sqrt, bias=EPS, scale=1.0), reads=[ss_b], writes=[ss_b])
                S.op("dve", lambda h, ss=ss: h.reciprocal(out=ss, in_=ss), reads=[ss_b], writes=[ss_b])
                S.op("dve", lambda h, s_=s_, ss=ss: h.scalar_tensor_tensor(out=hbuf[:, s_, :], in0=hbuf[:, s_, :], scalar=ss, in1=fnw[:], op0=ALU.mult, op1=ALU.mult),
                     reads=[B_hb[s_], ss_b, B_fnw], writes=[B_hb[s_]])
                S.dma("pool", job["y"][qt * TQ + s_ * 128:qt * TQ + (s_ + 1) * 128, :], hbuf[:, s_, :], sem_h[s_], reads=[B_hb[s_]])
        S.barrier()
        S.flush()
        free_to(n_global)

    free_to(0)
    for cm in reversed(psum_cms):
        cm.__exit__(None, None, None)
    S.close()
    return nc


def _rope_tables(npos):
    inv_freq = (np.float32(1.0) / (np.float32(10000.0) ** (np.arange(0, 64, 2, dtype=np.float32) / np.float32(64)))).astype(np.float32)
    ang = (np.arange(npos, dtype=np.float32)[:, None] * inv_freq[None, :]).astype(np.float32)
    cos = np.cos(ang).astype(np.float32).T
    sin = np.sin(ang).astype(np.float32).T
    tab = np.empty((2, 64, npos), np.float32)
    tab[0, 0:32] = cos
    tab[0, 32:64] = cos
    tab[1, 0:32] = sin
    tab[1, 32:64] = -sin
    return tab


_PROG_CACHE = {}


def kernel(x_prompt, x_sample, meta_tokens, norm_w, w_in, b_gate, q_a_norm_w, w_uq, kv_a_norm_w, w_ukv,
           w_o_attn, conv_w, w_o_conv, w_o, final_norm_w):
    f = lambda a: np.ascontiguousarray(np.asarray(a, dtype=np.float32))
    x_prompt, x_sample, meta = f(x_prompt), f(x_sample), f(meta_tokens)
    B, SP, _ = x_prompt.shape
    NS, SS, _ = x_sample.shape
    assert NS == 8 and B == 2 and SP == 4 * SS
    NQ = SS
    LS, LP = NMETA + SS, NMETA + SP
    key = (LS, LP, NQ)
    if key not in _PROG_CACHE:
        import os
        _PROG_CACHE[key] = build_program(LS, LP, NQ, stop=int(os.environ.get("K_STOP", "99")))
    nc = _PROG_CACHE[key]
    cvec = np.zeros((128, V_N), np.float32)
    cvec[:, V_NW:V_NW + 16] = f(norm_w)[0].reshape(16, 128).T
    cvec[:, V_QW:V_QW + 4] = f(q_a_norm_w)[0].reshape(4, 128).T
    cvec[:, V_KW:V_KW + 4] = f(kv_a_norm_w)[0].reshape(4, 128).T
    cw = f(conv_w)[0]
    cvec[:, V_CW:V_CW + 48] = cw.reshape(3, 16, 128).transpose(2, 1, 0).reshape(128, 48)
    cvec[:, V_BG:V_BG + 32] = f(b_gate)[0].reshape(32, 128).T
    fnw = np.ascontiguousarray(np.broadcast_to(f(final_norm_w)[None, :], (128, D)))
    ropek = _rope_tables(LP)
    zrow = np.zeros((1, D), np.float32)
    xps = [np.concatenate([meta, x_prompt[b], zrow], axis=0) for b in range(B)]
    shared = dict(w_in=f(w_in)[0], w_uq=f(w_uq)[0], w_ukv=f(w_ukv)[0], w_oa=f(w_o_attn)[0], w_oc=f(w_o_conv)[0],
                  w_o=f(w_o)[0], cvec=cvec, fnw=fnw, ropek=ropek)
    in_maps = []
    for c in range(8):
        b, j = c // 4, c % 4
        xs = np.concatenate([meta, x_sample[c], zrow], axis=0)
        r0 = NMETA - 1 + NQ * j
        xqp = np.ascontiguousarray(xps[b][r0:r0 + NQ + 2])
        ropeq = np.stack([ropek[:, :, NMETA:NMETA + NQ], ropek[:, :, NMETA + NQ * j:NMETA + NQ * (j + 1)]], axis=0)
        m = dict(shared)
        m.update(xs=xs, xp=xps[b], xqp=xqp, ropeq=np.ascontiguousarray(ropeq))
        in_maps.append(m)
    res = run_bass_kernel_spmd(nc, in_maps, core_ids=list(range(8)))
    y_prompt = np.empty((B, SP, D), np.float32)
    y_sample = np.empty((NS, SS, D), np.float32)
    for c in range(8):
        b, j = c // 4, c % 4
        y_sample[c] = res.results[c]["ys"]
        y_prompt[b, NQ * j:NQ * (j + 1)] = res.results[c]["yp"]
    return (y_prompt, y_sample)
```

```python
import numpy as np
import concourse.bass as bass
import concourse.mybir as mybir
from concourse.bass_utils import run_bass_kernel_spmd

F32 = mybir.dt.float32
BF16 = mybir.dt.bfloat16
AF = mybir.ActivationFunctionType
ALU = mybir.AluOpType

D = 2048
H = 16
NMETA = 16
INW = 15424
C_QA, C_CKV, C_KR, C_ZA, C_CX, C_CB, C_CC, C_ZC, C_GA, C_GC = 0, 512, 1024, 1088, 3136, 5184, 7232, 9280, 11328, 13376
EPS = 1e-6
SCALE = 192.0 ** -0.5
WC = 256
NWB = 3
TQ = 512
V_NW, V_QW, V_KW, V_CW, V_BG, V_N = 0, 16, 20, 24, 72, 104


class Buf:
    __slots__ = ("w", "r", "excl")

    def __init__(self, excl=False):
        self.w = None
        self.r = []
        self.excl = excl


class Sched:
    ENG = ("pe", "act", "dve", "pool", "sp")

    def __init__(self, nc):
        self.nc = nc
        self.ops = {e: [] for e in self.ENG}
        self.cnt = {}
        self.waited = {e: {} for e in self.ENG}
        self.sems = {}
        self._ctx = []
        self.pend = {e: ([], []) for e in self.ENG}
        for e in self.ENG:
            self.new_sem("p_" + e)

    def new_sem(self, name):
        cm = self.nc.semaphore(name)
        self.sems[name] = cm.__enter__()
        self._ctx.append(cm)
        self.cnt[name] = 0
        return name

    def _wait(self, eng, tok):
        if tok is None:
            return
        name, val = tok
        if self.waited[eng].get(name, 0) >= val:
            return
        self.waited[eng][name] = val
        sem = self.sems[name]
        self.ops[eng].append(lambda h, sem=sem, val=val: h.wait_ge(sem, val))

    def _hazards(self, eng, reads, writes):
        own = "p_" + eng
        for b in reads:
            self._wait(eng, b.w)
            if b.excl:
                for t in b.r:
                    if t[0] != own:
                        self._wait(eng, t)
        for b in writes:
            if b.w is not None and b.w[0] != own:
                self._wait(eng, b.w)
            for t in b.r:
                if t[0] != own:
                    self._wait(eng, t)

    def _assign(self, tok, reads, writes):
        for b in reads:
            b.r.append(tok)
        for b in writes:
            b.w = tok
            b.r = []

    def op(self, eng, fn, reads=(), writes=(), signal=True):
        self._hazards(eng, reads, writes)
        name = "p_" + eng
        pr, pw = self.pend[eng]
        if signal:
            self.cnt[name] += 1
            tok = (name, self.cnt[name])
            sem = self.sems[name]
            self.ops[eng].append(lambda h, fn=fn, sem=sem: fn(h).then_inc(sem, 1))
            self._assign(tok, list(reads) + pr, list(writes) + pw)
            pr.clear()
            pw.clear()
            return tok
        self.ops[eng].append(fn)
        pr.extend(reads)
        pw.extend(writes)
        return None

    def dma(self, eng, out, in_, sem, reads=(), writes=()):
        self._hazards(eng, reads, writes)
        self.cnt[sem] += 16
        tok = (sem, self.cnt[sem])
        s = self.sems[sem]
        self.ops[eng].append(lambda h, out=out, in_=in_, s=s: h.dma_start(out=out, in_=in_).then_inc(s, 16))
        self._assign(tok, reads, writes)
        return tok

    def barrier(self):
        for e in self.ENG:
            assert not self.pend[e][0] and not self.pend[e][1], e
        for e in self.ENG:
            for name, c in self.cnt.items():
                if c > 0 and name != "p_" + e:
                    self._wait(e, (name, c))

    def flush(self):
        S = self
        with self.nc.Block() as block:
            @block.tensor
            def _(h):
                for f in S.ops["pe"]:
                    f(h)

            @block.scalar
            def _(h):
                for f in S.ops["act"]:
                    f(h)

            @block.vector
            def _(h):
                for f in S.ops["dve"]:
                    f(h)

            @block.gpsimd
            def _(h):
                for f in S.ops["pool"]:
                    f(h)

            @block.sync
            def _(h):
                for f in S.ops["sp"]:
                    f(h)
        self.ops = {e: [] for e in self.ENG}

    def close(self):
        for cm in reversed(self._ctx):
            cm.__exit__(None, None, None)


class Ring:
    def __init__(self, S, name, tensors):
        self.t = tensors
        self.b = [Buf() for _ in tensors]
        self.s = [S.new_sem("%s%d" % (name, i)) for i in range(len(tensors))]
        self.i = 0

    def next(self):
        k = self.i % len(self.t)
        self.i += 1
        return self.t[k], self.b[k], self.s[k]


def build_program(LS, LP, NQ, stop=99):
    assert (LS - NMETA) % 512 == 0 and (LP - NMETA) % 512 == 0 and NQ % TQ == 0
    nc = bass.Bass("TRN2", target_bir_lowering=False)
    dram = lambda n, shp, dt, kind: nc.dram_tensor(n, list(shp), dt, kind=kind).ap()
    xs_d = dram("xs", (LS + 1, D), F32, "ExternalInput")
    xp_d = dram("xp", (LP + 1, D), F32, "ExternalInput")
    xqp_d = dram("xqp", (NQ + 2, D), F32, "ExternalInput")
    ropek_d = dram("ropek", (2, 64, LP), F32, "ExternalInput")
    ropeq_d = dram("ropeq", (2, 2, 64, NQ), F32, "ExternalInput")
    w_in_d = dram("w_in", (D, INW), F32, "ExternalInput")
    w_uq_d = dram("w_uq", (512, 3072), F32, "ExternalInput")
    w_ukv_d = dram("w_ukv", (512, 4096), F32, "ExternalInput")
    w_oa_d = dram("w_oa", (D, D), F32, "ExternalInput")
    w_oc_d = dram("w_oc", (D, D), F32, "ExternalInput")
    w_o_d = dram("w_o", (D, D), F32, "ExternalInput")
    cvec_d = dram("cvec", (128, V_N), F32, "ExternalInput")
    fnw_d = dram("fnw", (128, D), F32, "ExternalInput")
    ys_d = dram("ys", (NQ, D), F32, "ExternalOutput")
    yp_d = dram("yp", (NQ, D), F32, "ExternalOutput")
    wb_in = dram("wb_in", (D, INW), BF16, "Internal")
    wb_uq = dram("wb_uq", (512, 3072), BF16, "Internal")
    wb_uk = dram("wb_uk", (512, 2048), BF16, "Internal")
    wb_uv = dram("wb_uv", (512, 2048), BF16, "Internal")
    wb_oa = dram("wb_oa", (D, D), BF16, "Internal")
    wb_oc = dram("wb_oc", (D, D), BF16, "Internal")
    wb_o = dram("wb_o", (D, D), BF16, "Internal")
    NBLK = 1 + (LP - NMETA) // 128
    kT_d = dram("kT", (H, 128, LP), BF16, "Internal")
    krT_d = dram("krT", (64, LP), BF16, "Internal")
    V_d = dram("Vs", (H, 128, NBLK, 128), BF16, "Internal")

    S = Sched(nc)
    cms = []

    uid = [0]

    def sb(name, shape, dt):
        uid[0] += 1
        cm = nc.sbuf_tensor("sb%d_%s" % (uid[0], name), list(shape), dt)
        t = cm.__enter__()
        cms.append(cm)
        return t

    def free_to(n):
        while len(cms) > n:
            cms.pop().__exit__(None, None, None)

    psum_cms = [nc.psum_tensor("ps%d" % i, [128, 512], F32) for i in range(8)]
    PS = [cm.__enter__() for cm in psum_cms]
    PSB = [Buf(excl=True) for _ in range(8)]
    ident = sb("ident", (128, 128), BF16)
    ones_f = sb("ones_f", (128, 128), F32)
    cvec = sb("cvec", (128, V_N), F32)
    fnw = sb("fnw", (128, D), F32)
    xnT = sb("xnT", (128, 16, TQ + 2), BF16)
    hbuf = sb("hbuf", (128, 4, D), F32)
    xin = [hbuf[:, i, :] for i in range(4)]
    xsb = [sb("xsb%d" % i, (128, D), BF16) for i in range(2)]
    tmpA = sb("tmpA", (128, 4, TQ + 2), F32)
    stat = sb("stat", (128, 16), F32)
    B_ident, B_ones, B_cvec, B_fnw, B_xnT, B_tmpA = Buf(), Buf(), Buf(), Buf(), Buf(), Buf()
    R_xin = Ring(S, "xin", xin)
    R_xsb = Ring(S, "xsb", xsb)
    B_stat = [Buf() for _ in range(16)]
    stat_i = [0]
    csem_i = [0]

    def csem():
        csem_i[0] += 1
        return S.new_sem("cst%d" % csem_i[0])
    sem_st = [S.new_sem("st%d" % i) for i in range(4)]
    st_i = [0]
    n_global = len(cms)

    def store(out, in_, reads, sem):
        return S.dma("pool", out, in_, sem, reads=reads)

    def stat_slot():
        k = stat_i[0] % 16
        stat_i[0] += 1
        return stat[:, k:k + 1], B_stat[k]

    S.dma("sp", cvec[:], cvec_d, csem(), writes=[B_cvec])
    S.dma("sp", fnw[:], fnw_d, csem(), writes=[B_fnw])
    S.op("pool", lambda h: h.memset(ident[:], 0.0), writes=[B_ident])
    S.op("pool", lambda h: h.affine_select(out=ident[:], in_=ident[:], compare_op=ALU.not_equal, fill=1.0,
                                           base=0, pattern=[[-1, 128]], channel_multiplier=1),
         reads=[B_ident], writes=[B_ident])
    S.op("pool", lambda h: h.memset(ones_f[:], 1.0), writes=[B_ones])
    sem_w = S.new_sem("wcast")
    sem_wk = S.new_sem("wcastk")
    B_wbkv = Buf()
    ukv4 = w_ukv_d.rearrange("k (h t d) -> k h t d", h=H, t=2)
    S.dma("pool", wb_in[:, C_CKV:C_CKV + 576], w_in_d[:, C_CKV:C_CKV + 576], sem_wk, writes=[B_wbkv])
    S.dma("pool", wb_uk.rearrange("k (h d) -> k h d", h=H), ukv4[:, :, 0, :], sem_wk, writes=[B_wbkv])
    S.dma("pool", wb_uv.rearrange("k (h d) -> k h d", h=H), ukv4[:, :, 1, :], sem_wk, writes=[B_wbkv])
    for r in range(0, D, 128):
        S.dma("pool", wb_in[r:r + 128, 0:C_CKV], w_in_d[r:r + 128, 0:C_CKV], sem_w)
        S.dma("pool", wb_in[r:r + 128, C_ZA:INW], w_in_d[r:r + 128, C_ZA:INW], sem_w)
    for r in range(0, D, 512):
        S.dma("pool", wb_oa[r:r + 512, :], w_oa_d[r:r + 512, :], sem_w)
        S.dma("pool", wb_oc[r:r + 512, :], w_oc_d[r:r + 512, :], sem_w)
        S.dma("pool", wb_o[r:r + 512, :], w_o_d[r:r + 512, :], sem_w)
    S.dma("pool", wb_uq, w_uq_d, sem_w)

    def make_xnT_gen(x_ap, row0, n, col0, dst=None, dstB=None, srcs=None):
        dst = xnT if dst is None else dst
        dstB = B_xnT if dstB is None else dstB

        def stage_b(xs_t, xs_b, off, m):
            for half in range(2):
                pb = 6 + half
                pv = PS[pb].bitcast(BF16)
                for j in range(8):
                    kc = half * 8 + j
                    S.op("pe", lambda h, pv=pv, xs_t=xs_t, j=j, kc=kc, m=m: h.transpose(
                        out=pv[:, j * 128:j * 128 + m], in_=xs_t[0:m, kc * 128:(kc + 1) * 128], identity=ident[0:m, 0:m]),
                        reads=[xs_b, B_ident], writes=[PSB[pb]], signal=(j == 7))
                for j in range(8):
                    kc = half * 8 + j
                    d_ = dst[:, kc, col0 + off:col0 + off + m]
                    src = pv[:, j * 128:j * 128 + m]
                    nw = cvec[:, V_NW + kc:V_NW + kc + 1]
                    if half == 0:
                        S.op("dve", lambda h, d_=d_, src=src, nw=nw: h.tensor_scalar(
                            out=d_, in0=src, scalar1=nw, scalar2=None, op0=ALU.mult),
                            reads=[PSB[pb], B_cvec], writes=[dstB])
                    else:
                        S.op("act", lambda h, d_=d_, src=src, nw=nw: h.activation(
                            out=d_, in_=src, func=AF.Copy, scale=nw),
                            reads=[PSB[pb], B_cvec], writes=[dstB])

        off = 0
        prev = None
        while off < n:
            m = min(128, n - off)
            xt, xb_, xsem = R_xin.next()
            if srcs is None:
                S.dma("sp", xt[0:m, :], x_ap[row0 + off:row0 + off + m, :], xsem, writes=[xb_])
            else:
                assert n <= 128
                for (d0, dn, sap) in srcs:
                    S.dma("sp", xt[d0:d0 + dn, :], sap, xsem, writes=[xb_])
            xs_t, xs_b, _ = R_xsb.next()
            ss, ss_b = stat_slot()
            S.op("act", lambda h, xt=xt, xs_t=xs_t, ss=ss, m=m: h.activation(
                out=xs_t[0:m, :], in_=xt[0:m, :], func=AF.Square, scale=float(D) ** -0.5, accum_out=ss[0:m, :]),
                reads=[xb_], writes=[xs_b, ss_b])
            S.op("act", lambda h, ss=ss, m=m: h.activation(out=ss[0:m, :], in_=ss[0:m, :], func=AF.Sqrt, bias=EPS, scale=1.0),
                 reads=[ss_b], writes=[ss_b])
            S.op("dve", lambda h, ss=ss, m=m: h.reciprocal(out=ss[0:m, :], in_=ss[0:m, :]), reads=[ss_b], writes=[ss_b])
            S.op("act", lambda h, xt=xt, xs_t=xs_t, ss=ss, m=m: h.activation(
                out=xs_t[0:m, :], in_=xt[0:m, :], func=AF.Copy, scale=ss[0:m, :]),
                reads=[xb_, ss_b], writes=[xs_b])
            if prev is not None:
                stage_b(*prev)
            prev = (xs_t, xs_b, off, m)
            off += m
            yield
        stage_b(*prev)
        yield

    def make_xnT(x_ap, row0, n, col0):
        for _ in make_xnT_gen(x_ap, row0, n, col0):
            pass

    def mm_group(pb, out_ap, pairs, reads):
        n = len(pairs)
        tok = None
        for i, (l, r) in enumerate(pairs):
            tok = S.op("pe", lambda h, l=l, r=r, i=i: h.matmul(out_ap, lhsT=l, rhs=r, start=(i == 0), stop=(i == n - 1)),
                       reads=reads, writes=[PSB[pb]], signal=(i == n - 1))
        return tok

    jobs = [
        dict(name="s", xk=xs_d, L=LS, xq=xs_d, xq_row0=NMETA - 1, rq=0, y=ys_d),
        dict(name="p", xk=xp_d, L=LP, xq=xqp_d, xq_row0=0, rq=1, y=yp_d),
    ]
    phase = [0]
    for job in jobs:
        L = job["L"]
        phase[0] += 1
        if phase[0] > stop:
            break
        wkv = sb("wkv", (128, 16, 576), BF16)
        wuk = sb("wuk", (128, 4, 2048), BF16)
        wuv = sb("wuv", (128, 4, 2048), BF16)
        ckvT = sb("ckvT", (128, 4, 512), BF16)
        rkb = sb("rkb", (128, 512), F32)
        rkc = sb("rkc", (128, 4), F32)
        kst = [sb("kst%d" % i, (128, 4, 512), BF16) for i in range(2)]
        vst = [sb("vst%d" % i, (128, 2048), BF16) for i in range(2)]
        krs = [sb("krs%d" % i, (64, 512), BF16) for i in range(2)]
        rtab = [sb("rtab%d" % i, (64, 2, 512), F32) for i in range(2)]
        rt1 = sb("rt1", (64, 512), F32)
        rt2 = sb("rt2", (64, 512), F32)
        B_wkv, B_wuk, B_wuv, B_ckvT, B_rkb, B_rkc, B_rt1, B_rt2 = [Buf() for _ in range(8)]
        R_kst = Ring(S, "kst" + job["name"], kst)
        R_vst = Ring(S, "vst" + job["name"], vst)
        R_krs = Ring(S, "krs" + job["name"], krs)
        R_rtab = Ring(S, "rtab" + job["name"], rtab)
        S.dma("sp", wkv[:], wb_in.rearrange("(kc p) c -> p kc c", p=128)[:, :, C_CKV:C_CKV + 576], csem(), reads=[B_wbkv], writes=[B_wkv])
        S.dma("sp", wuk[:], wb_uk.rearrange("(kc p) c -> p kc c", p=128), csem(), reads=[B_wbkv], writes=[B_wuk])
        S.dma("sp", wuv[:], wb_uv.rearrange("(kc p) c -> p kc c", p=128), csem(), reads=[B_wbkv], writes=[B_wuv])
        tiles = [(0, NMETA)] + [(NMETA + 512 * i, 512) for i in range((L - NMETA) // 512)]
        accb = [0]

        def next_acc():
            k = accb[0] % 6
            accb[0] += 1
            return k

        xnT2 = sb("xnT2", (128, 16, 512), BF16)
        B_xnT2 = Buf()
        xn_bufs = [(xnT, B_xnT), (xnT2, B_xnT2)]
        for _ in make_xnT_gen(job["xk"], tiles[0][0], tiles[0][1], 0, *xn_bufs[0]):
            pass
        for ti, (t0, n) in enumerate(tiles):
            xnT_c, B_xnT_c = xn_bufs[ti % 2]
            if ti + 1 < len(tiles):
                gen = make_xnT_gen(job["xk"], tiles[ti + 1][0], tiles[ti + 1][1], 0, *xn_bufs[(ti + 1) % 2])
            else:
                gen = iter(())
            rt, rt_b, rt_s = R_rtab.next()
            S.dma("sp", rt[:, :, 0:n], ropek_d[:, :, t0:t0 + n].rearrange("t r n -> r t n"), rt_s, writes=[rt_b])
            for mt in range(4):
                pb = next_acc()
                mm_group(pb, PS[pb][:, 0:n], [(wkv[:, kc, mt * 128:(mt + 1) * 128], xnT_c[:, kc, 0:n]) for kc in range(16)],
                         reads=[B_wkv, B_xnT_c])
                S.op("dve", lambda h, pb=pb, mt=mt, n=n: h.tensor_scalar(
                    out=ckvT[:, mt, 0:n], in0=PS[pb][:, 0:n], scalar1=cvec[:, V_KW + mt:V_KW + mt + 1], scalar2=None, op0=ALU.mult),
                    reads=[PSB[pb], B_cvec], writes=[B_ckvT])
                S.op("act", lambda h, pb=pb, mt=mt, n=n: h.activation(out=tmpA[:, mt, 0:n], in_=PS[pb][:, 0:n], func=AF.Square),
                     reads=[PSB[pb]], writes=[B_tmpA])
            pb = next_acc()
            mm_group(pb, PS[pb][0:64, 0:n], [(wkv[:, kc, 512:576], xnT_c[:, kc, 0:n]) for kc in range(16)], reads=[B_wkv, B_xnT_c])
            kr_t, kr_b, kr_s = R_krs.next()
            S.op("dve", lambda h, pb=pb, n=n, rt=rt: h.tensor_tensor(out=rt1[:, 0:n], in0=PS[pb][0:64, 0:n], in1=rt[:, 0, 0:n], op=ALU.mult),
                 reads=[PSB[pb], rt_b], writes=[B_rt1])
            S.op("dve", lambda h, pb=pb, n=n, rt=rt: h.tensor_tensor(out=rt2[0:32, 0:n], in0=PS[pb][32:64, 0:n], in1=rt[32:64, 1, 0:n], op=ALU.mult),
                 reads=[PSB[pb], rt_b], writes=[B_rt2])
            S.op("dve", lambda h, pb=pb, n=n, rt=rt: h.tensor_tensor(out=rt2[32:64, 0:n], in0=PS[pb][0:32, 0:n], in1=rt[0:32, 1, 0:n], op=ALU.mult),
                 reads=[PSB[pb], rt_b], writes=[B_rt2])
            S.op("dve", lambda h, n=n, kr_t=kr_t: h.tensor_tensor(out=kr_t[:, 0:n], in0=rt1[:, 0:n], in1=rt2[:, 0:n], op=ALU.add),
                 reads=[B_rt1, B_rt2], writes=[kr_b])
            store(krT_d[:, t0:t0 + n], kr_t[:, 0:n], reads=[kr_b], sem=kr_s)
            pb = next_acc()
            mm_group(pb, PS[pb][:, 0:n], [(ones_f[:], tmpA[:, mt, 0:n]) for mt in range(4)], reads=[B_ones, B_tmpA])
            S.op("act", lambda h, pb=pb, n=n: h.activation(out=rkb[:, 0:n], in_=PS[pb][:, 0:n], func=AF.Sqrt, bias=EPS, scale=1.0 / 512),
                 reads=[PSB[pb]], writes=[B_rkb])
            S.op("dve", lambda h, n=n: h.reciprocal(out=rkb[:, 0:n], in_=rkb[:, 0:n]), reads=[B_rkb], writes=[B_rkb])
            nsub = (n + 127) // 128
            pb = next_acc()
            for s_ in range(nsub):
                m = min(128, n - s_ * 128)
                S.op("pe", lambda h, pb=pb, s_=s_, m=m: h.matmul(PS[pb][0:m, s_:s_ + 1], lhsT=rkb[0:1, s_ * 128:s_ * 128 + m],
                                                                  rhs=ones_f[0:1, 0:1], start=True, stop=True),
                     reads=[B_rkb, B_ones], writes=[PSB[pb]], signal=(s_ == nsub - 1))
            S.op("dve", lambda h, pb=pb, nsub=nsub: h.tensor_copy(out=rkc[:, 0:nsub], in_=PS[pb][:, 0:nsub]),
                 reads=[PSB[pb]], writes=[B_rkc])
            for g in range(4):
                ks_t, ks_b, ks_s = R_kst.next()
                for hh in range(4):
                    hd = g * 4 + hh
                    pb = next_acc()
                    mm_group(pb, PS[pb][:, 0:n], [(wuk[:, kc, hd * 128:(hd + 1) * 128], ckvT[:, kc, 0:n]) for kc in range(4)],
                             reads=[B_wuk, B_ckvT])
                    S.op("dve", lambda h, pb=pb, hh=hh, n=n, ks_t=ks_t: h.tensor_tensor(
                        out=ks_t[:, hh, 0:n], in0=PS[pb][:, 0:n], in1=rkb[:, 0:n], op=ALU.mult),
                        reads=[PSB[pb], B_rkb], writes=[ks_b])
                store(kT_d[g * 4:(g + 1) * 4, :, t0:t0 + n].rearrange("h p t -> p h t"), ks_t[:, :, 0:n], reads=[ks_b], sem=ks_s)
                next(gen, None)
            for s_ in range(nsub):
                m = min(128, n - s_ * 128)
                blk = 0 if t0 == 0 else 1 + (t0 - NMETA) // 128 + s_
                vs_t, vs_b, vs_s = R_vst.next()
                ms = m
                if t0 == 0:
                    S.op("pool", lambda h, vs_t=vs_t: h.memset(vs_t[:], 0.0), writes=[vs_b])
                    ms = 128
                for g in range(4):
                    pb = next_acc()
                    mm_group(pb, PS[pb][0:m, :], [(ckvT[:, kc, s_ * 128:s_ * 128 + m], wuv[:, kc, g * 512:(g + 1) * 512]) for kc in range(4)],
                             reads=[B_wuv, B_ckvT])
                    S.op("act", lambda h, pb=pb, g=g, m=m, s_=s_, vs_t=vs_t: h.activation(
                        out=vs_t[0:m, g * 512:(g + 1) * 512], in_=PS[pb][0:m, :], func=AF.Copy, scale=rkc[0:m, s_:s_ + 1]),
                        reads=[PSB[pb], B_rkc], writes=[vs_b])
                store(V_d[:, 0:ms, blk, :].rearrange("h p d -> p h d"), vs_t[0:ms, :].rearrange("p (h d) -> p h d", h=H), reads=[vs_b], sem=vs_s)
                next(gen, None)
            for _ in gen:
                pass
        S.barrier()
        S.flush()
        free_to(n_global)

        phase[0] += 1
        if phase[0] > stop:
            break
        CKB = 8 if (L - NMETA) % 1024 == 0 else 4
        CKT = CKB * 128
        nchunk = (L - NMETA) // CKT
        qanT = sb("qanT", (128, 4, TQ), BF16)
        rqb = sb("rqb", (128, TQ), F32)
        szT = sb("szT", (128, 16, TQ), BF16)
        gcT = sb("gcT", (128, 16, TQ), BF16)
        mgT = sb("mgT", (128, 16, TQ), BF16)
        tmpB = sb("tmpB", (128, 2, TQ), F32)
        silt = [sb("silt%d" % i, (128, TQ), F32) for i in range(2)]
        wts = [sb("wt%d" % i, (128, 16, WC), BF16) for i in range(NWB)]
        wqs = [sb("wq%d" % i, (128, 4, 192), BF16) for i in range(2)]
        kbs = [sb("kb%d" % i, (128, CKT + NMETA), BF16) for i in range(3)]
        krb = [sb("krb%d" % i, (128, CKT + NMETA), BF16) for i in range(3)]
        vbs = [sb("vb%d" % i, (128, CKB + 1, 132), BF16) for i in range(3)]
        pts = [sb("pt%d" % i, (128, TQ), BF16) for i in range(4)]
        qns = [sb("qn%d" % i, (128, TQ), BF16) for i in range(2)]
        qrs = [sb("qr%d" % i, (128, TQ), BF16) for i in range(2)]
        onb = sb("onb", (128, 4, 128), BF16)
        rq_tab = sb("rq_tab", (64, 2, TQ), F32)
        qt1 = sb("qt1", (64, TQ), F32)
        qt2 = sb("qt2", (64, TQ), F32)
        B_qanT, B_rqb, B_gcT, B_mgT, B_tmpB, B_onb, B_rqtab, B_qt1, B_qt2 = [Buf() for _ in range(9)]
        B_szT = [Buf() for _ in range(16)]
        B_hb = R_xin.b
        B_sil = [Buf(), Buf()]
        B_qr = [Buf(), Buf()]
        B_qn = [Buf(), Buf()]
        B_pt = [Buf() for _ in range(4)]
        nm = job["name"]
        R_wt = Ring(S, "wt" + nm, wts)
        R_wq = Ring(S, "wq" + nm, wqs)
        R_kb = Ring(S, "kb" + nm, kbs)
        R_krb = Ring(S, "krb" + nm, krb)
        R_vb = Ring(S, "vb" + nm, vbs)
        sem_q = S.new_sem("semq" + nm)
        sem_h = [S.new_sem("semh%s%d" % (nm, i)) for i in range(4)]
        for i in range(3):
            S.op("pool", lambda h, i=i: h.memset(krb[i][64:128, :], 0.0), writes=[R_krb.b[i]])
            S.op("pool", lambda h, i=i: h.memset(vbs[i][:, :, 128:129], 1.0), writes=[R_vb.b[i]])
        for i in range(2):
            S.op("pool", lambda h, i=i: h.memset(qrs[i][64:128, :], 0.0), writes=[B_qr[i]])

        wv_in = wb_in.rearrange("(kc p) c -> p kc c", p=128)
        wv_oa = wb_oa.rearrange("(kc p) c -> p kc c", p=128)
        wv_oc = wb_oc.rearrange("(kc p) c -> p kc c", p=128)
        wv_o = wb_o.rearrange("(kc p) c -> p kc c", p=128)
        wv_uq = wb_uq.rearrange("(kc p) c -> p kc c", p=128)

        NT = NQ // TQ
        xq0 = job["xq_row0"]
        xq_ = job["xq"]
        srcs = [(0, NT, xq_[xq0:xq0 + TQ * NT, :].rearrange("(t r) d -> t r d", r=TQ)[:, 0, :])]
        if NT > 1:
            srcs.append((NT, NT - 1, xq_[xq0 + 1:xq0 + 1 + TQ * NT, :].rearrange("(t r) d -> t r d", r=TQ)[1:NT, 0, :]))
        srcs.append((2 * NT - 1, 1, xq_[xq0 + NQ + 1:xq0 + NQ + 2, :]))
        xnTh = sb("xnTh", (128, 16, 2 * NT), BF16)
        uh = sb("uh", (128, 16, 2 * NT), F32)
        uht = sb("uht", (128, 2, 2 * NT), F32)
        B_xnTh, B_uh, B_uht = Buf(), Buf(), Buf()
        for _ in make_xnT_gen(None, 0, 2 * NT, 0, xnTh, B_xnTh, srcs=srcs):
            pass
        hplan = []
        for c0 in range(0, 2048, WC):
            hplan.append((wv_in, C_CX + c0))
            hplan.append((wv_in, C_CC + c0))
        hl = []
        for i_, (src_, c0_) in enumerate(hplan):
            if i_ < NWB:
                t_, b_, s_ = R_wt.next()
                S.dma("sp", t_[:], src_[:, :, c0_:c0_ + WC], s_, writes=[b_])
                hl.append((t_, b_))
        hacc = [0]
        for i_ in range(len(hplan)):
            wt, wb_ = hl[i_]
            if i_ + NWB < len(hplan):
                src_, c0_ = hplan[i_ + NWB]
            is_cc = i_ % 2 == 1
            for mi in range(WC // 128):
                c = (i_ // 2) * (WC // 128) + mi
                pb = hacc[0] % 6
                hacc[0] += 1
                mm_group(pb, PS[pb][:, 0:2 * NT], [(wt[:, kc, mi * 128:(mi + 1) * 128], xnTh[:, kc, :]) for kc in range(16)], reads=[wb_, B_xnTh])
                if not is_cc:
                    S.op("act", lambda h, pb=pb, mi=mi: h.activation(out=uht[:, mi, :], in_=PS[pb][:, 0:2 * NT], func=AF.Copy),
                         reads=[PSB[pb]], writes=[B_uht])
                else:
                    S.op("dve", lambda h, pb=pb, mi=mi, c=c: h.tensor_tensor(out=uh[:, c, :], in0=PS[pb][:, 0:2 * NT], in1=uht[:, mi, :], op=ALU.mult),
                         reads=[PSB[pb], B_uht], writes=[B_uh])
            if i_ + NWB < len(hplan):
                t_, b_, s_ = R_wt.next()
                S.dma("sp", t_[:], src_[:, :, c0_:c0_ + WC], s_, writes=[b_])
                hl.append((t_, b_))

        for qt in range(NQ // TQ):
            r0 = job["xq_row0"] + qt * TQ
            plan = []
            for c0 in range(0, 512, WC):
                plan.append((wv_in, C_QA + c0))
            for c0 in range(0, 2048, WC):
                plan.append((wv_in, C_ZA + c0))
            n_pre = len(plan)
            for c0 in range(0, 2048, WC):
                for base in (C_CX, C_CC, C_CB, C_ZC):
                    plan.append((wv_in, base + c0))
            for c0 in range(0, 2048, WC):
                plan.append((wv_in, C_GA + c0))
                plan.append((wv_oa, c0))
                plan.append((wv_in, C_GC + c0))
                plan.append((wv_oc, c0))
            for c0 in range(0, 2048, WC):
                plan.append((wv_o, c0))
            loaded = {}
            nload = [0]

            def wload_upto(k):
                while nload[0] < min(k, len(plan)):
                    i = nload[0]
                    t, b, s = R_wt.next()
                    src, c0 = plan[i]
                    S.dma("sp", t[:], src[:, :, c0:c0 + WC], s, writes=[b])
                    loaded[i] = (t, b)
                    nload[0] += 1

            def wget(i):
                wload_upto(i + NWB)
                return loaded[i]

            accb = [0]

            def next_acc():
                k = accb[0] % 6
                accb[0] += 1
                return k

            make_xnT(job["xq"], r0 + 1, TQ, 0)
            S.dma("sp", rq_tab[:], ropeq_d[job["rq"], :, :, qt * TQ:(qt + 1) * TQ].rearrange("t r n -> r t n"), sem_q, writes=[B_rqtab])
            XO = slice(0, TQ)
            wi = 0
            for c0 in range(0, 512, WC):
                wt, wb_ = wget(wi); wi += 1
                for mi in range(WC // 128):
                    mt = (c0 // 128) + mi
                    pb = next_acc()
                    mm_group(pb, PS[pb][:, :], [(wt[:, kc, mi * 128:(mi + 1) * 128], xnT[:, kc, XO]) for kc in range(16)], reads=[wb_, B_xnT])
                    S.op("dve", lambda h, pb=pb, mt=mt: h.tensor_scalar(out=qanT[:, mt, :], in0=PS[pb][:, :], scalar1=cvec[:, V_QW + mt:V_QW + mt + 1],
                                                                      scalar2=None, op0=ALU.mult), reads=[PSB[pb], B_cvec], writes=[B_qanT])
                    S.op("act", lambda h, pb=pb, mt=mt: h.activation(out=tmpA[:, mt, 0:TQ], in_=PS[pb][:, :], func=AF.Square),
                         reads=[PSB[pb]], writes=[B_tmpA])
            pb = next_acc()
            mm_group(pb, PS[pb][:, :], [(ones_f[:], tmpA[:, mt, 0:TQ]) for mt in range(4)], reads=[B_ones, B_tmpA])
            S.op("act", lambda h, pb=pb: h.activation(out=rqb[:], in_=PS[pb][:, :], func=AF.Sqrt, bias=EPS, scale=1.0 / 512),
                 reads=[PSB[pb]], writes=[B_rqb])
            S.op("dve", lambda h: h.reciprocal(out=rqb[:], in_=rqb[:]), reads=[B_rqb], writes=[B_rqb])
            S.op("dve", lambda h: h.tensor_tensor(out=rq_tab[:, 0, :], in0=rq_tab[:, 0, :], in1=rqb[0:64, :], op=ALU.mult),
                 reads=[B_rqtab, B_rqb], writes=[B_rqtab])
            S.op("dve", lambda h: h.tensor_tensor(out=rq_tab[:, 1, :], in0=rq_tab[:, 1, :], in1=rqb[0:64, :], op=ALU.mult),
                 reads=[B_rqtab, B_rqb], writes=[B_rqtab])
            for c0 in range(0, 2048, WC):
                wt, wb_ = wget(wi); wi += 1
                for mi in range(WC // 128):
                    c = (c0 // 128) + mi
                    pb = next_acc()
                    mm_group(pb, PS[pb][:, :], [(wt[:, kc, mi * 128:(mi + 1) * 128], xnT[:, kc, XO]) for kc in range(16)], reads=[wb_, B_xnT])
                    S.op("act", lambda h, pb=pb, c=c: h.activation(out=szT[:, c, :], in_=PS[pb][:, :], func=AF.Silu),
                         reads=[PSB[pb]], writes=[B_szT[c]])
            assert wi == n_pre
            wload_upto(n_pre + NWB)

            ST = [0, 1, 2]
            OB = [[3, 4], [5, 6]]
            PQN, PQR = 7, 7

            def q_proj(hd):
                wq, wq_b, wq_s = R_wq.next()
                S.dma("sp", wq[:], wv_uq[:, :, hd * 192:(hd + 1) * 192], wq_s, writes=[wq_b])
                i = hd % 2
                mm_group(PQN, PS[PQN][:, :], [(wq[:, kc, 0:128], qanT[:, kc, :]) for kc in range(4)], reads=[wq_b, B_qanT])
                S.op("dve", lambda h, i=i: h.tensor_tensor(out=qns[i][:], in0=PS[PQN][:, :], in1=rqb[:], op=ALU.mult),
                     reads=[PSB[PQN], B_rqb], writes=[B_qn[i]])
                mm_group(PQR, PS[PQR][0:64, :], [(wq[:, kc, 128:192], qanT[:, kc, :]) for kc in range(4)], reads=[wq_b, B_qanT])
                S.op("dve", lambda h: h.tensor_tensor(out=qt1[:], in0=PS[PQR][0:64, :], in1=rq_tab[:, 0, :], op=ALU.mult),
                     reads=[PSB[PQR], B_rqtab], writes=[B_qt1])
                S.op("dve", lambda h: h.tensor_tensor(out=qt2[0:32, :], in0=PS[PQR][32:64, :], in1=rq_tab[32:64, 1, :], op=ALU.mult),
                     reads=[PSB[PQR], B_rqtab], writes=[B_qt2])
                S.op("dve", lambda h: h.tensor_tensor(out=qt2[32:64, :], in0=PS[PQR][0:32, :], in1=rq_tab[0:32, 1, :], op=ALU.mult),
                     reads=[PSB[PQR], B_rqtab], writes=[B_qt2])
                S.op("dve", lambda h, i=i: h.tensor_tensor(out=qrs[i][0:64, :], in0=qt1[:], in1=qt2[:], op=ALU.add),
                     reads=[B_qt1, B_qt2], writes=[B_qr[i]])

            items = []
            for hd in range(H):
                for ck in range(nchunk):
                    nb = CKB + (1 if ck == 0 else 0)
                    for bi in range(nb):
                        items.append((hd, ck, bi))
            chunk_bufs = {}

            def load_chunk(hd, ck):
                kt, kb_, ks_ = R_kb.next()
                krt, krb_, krs_ = R_krb.next()
                vt, vb_, vs_ = R_vb.next()
                tok0 = 0 if ck == 0 else NMETA + ck * CKT
                ntok = CKT + (NMETA if ck == 0 else 0)
                blk0 = 0 if ck == 0 else 1 + ck * CKB
                nb = CKB + (1 if ck == 0 else 0)
                S.dma("sp", kt[:, 0:ntok], kT_d[hd, :, tok0:tok0 + ntok], ks_, writes=[kb_])
                S.dma("sp", krt[0:64, 0:ntok], krT_d[:, tok0:tok0 + ntok], krs_, writes=[krb_])
                S.dma("sp", vt[:, 0:nb, 0:128], V_d[hd, :, blk0:blk0 + nb, :], vs_, writes=[vb_])
                chunk_bufs[(hd, ck)] = (kt, kb_, krt, krb_, vt, vb_)

            def item_geom(it):
                hd, ck, bi = it
                if ck == 0:
                    nk = NMETA if bi == 0 else 128
                    col = 0 if bi == 0 else NMETA + (bi - 1) * 128
                else:
                    nk = 128
                    col = bi * 128
                return nk, col

            def qk(k):
                hd, ck, bi = items[k]
                if bi == 0:
                    if (hd, ck) not in chunk_bufs:
                        load_chunk(hd, ck)
                    nxt = (hd, ck + 1) if ck + 1 < nchunk else ((hd + 1, 0) if hd + 1 < H else None)
                    if nxt is not None and nxt not in chunk_bufs:
                        load_chunk(*nxt)
                kt, kb_, krt, krb_, vt, vb_ = chunk_bufs[(hd, ck)]
                nk, col = item_geom(items[k])
                st = ST[k % 3]
                i = hd % 2
                S.op("pe", lambda h, st=st, nk=nk, col=col, kt=kt, i=i: h.matmul(
                    PS[st][0:nk, :], lhsT=kt[:, col:col + nk], rhs=qns[i][:], start=True, stop=False),
                    reads=[kb_, B_qn[i]], writes=[PSB[st]], signal=False)
                S.op("pe", lambda h, st=st, nk=nk, col=col, krt=krt, i=i: h.matmul(
                    PS[st][0:nk, :], lhsT=krt[:, col:col + nk], rhs=qrs[i][:], start=False, stop=True),
                    reads=[krb_, B_qr[i]], writes=[PSB[st]])
                j = k % 4
                S.op("act", lambda h, st=st, nk=nk, j=j: h.activation(out=pts[j][0:nk, :], in_=PS[st][0:nk, :], func=AF.Exp, scale=SCALE),
                     reads=[PSB[st]], writes=[B_pt[j]])

            def pv(k):
                hd, ck, bi = items[k]
                kt, kb_, krt, krb_, vt, vb_ = chunk_bufs[(hd, ck)]
                nk, col = item_geom(items[k])
                j = k % 4
                ob = OB[hd % 2]
                first = (ck == 0 and bi == 0)
                last = (ck == nchunk - 1 and bi == CKB + (1 if ck == 0 else 0) - 1)
                for qs in range(4):
                    bank = ob[qs // 2]
                    co = (qs % 2) * 256
                    S.op("pe", lambda h, bank=bank, co=co, nk=nk, j=j, qs=qs, vt=vt, bi=bi, first=first, last=last: h.matmul(
                        PS[bank][:, co:co + 129], lhsT=pts[j][0:nk, qs * 128:(qs + 1) * 128], rhs=vt[0:nk, bi, 0:129],
                        start=(first and qs % 2 == 0), stop=last, skip_group_check=True),
                        reads=[B_pt[j], vb_], writes=[PSB[bank]], signal=(qs == 3))
                if last and ck == nchunk - 1:
                    del chunk_bufs[(hd, ck)]

            def head_norm(hd):
                ob = OB[hd % 2]
                for qs in range(4):
                    bank = ob[qs // 2]
                    co = (qs % 2) * 256
                    rc, rc_b = stat_slot()
                    S.op("dve", lambda h, bank=bank, co=co, rc=rc: h.reciprocal(out=rc, in_=PS[bank][:, co + 128:co + 129]),
                         reads=[PSB[bank]], writes=[rc_b])
                    S.op("dve", lambda h, bank=bank, co=co, rc=rc, qs=qs: h.tensor_scalar(
                        out=onb[:, qs, :], in0=PS[bank][:, co:co + 128], scalar1=rc, scalar2=None, op0=ALU.mult),
                        reads=[PSB[bank], rc_b], writes=[B_onb])

            def head_out(hd):
                pv_ = PS[PQN].bitcast(BF16)
                for qs in range(4):
                    S.op("pe", lambda h, qs=qs, pv_=pv_: h.transpose(out=pv_[:, qs * 128:(qs + 1) * 128], in_=onb[:, qs, :], identity=ident[:]),
                         reads=[B_onb, B_ident], writes=[PSB[PQN]], signal=(qs == 3))
                S.op("dve", lambda h, hd=hd, pv_=pv_: h.tensor_tensor(out=szT[:, hd, :], in0=pv_[:, 0:TQ], in1=szT[:, hd, :], op=ALU.mult),
                     reads=[PSB[PQN], B_szT[hd]], writes=[B_szT[hd]])

            deferred = {}
            q_proj(0)
            qk(0)
            qk(1)
            nit = len(items)
            per_head = nit // H
            for k in range(nit):
                hd, ck, bi = items[k]
                if k + 2 < nit:
                    qk(k + 2)
                pv(k)
                kin = k % per_head
                if kin == per_head - 1:
                    head_norm(hd)
                    deferred[k + 2] = hd
                if kin == min(2, per_head - 2) and hd + 1 < H:
                    q_proj(hd + 1)
                if k in deferred:
                    head_out(deferred.pop(k))
            for k in sorted(deferred):
                head_out(deferred.pop(k))

            NM = WC // 128
            for c0 in range(0, 2048, WC):
                wt, wb_ = wget(wi); wi += 1
                for mi in range(NM):
                    pb = next_acc()
                    mm_group(pb, PS[pb][:, :], [(wt[:, kc, mi * 128:(mi + 1) * 128], xnT[:, kc, XO]) for kc in range(16)], reads=[wb_, B_xnT])
                    S.op("act", lambda h, pb=pb, mi=mi: h.activation(out=tmpA[:, mi, 1:TQ + 1], in_=PS[pb][:, :], func=AF.Copy),
                         reads=[PSB[pb]], writes=[B_tmpA])
                wt, wb_ = wget(wi); wi += 1
                for mi in range(NM):
                    c = c0 // 128 + mi
                    pb = next_acc()
                    mm_group(pb, PS[pb][:, :], [(wt[:, kc, mi * 128:(mi + 1) * 128], xnT[:, kc, XO]) for kc in range(16)], reads=[wb_, B_xnT])
                    S.op("dve", lambda h, pb=pb, mi=mi: h.tensor_tensor(out=tmpA[:, mi, 1:TQ + 1], in0=PS[pb][:, :], in1=tmpA[:, mi, 1:TQ + 1], op=ALU.mult),
                         reads=[PSB[pb], B_tmpA], writes=[B_tmpA])
                    S.op("dve", lambda h, mi=mi, c=c, qt=qt: h.tensor_copy(out=tmpA[:, mi, 0:1], in_=uh[:, c, qt:qt + 1]),
                         reads=[B_uh], writes=[B_tmpA])
                    S.op("dve", lambda h, mi=mi, c=c, qt=qt: h.tensor_copy(out=tmpA[:, mi, TQ + 1:TQ + 2], in_=uh[:, c, NT + qt:NT + qt + 1]),
                         reads=[B_uh], writes=[B_tmpA])
                    cw = lambda k_, c=c: cvec[:, V_CW + c * 3 + k_:V_CW + c * 3 + k_ + 1]
                    S.op("dve", lambda h, mi=mi, cw=cw: h.tensor_scalar(out=tmpB[:, mi, :], in0=tmpA[:, mi, 0:TQ], scalar1=cw(0), scalar2=None, op0=ALU.mult),
                         reads=[B_tmpA, B_cvec], writes=[B_tmpB])
                    S.op("dve", lambda h, mi=mi, cw=cw: h.scalar_tensor_tensor(out=tmpB[:, mi, :], in0=tmpA[:, mi, 1:TQ + 1], scalar=cw(1), in1=tmpB[:, mi, :],
                                                                               op0=ALU.mult, op1=ALU.add), reads=[B_tmpA, B_tmpB, B_cvec], writes=[B_tmpB])
                    S.op("dve", lambda h, mi=mi, cw=cw: h.scalar_tensor_tensor(out=tmpB[:, mi, :], in0=tmpA[:, mi, 2:TQ + 2], scalar=cw(2), in1=tmpB[:, mi, :],
                                                                               op0=ALU.mult, op1=ALU.add), reads=[B_tmpA, B_tmpB, B_cvec], writes=[B_tmpB])
                wt, wb_ = wget(wi); wi += 1
                for mi in range(NM):
                    pb = next_acc()
                    mm_group(pb, PS[pb][:, :], [(wt[:, kc, mi * 128:(mi + 1) * 128], xnT[:, kc, XO]) for kc in range(16)], reads=[wb_, B_xnT])
                    S.op("dve", lambda h, pb=pb, mi=mi: h.tensor_tensor(out=tmpB[:, mi, :], in0=PS[pb][:, :], in1=tmpB[:, mi, :], op=ALU.mult),
                         reads=[PSB[pb], B_tmpB], writes=[B_tmpB])
                wt, wb_ = wget(wi); wi += 1
                for mi in range(NM):
                    c = c0 // 128 + mi
                    pb = next_acc()
                    mm_group(pb, PS[pb][:, :], [(wt[:, kc, mi * 128:(mi + 1) * 128], xnT[:, kc, XO]) for kc in range(16)], reads=[wb_, B_xnT])
                    si = c % 2
                    S.op("act", lambda h, pb=pb, si=si: h.activation(out=silt[si][:], in_=PS[pb][:, :], func=AF.Silu),
                         reads=[PSB[pb]], writes=[B_sil[si]])
                    S.op("dve", lambda h, si=si, mi=mi, c=c: h.tensor_tensor(out=gcT[:, c, :], in0=tmpB[:, mi, :], in1=silt[si][:], op=ALU.mult),
                         reads=[B_tmpB, B_sil[si]], writes=[B_gcT])
            for c0 in range(0, 2048, WC):
                wt, wb_ = wget(wi); wi += 1
                for mi in range(NM):
                    c = c0 // 128 + mi
                    pb = next_acc()
                    mm_group(pb, PS[pb][:, :], [(wt[:, kc, mi * 128:(mi + 1) * 128], xnT[:, kc, XO]) for kc in range(16)], reads=[wb_, B_xnT])
                    S.op("act", lambda h, pb=pb, mi=mi, c=c: h.activation(out=tmpA[:, mi, 0:TQ], in_=PS[pb][:, :], func=AF.Sigmoid,
                                                                           bias=cvec[:, V_BG + c:V_BG + c + 1], scale=1.0),
                         reads=[PSB[pb], B_cvec], writes=[B_tmpA])
                wt, wb_ = wget(wi); wi += 1
                for mi in range(NM):
                    pb = next_acc()
                    mm_group(pb, PS[pb][:, :], [(wt[:, kc, mi * 128:(mi + 1) * 128], szT[:, kc, :]) for kc in range(16)], reads=[wb_] + B_szT)
                    S.op("dve", lambda h, pb=pb, mi=mi: h.tensor_tensor(out=tmpA[:, mi, 0:TQ], in0=PS[pb][:, :], in1=tmpA[:, mi, 0:TQ], op=ALU.mult),
                         reads=[PSB[pb], B_tmpA], writes=[B_tmpA])
                wt, wb_ = wget(wi); wi += 1
                for mi in range(NM):
                    c = c0 // 128 + mi
                    pb = next_acc()
                    mm_group(pb, PS[pb][:, :], [(wt[:, kc, mi * 128:(mi + 1) * 128], xnT[:, kc, XO]) for kc in range(16)], reads=[wb_, B_xnT])
                    S.op("act", lambda h, pb=pb, mi=mi, c=c: h.activation(out=tmpB[:, mi, :], in_=PS[pb][:, :], func=AF.Sigmoid,
                                                                           bias=cvec[:, V_BG + 16 + c:V_BG + 16 + c + 1], scale=1.0),
                         reads=[PSB[pb], B_cvec], writes=[B_tmpB])
                wt, wb_ = wget(wi); wi += 1
                for mi in range(NM):
                    c = c0 // 128 + mi
                    pb = next_acc()
                    mm_group(pb, PS[pb][:, :], [(wt[:, kc, mi * 128:(mi + 1) * 128], gcT[:, kc, :]) for kc in range(16)], reads=[wb_, B_gcT])
                    S.op("dve", lambda h, pb=pb, mi=mi: h.tensor_tensor(out=tmpB[:, mi, :], in0=PS[pb][:, :], in1=tmpB[:, mi, :], op=ALU.mult),
                         reads=[PSB[pb], B_tmpB], writes=[B_tmpB])
                    S.op("dve", lambda h, mi=mi, c=c: h.tensor_tensor(out=mgT[:, c, :], in0=tmpA[:, mi, 0:TQ], in1=tmpB[:, mi, :], op=ALU.add),
                         reads=[B_tmpA, B_tmpB], writes=[B_mgT])
            for s_ in range(4):
                S.dma("sp", hbuf[:, s_, :], job["xq"][r0 + 1 + s_ * 128:r0 + 1 + (s_ + 1) * 128, :], R_xin.s[s_], writes=[B_hb[s_]])
            for c0 in range(0, 2048, WC):
                wt, wb_ = wget(wi); wi += 1
                for s_ in range(4):
                    pb = next_acc()
                    mm_group(pb, PS[pb][:, 0:WC], [(mgT[:, kc, s_ * 128:(s_ + 1) * 128], wt[:, kc, :]) for kc in range(16)], reads=[wb_, B_mgT])
                    S.op("dve", lambda h, pb=pb, s_=s_, c0=c0: h.tensor_tensor(out=hbuf[:, s_, c0:c0 + WC], in0=PS[pb][:, 0:WC], in1=hbuf[:, s_, c0:c0 + WC], op=ALU.add),
                         reads=[PSB[pb], B_hb[s_]], writes=[B_hb[s_]])
            assert wi == len(plan)
            for s_ in range(4):
                xs_t, xs_b, _ = R_xsb.next()
                ss, ss_b = stat_slot()
                S.op("act", lambda h, s_=s_, xs_t=xs_t, ss=ss: h.activation(out=xs_t[:], in_=hbuf[:, s_, :], func=AF.Square, scale=float(D) ** -0.5, accum_out=ss),
                     reads=[B_hb[s_]], writes=[xs_b, ss_b])
                S.op("act", lambda h, ss=ss: h.activation(out=ss, in_=ss, func=AF.Sqrt, bias=EPS, scale=1.0), reads=[ss_b], writes=[ss_b])
                S.op("dve", lambda h, ss=ss: h.reciprocal(out=ss, in_=ss), reads=[ss_b], writes=[ss_b])
                S.op("dve", lambda h, s_=s_, ss=ss: h.scalar_tensor_tensor(out=hbuf[:, s_, :], in0=hbuf[:, s_, :], scalar=ss, in1=fnw[:], op0=ALU.mult, op1=ALU.mult),
                     reads=[B_hb[s_], ss_b, B_fnw], writes=[B_hb[s_]])
                S.dma("pool", job["y"][qt * TQ + s_ * 128:qt * TQ + (s_ + 1) * 128, :], hbuf[:, s_, :], sem_h[s_], reads=[B_hb[s_]])
        S.barrier()
        S.flush()
        free_to(n_global)

    free_to(0)
    for cm in reversed(psum_cms):
        cm.__exit__(None, None, None)
    S.close()
    return nc


def _rope_tables(npos):
    inv_freq = (np.float32(1.0) / (np.float32(10000.0) ** (np.arange(0, 64, 2, dtype=np.float32) / np.float32(64)))).astype(np.float32)
    ang = (np.arange(npos, dtype=np.float32)[:, None] * inv_freq[None, :]).astype(np.float32)
    cos = np.cos(ang).astype(np.float32).T
    sin = np.sin(ang).astype(np.float32).T
    tab = np.empty((2, 64, npos), np.float32)
    tab[0, 0:32] = cos
    tab[0, 32:64] = cos
    tab[1, 0:32] = sin
    tab[1, 32:64] = -sin
    return tab


_PROG_CACHE = {}


def kernel(x_prompt, x_sample, meta_tokens, norm_w, w_in, b_gate, q_a_norm_w, w_uq, kv_a_norm_w, w_ukv,
           w_o_attn, conv_w, w_o_conv, w_o, final_norm_w):
    f = lambda a: np.ascontiguousarray(np.asarray(a, dtype=np.float32))
    x_prompt, x_sample, meta = f(x_prompt), f(x_sample), f(meta_tokens)
    B, SP, _ = x_prompt.shape
    NS, SS, _ = x_sample.shape
    assert NS == 8 and B == 2 and SP == 4 * SS
    NQ = SS
    LS, LP = NMETA + SS, NMETA + SP
    key = (LS, LP, NQ)
    if key not in _PROG_CACHE:
        import os
        _PROG_CACHE[key] = build_program(LS, LP, NQ, stop=int(os.environ.get("K_STOP", "99")))
    nc = _PROG_CACHE[key]
    cvec = np.zeros((128, V_N), np.float32)
    cvec[:, V_NW:V_NW + 16] = f(norm_w)[0].reshape(16, 128).T
    cvec[:, V_QW:V_QW + 4] = f(q_a_norm_w)[0].reshape(4, 128).T
    cvec[:, V_KW:V_KW + 4] = f(kv_a_norm_w)[0].reshape(4, 128).T
    cw = f(conv_w)[0]
    cvec[:, V_CW:V_CW + 48] = cw.reshape(3, 16, 128).transpose(2, 1, 0).reshape(128, 48)
    cvec[:, V_BG:V_BG + 32] = f(b_gate)[0].reshape(32, 128).T
    fnw = np.ascontiguousarray(np.broadcast_to(f(final_norm_w)[None, :], (128, D)))
    ropek = _rope_tables(LP)
    zrow = np.zeros((1, D), np.float32)
    xps = [np.concatenate([meta, x_prompt[b], zrow], axis=0) for b in range(B)]
    shared = dict(w_in=f(w_in)[0], w_uq=f(w_uq)[0], w_ukv=f(w_ukv)[0], w_oa=f(w_o_attn)[0], w_oc=f(w_o_conv)[0],
                  w_o=f(w_o)[0], cvec=cvec, fnw=fnw, ropek=ropek)
    in_maps = []
    for c in range(8):
        b, j = c // 4, c % 4
        xs = np.concatenate([meta, x_sample[c], zrow], axis=0)
        r0 = NMETA - 1 + NQ * j
        xqp = np.ascontiguousarray(xps[b][r0:r0 + NQ + 2])
        ropeq = np.stack([ropek[:, :, NMETA:NMETA + NQ], ropek[:, :, NMETA + NQ * j:NMETA + NQ * (j + 1)]], axis=0)
        m = dict(shared)
        m.update(xs=xs, xp=xps[b], xqp=xqp, ropeq=np.ascontiguousarray(ropeq))
        in_maps.append(m)
    res = run_bass_kernel_spmd(nc, in_maps, core_ids=list(range(8)))
    y_prompt = np.empty((B, SP, D), np.float32)
    y_sample = np.empty((NS, SS, D), np.float32)
    for c in range(8):
        b, j = c // 4, c % 4
        y_sample[c] = res.results[c]["ys"]
        y_prompt[b, NQ * j:NQ * (j + 1)] = res.results[c]["yp"]
    return (y_prompt, y_sample)
```
